# Optimizing a Trainium2 kernel written in Bass

```python
import jax, jax.numpy as jnp
from jax import lax
import numpy as np

D_MODEL = 1024
BATCH = 16
SEQ = 2048
DEPTH = 1

GRID_W = 64
CTX_LEN = 256
POOL_WIDTH = 512
POOL_WINDOWS = (2, 4, 8, 16)
POOL_GROUPS = len(POOL_WINDOWS)
POOL_GROUP_DIM = POOL_WIDTH // POOL_GROUPS
MLSTM_HEADS = 4
QK_HEAD_DIM = 64
V_HEAD_DIM = 128
QK_WIDTH = MLSTM_HEADS * QK_HEAD_DIM
MLSTM_WIDTH = MLSTM_HEADS * V_HEAD_DIM
N_GATES = 4 * MLSTM_HEADS
MIX_WIDTH = POOL_WIDTH + MLSTM_WIDTH
IN_WIDTH = POOL_WIDTH + 2 * QK_WIDTH + 2 * MLSTM_WIDTH + N_GATES
CONV_W = 3
CHUNK = 64
D_FF = -(-8 * D_MODEL // (3 * 256)) * 256
EPS = 1e-6

kernel_name = 'hybrid_pool_mlstm_dit_block'


def rmsnorm(x, g):
    xf = x.astype(jnp.float32)
    y = xf * lax.rsqrt(jnp.mean(xf * xf, axis=-1, keepdims=True) + EPS)
    return (y * g.astype(jnp.float32)).astype(x.dtype)


def modulate(h, shift, scale):
    return h * (1 + scale) + shift


def short_conv(u, w):
    pad = CONV_W // 2
    t = u.shape[1]
    up = jnp.pad(u, ((0, 0), (pad, pad), (0, 0)))
    return sum(w[j] * up[:, j:j + t] for j in range(CONV_W))


def pool_mixer(u, pool_w, pool_scale):
    length = u.shape[-2]
    pos = jnp.arange(length)
    uf = u.astype(jnp.float32)
    cs = jnp.cumsum(uf, axis=-2)
    cs = jnp.concatenate([jnp.zeros_like(cs[..., :1, :]), cs], axis=-2)
    outs = []
    for gi, win in enumerate(POOL_WINDOWS):
        lo = jnp.clip(pos - win // 2, 0, length - 1)
        hi = jnp.clip(pos + win // 2 - 1, 0, length - 1)
        sl = slice(gi * POOL_GROUP_DIM, (gi + 1) * POOL_GROUP_DIM)
        csg = cs[..., sl]
        s = jnp.take(csg, hi + 1, axis=-2) - jnp.take(csg, lo, axis=-2)
        cnt = (hi - lo + 1).astype(jnp.float32)[:, None]
        d = (s / cnt - uf[..., sl]).astype(u.dtype)
        outs.append(jnp.einsum('...lc,cd->...ld', d, pool_w[gi]))
    return jnp.concatenate(outs, axis=-1) * pool_scale


def mlstm_chunk_scan(q, k, v, i_pre, logf, state):
    bsz, nh, t, _ = q.shape
    dv = v.shape[-1]
    nc = t // CHUNK
    mask = jnp.tril(jnp.ones((CHUNK, CHUNK), dtype=bool))

    def to_chunks(a):
        a = a.reshape(a.shape[:2] + (nc, CHUNK) + a.shape[3:])
        return jnp.moveaxis(a, 2, 0)

    def step(carry, xs):
        c0, n0, m0 = carry
        qc, kc, vc, ic, fc = xs
        b = jnp.cumsum(fc, axis=-1)
        dmat = jnp.where(mask, b[..., :, None] - b[..., None, :] + ic[..., None, :], -jnp.inf)
        inter = b + m0[..., None]
        m = jnp.maximum(inter, jnp.max(dmat, axis=-1))
        s = jnp.einsum('bhjd,bhsd->bhjs', qc, kc) * jnp.exp(dmat - m[..., None])
        w_inter = jnp.exp(inter - m)
        num = jnp.einsum('bhjs,bhsv->bhjv', s, vc) + w_inter[..., None] * jnp.einsum('bhvd,bhjd->bhjv', c0, qc)
        den = jnp.sum(s, axis=-1) + w_inter * jnp.einsum('bhd,bhjd->bhj', n0, qc)
        h = num / jnp.maximum(jnp.abs(den), jnp.exp(-m))[..., None]
        bl = b[..., -1]
        wk = bl[..., None] - b + ic
        m_new = jnp.maximum(bl + m0, jnp.max(wk, axis=-1))
        decay = jnp.exp(bl + m0 - m_new)
        wk = jnp.exp(wk - m_new[..., None])
        c_new = decay[..., None, None] * c0 + jnp.einsum('bhs,bhsv,bhsd->bhvd', wk, vc, kc)
        n_new = decay[..., None] * n0 + jnp.einsum('bhs,bhsd->bhd', wk, kc)
        return (c_new, n_new, m_new), h

    xs = (to_chunks(q), to_chunks(k), to_chunks(v), to_chunks(i_pre), to_chunks(logf))
    state, hs = lax.scan(step, state, xs)
    hs = jnp.moveaxis(hs, 0, 2).reshape(bsz, nh, t, dv)
    return hs, state


def zero_state(bsz):
    return (jnp.zeros((bsz, MLSTM_HEADS, V_HEAD_DIM, QK_HEAD_DIM), jnp.float32),
            jnp.zeros((bsz, MLSTM_HEADS, QK_HEAD_DIM), jnp.float32),
            jnp.zeros((bsz, MLSTM_HEADS), jnp.float32))


def mlstm_bidir(q, k, v, gates, st_f, st_b):
    bsz, t, _ = gates.shape
    g = jnp.transpose(gates.astype(jnp.float32).reshape(bsz, t, 4, MLSTM_HEADS), (2, 0, 3, 1))
    i_f, i_b = g[0], g[1]
    lf_f, lf_b = jax.nn.log_sigmoid(g[2]), jax.nn.log_sigmoid(g[3])
    h_f, st_f = mlstm_chunk_scan(q, k, v, i_f, lf_f, st_f)
    fl = lambda a: jnp.flip(a, axis=2)
    h_b, st_b = mlstm_chunk_scan(fl(q), fl(k), fl(v), fl(i_b), fl(lf_b), st_b)
    return h_f + fl(h_b), st_f, st_b


def heads(a, d):
    bsz, t, _ = a.shape
    return jnp.transpose(a.reshape(bsz, t, MLSTM_HEADS, d), (0, 2, 1, 3)).astype(jnp.float32)


def project(h, w_in, conv_qk, gate_bias):
    p = h @ w_in
    o1 = POOL_WIDTH
    o2 = o1 + 2 * QK_WIDTH
    o3 = o2 + MLSTM_WIDTH
    o4 = o3 + MLSTM_WIDTH
    pool_in = p[..., :o1]
    qk = jax.nn.silu(short_conv(p[..., o1:o2], conv_qk))
    q = heads(qk[..., :QK_WIDTH], QK_HEAD_DIM) * (QK_HEAD_DIM ** -0.5)
    k = heads(qk[..., QK_WIDTH:], QK_HEAD_DIM)
    v = heads(p[..., o2:o3], V_HEAD_DIM)
    o = p[..., o3:o4]
    gates = p[..., o4:] + gate_bias
    return pool_in, q, k, v, o, gates


def mlstm_output(hm, o, head_norm):
    hn = hm * lax.rsqrt(jnp.mean(hm * hm, axis=-1, keepdims=True) + EPS)
    bsz, _, t, _ = hm.shape
    hn = jnp.transpose(hn, (0, 2, 1, 3)).reshape(bsz, t, MLSTM_WIDTH).astype(o.dtype)
    return hn * head_norm * jax.nn.sigmoid(o)


def swiglu(h, w_up, w_down):
    u = h @ w_up
    g, a = jnp.split(u, 2, axis=-1)
    return (jax.nn.silu(g) * a) @ w_down


def setup_inputs(seed: int = 0) -> dict:
    key = jax.random.key(seed)
    ks = jax.random.split(key, 20)
    f32 = jnp.float32
    nrm = lambda k, shape, s: jax.random.normal(k, shape, f32) * s
    gate_i = nrm(ks[7], (DEPTH, 2 * MLSTM_HEADS), 0.1)
    gate_f = 3.0 + 3.0 * jax.random.uniform(ks[8], (DEPTH, 2 * MLSTM_HEADS), f32)
    return {
        'x': nrm(ks[0], (BATCH, SEQ, D_MODEL), 1.0),
        'c': nrm(ks[1], (BATCH, D_MODEL), 1.0),
        'ctx': nrm(ks[2], (BATCH, CTX_LEN, D_MODEL), 1.0),
        'c_ctx': nrm(ks[3], (D_MODEL,), 1.0),
        'w_ada': nrm(ks[4], (DEPTH, D_MODEL, 6 * D_MODEL), D_MODEL ** -0.5),
        'b_ada': nrm(ks[5], (DEPTH, 6 * D_MODEL), 0.02),
        'norm1': 1.0 + nrm(ks[6], (DEPTH, D_MODEL), 0.02),
        'w_in': nrm(ks[9], (DEPTH, D_MODEL, IN_WIDTH), D_MODEL ** -0.5),
        'conv_qk': nrm(ks[10], (DEPTH, CONV_W, 2 * QK_WIDTH), CONV_W ** -0.5),
        'gate_bias': jnp.concatenate([gate_i, gate_f], axis=-1),
        'pool_w': nrm(ks[11], (DEPTH, POOL_GROUPS, POOL_GROUP_DIM, POOL_GROUP_DIM), POOL_GROUP_DIM ** -0.5),
        'pool_scale': 1.0 + nrm(ks[12], (DEPTH, POOL_WIDTH), 0.02),
        'head_norm': 1.0 + nrm(ks[13], (DEPTH, MLSTM_WIDTH), 0.02),
        'w_out': nrm(ks[14], (DEPTH, MIX_WIDTH, D_MODEL), MIX_WIDTH ** -0.5),
        'norm2': 1.0 + nrm(ks[15], (DEPTH, D_MODEL), 0.02),
        'w_up': nrm(ks[16], (DEPTH, D_MODEL, 2 * D_FF), D_MODEL ** -0.5),
        'w_down': nrm(ks[17], (DEPTH, D_FF, D_MODEL), D_FF ** -0.5),
        'norm_f': 1.0 + nrm(ks[18], (D_MODEL,), 0.02),
    }


def reference(x, c, ctx, c_ctx, w_ada, b_ada, norm1, w_in, conv_qk, gate_bias, pool_w,
              pool_scale, head_norm, w_out, norm2, w_up, w_down, norm_f):
    bsz, t, _ = x.shape
    rows = t // GRID_W
    xc = ctx
    for l in range(DEPTH):
        last = l == DEPTH - 1
        mod = (jax.nn.silu(c) @ w_ada[l] + b_ada[l])[:, None, :]
        sh1, sc1, g1, sh2, sc2, g2 = jnp.split(mod, 6, axis=-1)
        mod_c = jax.nn.silu(c_ctx) @ w_ada[l] + b_ada[l]
        csh1, csc1, cg1, csh2, csc2, cg2 = jnp.split(mod_c, 6, axis=-1)

        hx = modulate(rmsnorm(x, norm1[l]), sh1, sc1)
        hc = modulate(rmsnorm(xc, norm1[l]), csh1, csc1)
        px, qx, kx, vx, ox, gx = project(hx, w_in[l], conv_qk[l], gate_bias[l])
        pc, qc, kc, vc, oc, gc = project(hc, w_in[l], conv_qk[l], gate_bias[l])

        hc_m, st_f, st_b = mlstm_bidir(qc, kc, vc, gc, zero_state(bsz), zero_state(bsz))
        hx_m, _, _ = mlstm_bidir(qx, kx, vx, gx, st_f, st_b)

        pool_x = pool_mixer(px.reshape(bsz, rows, GRID_W, POOL_WIDTH), pool_w[l], pool_scale[l])
        pool_x = pool_x.reshape(bsz, t, POOL_WIDTH)
        mix_x = jnp.concatenate([pool_x, mlstm_output(hx_m, ox, head_norm[l])], axis=-1) @ w_out[l]
        x = x + g1 * mix_x
        x = x + g2 * swiglu(modulate(rmsnorm(x, norm2[l]), sh2, sc2), w_up[l], w_down[l])

        if not last:
            pool_c = pool_mixer(pc, pool_w[l], pool_scale[l])
            mix_c = jnp.concatenate([pool_c, mlstm_output(hc_m, oc, head_norm[l])], axis=-1) @ w_out[l]
            xc = xc + cg1 * mix_c
            xc = xc + cg2 * swiglu(modulate(rmsnorm(xc, norm2[l]), csh2, csc2), w_up[l], w_down[l])
    return rmsnorm(x, norm_f)
```

```python
import numpy as np
import ml_dtypes
import concourse.bass as bass
import concourse.mybir as mybir
from concourse.bass_utils import run_bass_kernel_spmd

F32 = mybir.dt.float32
BF16 = mybir.dt.bfloat16
AF = mybir.ActivationFunctionType
ALU = mybir.AluOpType

NCORES = 8
D = 1024
T = 2048
TC = 256
NT = 16
NTT = 18
DFF = 2816
NJ = 22
INW = 2064
EPS = 1e-6
LN8 = float(np.log(8.0))


class Buf:
    __slots__ = ("name", "lw", "rd")

    def __init__(self, name):
        self.name = name
        self.lw = None
        self.rd = []


class Op:
    __slots__ = ("eng", "fn", "raw", "oth", "idx", "signal", "ticket", "dma", "dsem", "dticket", "waits", "extra")

    def __init__(self, eng, fn, dma):
        self.eng = eng
        self.fn = fn
        self.dma = dma
        self.raw = []
        self.oth = []
        self.signal = False
        self.ticket = 0
        self.dsem = None
        self.dticket = 0
        self.waits = []
        self.extra = []


ENGS = ["pe", "act", "dve", "pool", "sp"]
NDSEM = {"sp": 12, "pool": 8, "act": 4}


class Sched:
    def __init__(self):
        self.q = {e: [] for e in ENGS}
        self.pending = {e: [] for e in ENGS}
        self.dma_since = []
        self.final_dma = {}

    def add(self, eng, fn, reads=(), writes=(), dma=False):
        op = Op(eng, fn, dma)
        op.idx = len(self.q[eng])
        for b in reads:
            if b.lw is not None:
                op.raw.append(b.lw)
        for b in writes:
            if b.lw is not None:
                op.oth.append(b.lw)
            op.oth.extend(b.rd)
        for b in reads:
            b.rd.append(op)
        for b in writes:
            b.lw = op
            b.rd = []
        if self.pending[eng]:
            op.oth.extend(self.pending[eng])
            self.pending[eng] = []
        self.q[eng].append(op)
        if dma:
            self.dma_since.append(op)
        return op

    def barrier(self):
        lasts = []
        for e in ENGS:
            for op in reversed(self.q[e]):
                if not op.dma:
                    lasts.append(op)
                    break
        lasts.extend(self.dma_since)
        self.dma_since = []
        for e in ENGS:
            self.pending[e] = self.pending[e] + list(lasts)

    def finalize(self, nc, sems, dsems):
        for e in ENGS:
            for op in self.q[e]:
                need = {}
                dmad = []
                for kind, lst in (("raw", op.raw), ("oth", op.oth)):
                    for d in lst:
                        if d is op:
                            continue
                        if d.dma:
                            dmad.append(d)
                            continue
                        if d.eng == op.eng and not op.dma:
                            if d.eng == "pe":
                                continue
                            if kind != "raw":
                                continue
                            if op.idx - d.idx > 2:
                                continue
                        k = d.eng
                        if k not in need or need[k].idx < d.idx:
                            need[k] = d
                op.extra = (list(need.values()), dmad)
                for d in need.values():
                    d.signal = True
        for e in ENGS:
            cnt = 0
            k = 0
            hist = {}
            for op in self.q[e]:
                if op.dma:
                    n = NDSEM[e]
                    si = k % n
                    op.dsem = dsems[e][si]
                    op.dticket = 16 * (k // n + 1)
                    if k >= n:
                        op.waits.append((op.dsem, 16 * (k // n)))
                    k += 1
                elif op.signal:
                    cnt += 1
                    op.ticket = cnt
            self.final_dma[e] = k
        for e in ENGS:
            waited = {}
            for op in self.q[e]:
                need, dmad = op.extra
                ws = list(op.waits)
                for d in need:
                    ws.append((sems[d.eng], d.ticket))
                for d in dmad:
                    ws.append((d.dsem, d.dticket))
                out = []
                for s, v in ws:
                    key = id(s)
                    if waited.get(key, 0) >= v:
                        continue
                    waited[key] = v
                    out.append((s, v))
                op.waits = out

    def emit(self, eng, e, sems, dsems):
        for op in self.q[eng]:
            for s, v in op.waits:
                e.wait_ge(s, v)
            ins = op.fn(e)
            if op.dma:
                ins.then_inc(op.dsem, 16)
            elif op.signal:
                ins.then_inc(sems[eng], 1)
        if eng in NDSEM:
            k = self.final_dma.get(eng, 0)
            n = NDSEM[eng]
            for si in range(min(n, k)):
                cntd = (k - si + n - 1) // n
                e.wait_ge(dsems[eng][si], 16 * cntd)


def _pool_mats():
    wins = (2, 4, 8, 16)
    out = np.zeros((4, 2, 128, 128), np.float32)
    for gi, win in enumerate(wins):
        A = np.zeros((64, 64), np.float64)
        for pos in range(64):
            lo = min(max(pos - win // 2, 0), 63)
            hi = min(max(pos + win // 2 - 1, 0), 63)
            cnt = hi - lo + 1
            if hi >= lo:
                A[pos, lo:hi + 1] += 1.0 / cnt
            A[pos, pos] -= 1.0
        A2 = np.zeros((128, 128), np.float64)
        A2[:64, :64] = A
        A2[64:, 64:] = A
        R = A2.T.astype(np.float32)
        hi_ = R.astype(ml_dtypes.bfloat16).astype(np.float32)
        lo_ = (R - hi_).astype(ml_dtypes.bfloat16).astype(np.float32)
        out[gi, 0] = hi_
        out[gi, 1] = lo_
    return out


class StopBuild(Exception):
    pass


class Region:
    def __init__(self, mem):
        self.mem = mem
        self.off = 0
        self.hi = 0

    def take(self, nbytes, dt=F32):
        nb = (nbytes + 31) // 32 * 32
        o = self.off
        self.off += nb
        self.hi = max(self.hi, self.off)
        ap = self.mem[:, o // 4:(o + nb) // 4]
        if dt != F32:
            ap = ap.bitcast(dt)
            return ap[:, 0:nbytes // 2]
        return ap[:, 0:nbytes // 4]


def build_program(stop_after=None, dbg_cols=0):
    nc = bass.Bass("TRN2", target_bir_lowering=False)

    def din(name, shape, dt=F32):
        return nc.dram_tensor(name, list(shape), dt, kind="ExternalInput").ap()

    x_d = din("x", [2, T, D])
    ctx_d = din("ctx", [2, TC, D])
    cT_d = din("cT", [128, 24])
    wada_d = din("w_ada_l", [6, 128, 8 * 1024])
    bcol_d = din("b_ada_col", [128, 32])
    brow_d = din("b_ada_row", [2, 1024])
    n1c_d = din("norm1c", [128, 8])
    n2c_d = din("norm2c", [128, 8])
    normf_d = din("normf_row", [1, 1024])
    hn_d = din("hn_row", [1, 512])
    win_d = din("w_in_l", [8, 128, INW])
    conv_d = din("conv_c", [128, 12])
    gb_d = din("gb_row", [1, 16])
    poolw_d = din("pool_w_l", [128, 512])
    poolsc_d = din("pool_sc", [128, 4])
    wout_d = din("w_out_l", [8, 128, 1024])
    wup_d = din("w_up_l", [NJ, 128, 2048])
    wdn_d = din("w_down_l", [NJ, 128, 1024])
    tri_d = din("tri", [3, 128, 128])
    ident_d = din("ident", [128, 128])
    ahl_d = din("ahl", [128, 1024])
    hmask_d = din("hmask", [128, 2])
    out_d = nc.dram_tensor("out", [2, T, D], F32, kind="ExternalOutput").ap()
    s_up = nc.dram_tensor("s_up", [NJ, 128, 2048], BF16, kind="Internal").ap()
    s_dn = nc.dram_tensor("s_dn", [NJ, 128, 1024], BF16, kind="Internal").ap()
    s_out = nc.dram_tensor("s_out", [8, 128, 1024], BF16, kind="Internal").ap()
    s_gt = nc.dram_tensor("s_gt", [2, 128, 1024], F32, kind="Internal").ap()
    dbg_d = None
    if dbg_cols:
        dbg_d = nc.dram_tensor("dbg", [128, dbg_cols], F32, kind="ExternalOutput").ap()

    S = Sched()
    TOTAL = 206000 // 4
    mem_g = nc.sbuf_tensor("mem", [128, TOTAL], F32)
    mem = mem_g.__enter__()
    ps_g = nc.psum_tensor("ps", [128, 4096], F32)
    ps = ps_g.__enter__()
    R = Region(mem)

    def bank(i, n=1):
        return ps[:, i * 512:(i + n) * 512]

    pb = [Buf("pb%d" % i) for i in range(8)]

    ident = R.take(256, BF16)
    tri_u = R.take(512)
    tri_l = R.take(512)
    ones_f = R.take(512)
    ahl = R.take(2048, BF16)
    poolw = R.take(1024, BF16)
    normf_t = R.take(4096)
    gbias_t = R.take(64)
    convc = R.take(48)
    poolsc = R.take(16)
    bcol = R.take(128)
    n1c = R.take(32)
    n2c = R.take(32)
    cT = R.take(96)
    scb = R.take(48, BF16)
    modcol = R.take(4 * 8 * 3 * 4)
    G1c = R.take(96)
    G2c = R.take(96)
    hmask = R.take(8)
    cst_m05 = R.take(4)
    cst_one = R.take(4)
    cst_ln8 = R.take(4)
    gtr = [R.take(4096) for _ in range(2)]
    small = [R.take(64) for _ in range(8)]
    A_end = R.off

    b_const = Buf("const")
    b_mod = Buf("mod")
    b_gt = Buf("gt")
    b_gtmp = Buf("gtmp")

    poolin = R.take(NT * 512 * 2, BF16)
    vaug = R.take(NTT * 516 * 2, BF16)
    sog = R.take(NT * 512 * 2, BF16)
    qT = R.take(2 * 2304 * 2, BF16)
    kT = R.take(2 * 2304 * 2, BF16)
    gsb = R.take(NTT * 16 * 4)
    E8 = R.take(NTT * 8 * 4)
    THR8 = R.take(NTT * 8 * 4)
    nlf = R.take(NTT * 8 * 4)
    a8 = R.take(NTT * 8 * 4)
    dec = R.take(NTT * 4 * 4)
    stR = [R.take(2 * 129 * 4) for _ in range(2)]
    R1_off = R.off
    hT = [R.take(8 * 512 * 2, BF16) for _ in range(2)]
    R.off = R1_off
    Cst = R.take(NT * 2 * 2 * 129 * 2, BF16)
    upring = [R.take(4096, BF16) for _ in range(3)]
    dnring = [R.take(2048, BF16) for _ in range(3)]
    DE_off = R.off

    w_in = R.take(8 * INW * 2, BF16)
    xs = [R.take(4096) for _ in range(2)]
    xn = [R.take(2048, BF16) for _ in range(2)]
    junk = R.take(2048, BF16)
    acc = [R.take(2048) for _ in range(2)]
    kz = [[R.take(512, BF16) for _ in range(2)] for _ in range(2)]
    otmp = [R.take(2048) for _ in range(1)]
    vpI = [R.take(1056, BF16) for _ in range(2)]
    hT.append(R.take(8 * 512 * 2, BF16))
    hhn_t = R.take(2048)
    D_end = R.off
    R.off = DE_off + 8 * INW * 2
    adaB = R.take(16384, BF16)
    browt = [R.take(4096) for _ in range(2)]
    gtmp = [R.take(4096) for _ in range(2)]
    screp = R.take(8 * 2 * 128 * 2, BF16)
    P_end = R.off
    adaA = mem[:, R1_off // 4:(R1_off + 16384) // 4].bitcast(BF16)

    R.off = DE_off
    x1b = [R.take(4096) for _ in range(4)]
    xn2 = [R.take(2048, BF16) for _ in range(1)]
    h2T = R.take(8 * 512 * 2, BF16)
    actT = R.take(NJ * 512 * 2, BF16)
    wout = actT[:, 0:8 * 1024]
    mixT = [R.take(8 * 128 * 2, BF16) for _ in range(1)]
    vpf = [R.take(1056, BF16) for _ in range(2)]
    vpb = [R.take(1056, BF16) for _ in range(2)]
    Pf = [R.take(1024, BF16) for _ in range(2)]
    Pb = [R.take(1024, BF16) for _ in range(2)]
    qz = [R.take(1024, BF16) for _ in range(2)]
    t12 = R.take(4096)
    t1 = t12[:, 0:512]
    t2 = t12[:, 512:1024]
    hm = t1
    tmp4k = t12
    mo = [R.take(1024, BF16) for _ in range(1)]
    dT = [R.take(1024, BF16) for _ in range(1)]
    sgt = [R.take(2048) for _ in range(1)]
    junk3 = R.take(2048, BF16)
    sqj = junk3[:, 0:512]
    E_end = R.off
    assert max(D_end, E_end, P_end) <= TOTAL * 4, (D_end, E_end, P_end, TOTAL * 4)

    def v3(ap, a, b):
        return ap.rearrange("p (a b) -> p a b", a=a, b=b)

    ahl_v = v3(ahl, 8, 128)
    poolw_v = v3(poolw, 4, 128)
    w_in_v = v3(w_in, 8, INW)
    wout_v = v3(wout, 8, 1024)
    hT_v = [v3(h, 8, 512) for h in hT]
    h2T_v = v3(h2T, 8, 512)
    actT_v = v3(actT, NJ, 512)
    qT_v = v3(qT, 2, 2304)
    kT_v = v3(kT, 2, 2304)
    poolin_v = v3(poolin, NT, 512)
    sog_v = v3(sog, NT, 512)
    vaug_v = vaug.rearrange("p (t h c) -> p t h c", t=NTT, h=4, c=129)
    gsb_v = v3(gsb, NTT, 16)
    E8_v = v3(E8, NTT, 8)
    THR8_v = v3(THR8, NTT, 8)
    nlf_v = v3(nlf, NTT, 8)
    a8_v = v3(a8, NTT, 8)
    dec_v = dec.rearrange("p (t d h) -> p t d h", t=NTT, d=2, h=2)
    Cst_v = Cst.rearrange("p (t d h c) -> p t d h c", t=NT, d=2, h=2, c=129)
    modcol_v = modcol.rearrange("p (s c v) -> p s c v", s=4, c=8, v=3)
    G1c_v = v3(G1c, 8, 3)
    G2c_v = v3(G2c, 8, 3)
    mixT_v = [v3(m, 8, 128) for m in mixT]

    b_win = Buf("w_in")
    b_hT = [Buf("hT%d" % i) for i in range(3)]
    b_xs = [Buf("xs%d" % i) for i in range(2)]
    b_xn = [Buf("xn%d" % i) for i in range(2)]
    b_junk = Buf("junk")
    b_small = [Buf("small%d" % i) for i in range(8)]
    b_acc = [Buf("acc%d" % i) for i in range(2)]
    b_kz = [Buf("kz%d" % i) for i in range(2)]
    b_qz = [Buf("qz%d" % i) for i in range(2)]
    b_otmp = [Buf("otmp%d" % i) for i in range(1)]
    b_vpI = [Buf("vpI%d" % i) for i in range(2)]
    b_poolin = [Buf("poolin%d" % i) for i in range(NT)]
    b_vaug = [Buf("vaug%d" % i) for i in range(NTT)]
    b_sog = [Buf("sog%d" % i) for i in range(NT)]
    b_qk = Buf("qk")
    b_gsb = Buf("gsb")
    b_gate = Buf("gate")
    b_st = [Buf("stR0"), Buf("stR1")]
    b_cst = Buf("cst")
    b_up = [Buf("up%d" % i) for i in range(3)]
    b_dn = [Buf("dn%d" % i) for i in range(3)]
    b_sup = [Buf("sup%d" % i) for i in range(NJ)]
    b_sdn = [Buf("sdn%d" % i) for i in range(NJ)]
    b_sout = [Buf("sout%d" % i) for i in range(8)]
    b_adaA = Buf("adaA")
    b_adaB = Buf("adaB")
    b_brow = Buf("brow")
    b_x1 = [Buf("x1_%d" % i) for i in range(4)]
    b_xn2 = [Buf("xn2_%d" % i) for i in range(1)]
    b_h2T = Buf("h2T")
    b_actT = Buf("actT")
    b_wout = b_actT
    b_mixT = [Buf("mixT%d" % i) for i in range(1)]
    b_vpf = [Buf("vpf%d" % i) for i in range(2)]
    b_vpb = [Buf("vpb%d" % i) for i in range(2)]
    b_Pf = [Buf("Pf%d" % i) for i in range(2)]
    b_Pb = [Buf("Pb%d" % i) for i in range(2)]
    b_t12 = Buf("t12")
    b_t1 = b_t12
    b_t2 = b_t12
    b_hm = b_t12
    b_tmp4k = b_t12
    b_mo = [Buf("mo%d" % i) for i in range(1)]
    b_dT = [Buf("dT%d" % i) for i in range(1)]
    b_sgt = [Buf("sgt%d" % i) for i in range(1)]
    b_junk3 = Buf("junk3")
    b_sqj = b_junk3
    b_hhn = Buf("hhn")
    b_sgt_dram = Buf("s_gt")
    b_out = Buf("outdram")

    small_i = [0]

    def next_small():
        i = small_i[0] % 8
        small_i[0] += 1
        return small[i], b_small[i]

    def dma(q, out, in_, reads=(), writes=()):
        return S.add(q, lambda e, o=out, i=in_: e.dma_start(out=o, in_=i), reads, writes, dma=True)

    def act(out, in_, func, reads, writes, scale=1.0, bias=None, accum=None):
        kw = {}
        if bias is not None:
            kw["bias"] = bias
        if accum is not None:
            kw["accum_out"] = accum
        return S.add("act", lambda e: e.activation(out=out, in_=in_, func=func, scale=scale, **kw), reads, writes)

    def tt(eng, out, in0, in1, op, reads, writes):
        return S.add(eng, lambda e: e.tensor_tensor(out=out, in0=in0, in1=in1, op=op), reads, writes)

    def ts(eng, out, in0, s1, s2, op0, op1, reads, writes):
        if s2 is None:
            return S.add(eng, lambda e: e.tensor_scalar(out=out, in0=in0, scalar1=s1, scalar2=None, op0=op0), reads, writes)
        return S.add(eng, lambda e: e.tensor_scalar(out=out, in0=in0, scalar1=s1, scalar2=s2, op0=op0, op1=op1), reads, writes)

    def stt(eng, out, in0, sc, in1, op0, op1, reads, writes):
        return S.add(eng, lambda e: e.scalar_tensor_tensor(out=out, in0=in0, scalar=sc, in1=in1, op0=op0, op1=op1), reads, writes)

    def cp(eng, out, in_, reads, writes):
        if eng == "act":
            return S.add(eng, lambda e: e.activation(out=out, in_=in_, func=AF.Copy), reads, writes)
        return S.add(eng, lambda e: e.tensor_copy(out, in_), reads, writes)

    def mm(out, lhsT, rhs, start, stop, reads, writes, tp=None):
        if tp is None:
            return S.add("pe", lambda e: e.matmul(out, lhsT=lhsT, rhs=rhs, start=start, stop=stop), reads, writes)
        return S.add("pe", lambda e: e.matmul(out, lhsT=lhsT, rhs=rhs, start=start, stop=stop, tile_position=tp), reads, writes)

    def tr(out, in_, reads, writes):
        return S.add("pe", lambda e: e.transpose(out, in_, ident), list(reads) + [b_const], writes)

    def rstd_of(ss_ap, b_ss, n):
        ms, b_ms = next_small()
        ts("dve", ms[:, 0:1], ss_ap, 1.0 / n, EPS, ALU.mult, ALU.add, [b_ss], [b_ms])
        rs, b_rs = next_small()
        tt("pool", rs[:, 0:1], ms[:, 0:1], cst_m05[:, 0:1], ALU.pow, [b_ms, b_const], [b_rs])
        return rs, b_rs

    S.add("pool", lambda e: e.memset(cst_m05[:, 0:1], -0.5), [], [b_const])
    S.add("pool", lambda e: e.memset(cst_one[:, 0:1], 1.0), [], [b_const])
    S.add("pool", lambda e: e.memset(cst_ln8[:, 0:1], LN8), [], [b_const])
    pc_ = {n: Buf("c_" + n) for n in ["hmask", "tri_u", "tri_l", "ones", "convc", "poolsc", "bcol", "n1c", "n2c", "cT", "gbias",
                                      "normf", "brow0", "brow1", "ident", "ahl", "poolw"]}
    for nm, dst, src in [
        ("cT", cT, cT_d[:, :]), ("bcol", bcol, bcol_d[:, :]), ("n1c", n1c, n1c_d[:, :]), ("n2c", n2c, n2c_d[:, :]),
        ("tri_u", tri_u, tri_d[0]), ("tri_l", tri_l, tri_d[1]), ("ones", ones_f, tri_d[2]), ("convc", convc, conv_d[:, :]),
        ("poolsc", poolsc, poolsc_d[:, :]), ("hmask", hmask, hmask_d[:, :]),
        ("gbias", gbias_t, gb_d.partition_broadcast(128)), ("normf", normf_t, normf_d.partition_broadcast(128)),
        ("brow0", browt[0], brow_d[0:1, :].partition_broadcast(128)), ("brow1", browt[1], brow_d[1:2, :].partition_broadcast(128)),
    ]:
        dma("sp", dst, src, [], [pc_[nm]])
    dma("pool", ident, ident_d[:, :], [], [pc_["ident"]])
    dma("pool", ahl, ahl_d[:, :], [], [pc_["ahl"]])
    dma("pool", poolw, poolw_d[:, :], [], [pc_["poolw"]])
    S.add("pool", lambda e: e.memset(vaug_v[:, :, :, 128:129], 1.0), [], b_vaug)

    act(scb[:, 0:24], cT[:, 0:24], AF.Silu, [pc_["cT"]], [b_mod])
    scb_v = v3(scb[:, 0:24], 8, 3)
    screp_v = screp.rearrange("p (c b m) -> p c b m", c=8, b=2, m=128)
    cp("dve", screp_v, scb_v[:, :, 0:2].unsqueeze(3).broadcast_to([128, 8, 2, 128]), [b_mod], [b_mod])

    adaA_v = v3(adaA, 8, 1024)
    adaB_v = v3(adaB, 8, 1024)
    seg_order = [0, 1, 3, 4, 2, 5]
    col_si = {0: 0, 1: 1, 3: 2, 4: 3}
    psc = bank(0)[:, 0:96].rearrange("p (s c v) -> p s c v", s=4, c=8, v=3)
    for k, seg in enumerate(seg_order):
        stg, stg_v, b_stg = (adaA, adaA_v, b_adaA) if k % 2 == 0 else (adaB, adaB_v, b_adaB)
        dma("pool", stg, wada_d[seg], [], [b_stg])
        if k == 1:
            for c in range(8):
                dma("pool", w_in_v[:, c, :], win_d[c], [], [b_win])
        if seg in col_si:
            si = col_si[seg]
            for pc in range(8):
                for c in range(8):
                    mm(psc[:, si, pc, :], stg_v[:, c, pc * 128:(pc + 1) * 128], scb_v[:, c, :], c == 0, c == 7,
                       [b_stg, b_mod], [pb[0]])
            tt("dve", modcol_v[:, si], psc[:, si], v3(bcol[:, 0:32], 4, 8)[:, si].unsqueeze(2).broadcast_to([128, 8, 3]),
               ALU.add, [pb[0], pc_["bcol"]], [b_mod])
        else:
            gi = 0 if seg == 2 else 1
            for b in range(2):
                for half in range(2):
                    bk = 1 + (b * 2 + half) % 4
                    for c in range(8):
                        mm(bank(bk), screp_v[:, c, b, :], stg_v[:, c, half * 512:(half + 1) * 512], c == 0, c == 7,
                           [b_stg, b_mod], [pb[bk]])
                    gdst = gtr[gi] if b == 0 else gtmp[gi]
                    tt("dve", gdst[:, half * 512:(half + 1) * 512], bank(bk), browt[gi][:, half * 512:(half + 1) * 512],
                       ALU.add, [pb[bk], pc_["brow%d" % gi]], [b_gt if b == 0 else b_gtmp])
            dma("sp", s_gt[gi], gtmp[gi][:, 0:1024], [b_gtmp], [b_sgt_dram])
    stt("dve", G1c_v, modcol_v[:, 1], 1.0, n1c[:, 0:8].unsqueeze(2).broadcast_to([128, 8, 3]), ALU.add, ALU.mult,
        [b_mod, pc_["n1c"]], [b_mod])
    stt("dve", G2c_v, modcol_v[:, 3], 1.0, n2c[:, 0:8].unsqueeze(2).broadcast_to([128, 8, 3]), ALU.add, ALU.mult,
        [b_mod, pc_["n2c"]], [b_mod])
    S1c_v = modcol_v[:, 0]
    S2c_v = modcol_v[:, 2]

    S.barrier()

    conv_jobs = []

    def _mk_up(j):
        def f():
            dma("pool", upring[j % 3], wup_d[j], [], [b_up[j % 3]])
            dma("pool", s_up[j], upring[j % 3], [b_up[j % 3]], [b_sup[j]])
        return f

    def _mk_dn(j):
        def f():
            dma("pool", dnring[j % 3], wdn_d[j], [], [b_dn[j % 3]])
            dma("pool", s_dn[j], dnring[j % 3], [b_dn[j % 3]], [b_sdn[j]])
        return f

    def _mk_out(c):
        def f():
            dma("pool", dnring[(c + 1) % 3], wout_d[c], [], [b_dn[(c + 1) % 3]])
            dma("pool", s_out[c], dnring[(c + 1) % 3], [b_dn[(c + 1) % 3]], [b_sout[c]])
        return f

    for c in range(8):
        conv_jobs.append(_mk_out(c))
    for j in range(NJ):
        conv_jobs.append(_mk_up(j))
        conv_jobs.append(_mk_dn(j))

    def pump_conv(n):
        for _ in range(n):
            if conv_jobs:
                conv_jobs.pop(0)()

    tp_i = [0]
    mmb_i = [0]

    def next_mm_bank():
        i = 4 + mmb_i[0] % 3
        mmb_i[0] += 1
        return i

    xs_i = [0]

    def norm_T(src_ap_fn, ntile, Gc, Sc, vec, dst_v, b_dst, load, xs_list, b_xs_list, xn_list, b_xn_list):
        for g0 in range(0, ntile, 2):
            pr = (tp_i[0] % 2) * 2
            tp_i[0] += 1
            tpv = bank(pr, 2).bitcast(BF16)[:, 0:2048].rearrange("p (c t) -> p c t", c=8, t=256)
            for tl in range(g0, min(g0 + 2, ntile)):
                xt, b_xt = src_ap_fn(tl)
                ss, b_ss = next_small()
                act(junk[:, 0:1024], xt, AF.Square, [b_xt], [b_junk, b_ss], accum=ss[:, 0:1])
                rs, b_rs = rstd_of(ss[:, 0:1], b_ss, D)
                k = xs_i[0] % len(xn_list)
                xs_i[0] += 1
                ts("dve", xn_list[k][:, 0:1024], xt, rs[:, 0:1], None, ALU.mult, None, [b_xt, b_rs], [b_xn_list[k]])
                for c in range(8):
                    tr(tpv[:, c, (tl - g0) * 128:(tl - g0 + 1) * 128], xn_list[k][:, c * 128:(c + 1) * 128],
                       [b_xn_list[k]], [pb[pr], pb[pr + 1]])
            n = min(2, ntile - g0) * 128
            for c in range(8):
                act(dst_v[:, c, g0 * 128:g0 * 128 + n], tpv[:, c, 0:n], AF.Identity, [pb[pr], pb[pr + 1], b_mod], [b_dst],
                    scale=Gc[:, c, vec:vec + 1], bias=Sc[:, c, vec:vec + 1])

    def phase1(b):
        if b > 0:
            for c in range(8):
                dma("pool", w_in_v[:, c, :], win_d[c], [], [b_win])
        dma("sp", hhn_t[:, 0:512], hn_d.partition_broadcast(128), [], [b_hhn])
        ts("dve", hhn_t[:, 0:512], hhn_t[:, 0:512], 0.5, None, ALU.mult, None, [b_hhn], [b_hhn])
        seqs = [("ctx", 0, 1, 2, 2), ("x", 256, 4, 4, b)]
        blk_id = [0]
        xload_i = [0]
        for name, tokoff, nblk, tpb, vec in seqs:
            src_d = ctx_d if name == "ctx" else x_d
            slots = {}

            def a1(bi):
                sl = blk_id[0] % 3
                blk_id[0] += 1
                slots[bi] = sl
                loaded = {}

                def src(tl):
                    pump_conv(3)
                    k = xload_i[0] % 2
                    xload_i[0] += 1
                    r0 = (bi * tpb + tl) * 128
                    dma("sp", xs[k][:, 0:1024], src_d[b, r0:r0 + 128, :], [], [b_xs[k]])
                    return xs[k][:, 0:1024], b_xs[k]
                norm_T(src, tpb, G1c_v, S1c_v, vec, hT_v[sl], b_hT[sl], None, xs, b_xs, xn, b_xn)

            def a2(bi):
                sl = slots[bi]
                n = tpb * 128
                h_v = hT_v[sl]
                for tl in range(tpb):
                    tti = (0 if name == "ctx" else 2) + bi * tpb + tl
                    groups = ["v", "gates"] if name == "ctx" else ["pool", "v", "o", "gates"]
                    for grp in groups:
                        c0, ncol = {"pool": (0, 512), "v": (1024, 512), "o": (1536, 512), "gates": (2048, 16)}[grp]
                        bk = next_mm_bank()
                        for c in range(8):
                            mm(bank(bk)[:, 0:ncol], h_v[:, c, tl * 128:(tl + 1) * 128], w_in_v[:, c, c0:c0 + ncol],
                               c == 0, c == 7, [b_hT[sl], b_win], [pb[bk]])
                        if grp == "pool":
                            ti = bi * tpb + tl
                            act(poolin_v[:, ti, :], bank(bk), AF.Copy, [pb[bk]], [b_poolin[ti]])
                        elif grp == "v":
                            cp("dve", vaug_v[:, tti, :, 0:128], v3(bank(bk), 4, 128), [pb[bk]], [b_vaug[tti]])
                        elif grp == "o":
                            ti = bi * tpb + tl
                            act(otmp[0][:, 0:512], bank(bk), AF.Tanh, [pb[bk]], [b_otmp[0]], scale=0.5)
                            stt("dve", sog_v[:, ti, :], otmp[0][:, 0:512], 1.0, hhn_t[:, 0:512], ALU.add, ALU.mult,
                                [b_otmp[0], b_hhn], [b_sog[ti]])
                        else:
                            tt("dve", gsb_v[:, tti, :], bank(bk)[:, 0:16], gbias_t[:, 0:16], ALU.add,
                               [pb[bk], b_const], [b_gsb])
                left = bi > 0
                right = bi < nblk - 1
                hb = None
                if left or right:
                    hb = 7
                    halo = bank(hb)[:, 0:8]
                for qc in range(4):
                    bk = next_mm_bank()
                    cw = 512 + qc * 128
                    for c in range(8):
                        mm(bank(bk)[:, 0:n], w_in_v[:, c, cw:cw + 128], h_v[:, c, 0:n], c == 0, c == 7,
                           [b_hT[sl], b_win], [pb[bk]])
                    if left:
                        hp = hT_v[slots[bi - 1]]
                        for c in range(8):
                            mm(halo[:, 2 * qc:2 * qc + 1], w_in_v[:, c, cw:cw + 128], hp[:, c, 511:512], c == 0, c == 7,
                               [b_hT[slots[bi - 1]], b_win], [pb[hb]])
                    if right:
                        hn_ = hT_v[slots[bi + 1]]
                        for c in range(8):
                            mm(halo[:, 2 * qc + 1:2 * qc + 2], w_in_v[:, c, cw:cw + 128], hn_[:, c, 0:1], c == 0, c == 7,
                               [b_hT[slots[bi + 1]], b_win], [pb[hb]])
                    k = qc % 2
                    a = acc[k]
                    psb = bank(bk)
                    act(a[:, 0:n], psb[:, 0:n], AF.Identity, [pb[bk], b_const], [b_acc[k]], scale=convc[:, qc * 3 + 1:qc * 3 + 2])
                    stt("dve", a[:, 1:n], psb[:, 0:n - 1], convc[:, qc * 3:qc * 3 + 1], a[:, 1:n], ALU.mult, ALU.add,
                        [pb[bk], b_const, b_acc[k]], [b_acc[k]])
                    stt("dve", a[:, 0:n - 1], psb[:, 1:n], convc[:, qc * 3 + 2:qc * 3 + 3], a[:, 0:n - 1], ALU.mult, ALU.add,
                        [pb[bk], b_const, b_acc[k]], [b_acc[k]])
                    if left:
                        stt("dve", a[:, 0:1], halo[:, 2 * qc:2 * qc + 1], convc[:, qc * 3:qc * 3 + 1], a[:, 0:1], ALU.mult, ALU.add,
                            [pb[hb], b_const, b_acc[k]], [b_acc[k]])
                    if right:
                        stt("dve", a[:, n - 1:n], halo[:, 2 * qc + 1:2 * qc + 2], convc[:, qc * 3 + 2:qc * 3 + 3], a[:, n - 1:n],
                            ALU.mult, ALU.add, [pb[hb], b_const, b_acc[k]], [b_acc[k]])
                    dstv = qT_v if qc < 2 else kT_v
                    t0 = tokoff + bi * n
                    act(dstv[:, qc % 2, t0:t0 + n], a[:, 0:n], AF.Silu, [b_acc[k]], [b_qk])

            a1(0)
            for bi in range(nblk):
                if bi + 1 < nblk:
                    a1(bi + 1)
                a2(bi)
        pump_conv(1000)

    def phase1b(b):
        act(a8_v[:, :, :], gsb_v[:, :, 8:16], AF.Exp, [b_gsb], [b_gate], scale=-1.0)
        act(nlf_v[:, :, :], a8_v[:, :, :], AF.Ln, [b_gate, b_const], [b_gate], bias=cst_one[:, 0:1])
        cs = bank(0)[:, 0:NTT * 8].rearrange("p (t g) -> p t g", t=NTT, g=8)
        bl = bank(1)[:, 0:NTT * 8].rearrange("p (t g) -> p t g", t=NTT, g=8)
        for tti in range(NTT):
            mm(cs[:, tti, 0:4], tri_u[:, 0:128], nlf_v[:, tti, 0:4], True, True, [b_gate, b_const], [pb[0]])
            mm(cs[:, tti, 4:8], tri_l[:, 0:128], nlf_v[:, tti, 4:8], True, True, [b_gate, b_const], [pb[0]])
            mm(bl[:, tti, :], ones_f[:, 0:128], nlf_v[:, tti, :], True, True, [b_gate, b_const], [pb[1]])
        tt("dve", a8_v[:, :, :], gsb_v[:, :, 0:8], cs, ALU.add, [b_gsb, pb[0], b_gate], [b_gate])
        act(E8_v[:, :, :], a8_v[:, :, :], AF.Exp, [b_gate], [b_gate])
        act(THR8_v[:, :, :], cs, AF.Exp, [pb[0], b_const], [b_gate], bias=cst_ln8[:, 0:1])
        blv = bl.rearrange("p t (d h) -> p t d h", d=2, h=4)
        act(dec_v[0:64], blv[0:64, :, :, 0:4:2], AF.Exp, [pb[1]], [b_gate], scale=-1.0)
        act(dec_v[64:128], blv[64:128, :, :, 1:4:2], AF.Exp, [pb[1]], [b_gate], scale=-1.0)

        def tok0(tti):
            return tti * 128 if tti < 2 else 256 + (tti - 2) * 128

        for k_ in range(2):
            for hf in range(2):
                S.add("pool", lambda e, a=kz[k_][hf]: e.memset(a[:, 0:256], 0.0), [], [b_kz[k_]])
        for d, order in ((0, list(range(NTT))), (1, [1, 0] + list(range(NTT - 1, 1, -1)))):
            Rv = v3(stR[d][:, 0:258], 2, 129)
            prev = None
            for step, tti in enumerate(order):
                k = step % 2
                kb = 2 + k
                ktp = bank(kb).bitcast(BF16)[:, 0:256]
                t0 = tok0(tti)
                for pr in range(2):
                    tr(ktp[:, pr * 128:(pr + 1) * 128], kT_v[:, pr, t0:t0 + 128], [b_qk], [pb[kb]])
                ktp_v = v3(ktp, 2, 128)
                cp("act", v3(kz[k][0][:, 0:256], 2, 128)[:, :, 0:64], ktp_v[:, :, 0:64], [pb[kb]], [b_kz[k]])
                cp("act", v3(kz[k][1][:, 0:256], 2, 128)[:, :, 64:128], ktp_v[:, :, 64:128], [pb[kb]], [b_kz[k]])
                vp = vpI[k][:, 0:516].rearrange("p (h c) -> p h c", h=4, c=129)
                tt("dve", vp, vaug_v[:, tti], E8_v[:, tti, d * 4:d * 4 + 4].unsqueeze(2).broadcast_to([128, 4, 129]), ALU.mult,
                   [b_vaug[tti], b_gate], [b_vpI[k]])
                ub = 4 + d * 2 + k
                U = bank(ub)[:, 0:258].rearrange("p (h c) -> p h c", h=2, c=129)
                for pr in range(2):
                    mm(U[:, pr, :], kz[k][0][:, pr * 128:(pr + 1) * 128], vp[:, 2 * pr, :], True, False,
                       [b_kz[k], b_vpI[k]], [pb[ub]])
                    mm(U[:, pr, :], kz[k][1][:, pr * 128:(pr + 1) * 128], vp[:, 2 * pr + 1, :], False, True,
                       [b_kz[k], b_vpI[k]], [pb[ub]])
                if prev is None:
                    cp("dve", Rv, U, [pb[ub]], [b_st[d]])
                else:
                    for hh in range(2):
                        dcol = dec_v[:, prev, d, hh:hh + 1]
                        if tti >= 2:
                            act(Cst_v[:, tti - 2, d, hh, :], Rv[:, hh, :], AF.Identity, [b_st[d], b_gate], [b_cst], scale=dcol)
                        stt("dve", Rv[:, hh, :], Rv[:, hh, :], dcol, U[:, hh, :], ALU.mult, ALU.add,
                            [b_st[d], b_gate, pb[ub]], [b_st[d]])
                prev = tti

    def phase3(b):
        if b > 0:
            for gi in range(2):
                dma("sp", gtr[gi][:, 0:1024], s_gt[gi], [b_sgt_dram], [b_gt])
        for nb in range(4):
            for c in range(8):
                dma("sp", wout_v[:, c, :], s_out[c], [b_sout[c]], [b_wout])
            for tl in range(4):
                r0 = (nb * 4 + tl) * 128
                dma("sp", x1b[tl][:, 0:1024], x_d[b, r0:r0 + 128, :], [], [b_x1[tl]])
            for tl in range(4):
                t = nb * 4 + tl
                tti = t + 2
                t0 = 256 + t * 128
                k = t % 2
                if stop_after == "3a0":
                    raise StopBuild()
                Sps = v3(bank(0), 4, 128)
                qz_v = qz[k][:, 0:512].rearrange("p (f r c) -> p f r c", f=2, r=2, c=128)
                for hf in range(2):
                    act(qz_v[:, hf], qT_v[:, :, t0:t0 + 128], AF.Copy, [b_qk, b_const], [b_qz[k]], scale=hmask[:, hf:hf + 1])
                for h in range(4):
                    mm(Sps[:, h, :], kT_v[:, h // 2, t0:t0 + 128], qz_v[:, h % 2, h // 2, :], True, True,
                       [b_qk, b_qz[k]], [pb[0]])
                if stop_after == "3a1":
                    raise StopBuild()
                Pf_v = v3(Pf[k][:, 0:512], 4, 128)
                Pb_v = v3(Pb[k][:, 0:512], 4, 128)
                tt("dve", Pf_v, Sps, tri_u[:, 0:128].unsqueeze(1).broadcast_to([128, 4, 128]), ALU.mult, [pb[0], b_const], [b_Pf[k]])
                tt("dve", Pb_v, Sps, tri_l[:, 0:128].unsqueeze(1).broadcast_to([128, 4, 128]), ALU.mult, [pb[0], b_const], [b_Pb[k]])
                vf = vpf[k][:, 0:516].rearrange("p (h c) -> p h c", h=4, c=129)
                vb = vpb[k][:, 0:516].rearrange("p (h c) -> p h c", h=4, c=129)
                if stop_after == "3a2":
                    raise StopBuild()
                tt("pool", vf, vaug_v[:, tti], E8_v[:, tti, 0:4].unsqueeze(2).broadcast_to([128, 4, 129]), ALU.mult,
                   [b_vaug[tti], b_gate], [b_vpf[k]])
                tt("pool", vb, vaug_v[:, tti], E8_v[:, tti, 4:8].unsqueeze(2).broadcast_to([128, 4, 129]), ALU.mult,
                   [b_vaug[tti], b_gate], [b_vpb[k]])
                if stop_after == "3a":
                    raise StopBuild()
                den = bank(3)[:, 0:8]
                for d, (Pv, vv, bP, bV) in enumerate(((Pf_v, vf, b_Pf[k], b_vpf[k]), (Pb_v, vb, b_Pb[k], b_vpb[k]))):
                    NUM = v3(bank(1 + d), 4, 128)
                    for h in range(4):
                        qh = qz_v[:, h % 2, h // 2, :]
                        mm(NUM[:, h, :], Pv[:, h, :], vv[:, h, 0:128], True, False, [bP, bV], [pb[1 + d]])
                        mm(NUM[:, h, :], qh, Cst_v[:, t, d, h // 2, 0:128], False, True,
                           [b_qz[k], b_cst], [pb[1 + d]])
                        mm(den[:, d * 4 + h:d * 4 + h + 1], Pv[:, h, :], vv[:, h, 128:129], True, False, [bP, bV], [pb[3]])
                        mm(den[:, d * 4 + h:d * 4 + h + 1], qh, Cst_v[:, t, d, h // 2, 128:129], False, True,
                           [b_qz[k], b_cst], [pb[3]])
                if stop_after == "3b0":
                    raise StopBuild()
                dabs, b_dabs = next_small()
                act(dabs[:, 0:8], den, AF.Abs, [pb[3]], [b_dabs])
                dm, b_dm = next_small()
                tt("dve", dm[:, 0:8], dabs[:, 0:8], THR8_v[:, tti, :], ALU.max, [b_dabs, b_gate], [b_dm])
                r8, b_r8 = next_small()
                S.add("dve", lambda e, o=r8, i=dm: e.reciprocal(out=o[:, 0:8], in_=i[:, 0:8]), [b_dm], [b_r8])
                if stop_after == "3b":
                    raise StopBuild()
                tt("dve", v3(t1[:, 0:512], 4, 128), v3(bank(1), 4, 128), r8[:, 0:4].unsqueeze(2).broadcast_to([128, 4, 128]), ALU.mult,
                   [pb[1], b_r8], [b_t1])
                tt("dve", v3(t2[:, 0:512], 4, 128), v3(bank(2), 4, 128), r8[:, 4:8].unsqueeze(2).broadcast_to([128, 4, 128]), ALU.mult,
                   [pb[2], b_r8], [b_t2])
                tt("pool", hm[:, 0:512], t1[:, 0:512], t2[:, 0:512], ALU.add, [b_t1, b_t2], [b_hm])
                ss4, b_ss4 = next_small()
                for h in range(4):
                    act(sqj[:, h * 128:(h + 1) * 128], hm[:, h * 128:(h + 1) * 128], AF.Square, [b_hm], [b_sqj, b_ss4],
                        accum=ss4[:, h:h + 1])
                k1 = 0
                ms4, b_ms4 = next_small()
                ts("dve", ms4[:, 0:4], ss4[:, 0:4], 1.0 / 128, EPS, ALU.mult, ALU.add, [b_ss4], [b_ms4])
                rs4, b_rs4 = next_small()
                tt("pool", rs4[:, 0:4], ms4[:, 0:4], cst_m05[:, 0:1].broadcast_to([128, 4]), ALU.pow, [b_ms4, b_const], [b_rs4])
                for h in range(4):
                    stt("dve", mo[k1][:, h * 128:(h + 1) * 128], hm[:, h * 128:(h + 1) * 128], rs4[:, h:h + 1],
                        sog_v[:, t, h * 128:(h + 1) * 128], ALU.mult, ALU.mult, [b_hm, b_rs4, b_sog[t]], [b_mo[k1]])
                if stop_after == "3c":
                    raise StopBuild()
                moT = bank(3).bitcast(BF16)[:, 512:1024].rearrange("p (h c) -> p h c", h=4, c=128)
                for h in range(4):
                    tr(moT[:, h, :], mo[k1][:, h * 128:(h + 1) * 128], [b_mo[k1]], [pb[3]])
                cp("act", mixT_v[k1][:, 4:8, :], moT, [pb[3]], [b_mixT[k1]])
                if stop_after == "3c2":
                    raise StopBuild()
                dps = v3(bank(4), 4, 128)
                for g in range(4):
                    mm(dps[:, g, :], poolin_v[:, t, g * 128:(g + 1) * 128], ahl_v[:, 2 * g, :], True, False,
                       [b_poolin[t], b_const], [pb[4]])
                    mm(dps[:, g, :], poolin_v[:, t, g * 128:(g + 1) * 128], ahl_v[:, 2 * g + 1, :], False, True,
                       [b_poolin[t], b_const], [pb[4]])
                cp("act", dT[k1][:, 0:512], bank(4), [pb[4]], [b_dT[k1]])
                pops = v3(bank(5), 4, 128)
                for g in range(4):
                    mm(pops[:, g, :], poolw_v[:, g, :], dT[k1][:, g * 128:(g + 1) * 128], True, True, [b_dT[k1], b_const], [pb[5]])
                tt("dve", mixT_v[k1][:, 0:4, :], pops, poolsc[:, 0:4].unsqueeze(2).broadcast_to([128, 4, 128]), ALU.mult,
                   [pb[5], b_const], [b_mixT[k1]])
                if stop_after == "3e":
                    raise StopBuild()
                for half in range(2):
                    for kc in range(8):
                        mm(bank(6 + half), mixT_v[k1][:, kc, :], wout_v[:, kc, half * 512:(half + 1) * 512], kc == 0, kc == 7,
                           [b_mixT[k1], b_wout], [pb[6 + half]])
                tt("dve", tmp4k[:, 0:1024], bank(6, 2), gtr[0][:, 0:1024], ALU.mult, [pb[6], pb[7], b_gt], [b_tmp4k])
                tt("pool", x1b[tl][:, 0:1024], tmp4k[:, 0:1024], x1b[tl][:, 0:1024], ALU.add, [b_tmp4k, b_x1[tl]], [b_x1[tl]])
                if stop_after == "3m1":
                    raise StopBuild()
            if stop_after == "3m":
                raise StopBuild()
            def src2(tl):
                return x1b[tl][:, 0:1024], b_x1[tl]
            norm_T(src2, 4, G2c_v, S2c_v, b, h2T_v, b_h2T, None, None, None, xn2, b_xn2)
            if stop_after == "3n":
                raise StopBuild()
            for j in range(min(2, NJ)):
                dma("sp", upring[j % 3], s_up[j], [b_sup[j]], [b_up[j % 3]])
            for j in range(NJ):
                if j + 2 < NJ:
                    dma("sp", upring[(j + 2) % 3], s_up[j + 2], [b_sup[j + 2]], [b_up[(j + 2) % 3]])
                if j == NJ - 2:
                    for jj in range(2):
                        dma("sp", dnring[jj % 3], s_dn[jj], [b_sdn[jj]], [b_dn[jj % 3]])
                upv = v3(upring[j % 3], 8, 256)
                gb_, ab_ = 2 + (j % 2), 4 + (j % 2)
                for c in range(8):
                    mm(bank(gb_), upv[:, c, 0:128], h2T_v[:, c, :], c == 0, c == 7, [b_up[j % 3], b_h2T], [pb[gb_]])
                for c in range(8):
                    mm(bank(ab_), upv[:, c, 128:256], h2T_v[:, c, :], c == 0, c == 7, [b_up[j % 3], b_h2T], [pb[ab_]])
                kk = 0
                act(sgt[kk][:, 0:512], bank(gb_), AF.Silu, [pb[gb_]], [b_sgt[kk]])
                tt("dve", actT_v[:, j, :], bank(ab_), sgt[kk][:, 0:512], ALU.mult, [pb[ab_], b_sgt[kk]], [b_actT])
            if stop_after == "3u":
                raise StopBuild()
            for j in range(NJ):
                if j + 2 < NJ:
                    dma("sp", dnring[(j + 2) % 3], s_dn[j + 2], [b_sdn[j + 2]], [b_dn[(j + 2) % 3]])
                for tl in range(4):
                    for half in range(2):
                        bk = tl * 2 + half
                        mm(bank(bk), actT_v[:, j, tl * 128:(tl + 1) * 128], dnring[j % 3][:, half * 512:(half + 1) * 512],
                           j == 0, j == NJ - 1, [b_actT, b_dn[j % 3]], [pb[bk]])
            for tl in range(4):
                t = nb * 4 + tl
                tt("dve", tmp4k[:, 0:1024], bank(tl * 2, 2), gtr[1][:, 0:1024], ALU.mult, [pb[tl * 2], pb[tl * 2 + 1], b_gt], [b_tmp4k])
                tt("pool", x1b[tl][:, 0:1024], tmp4k[:, 0:1024], x1b[tl][:, 0:1024], ALU.add, [b_tmp4k, b_x1[tl]], [b_x1[tl]])
                ss, b_ss = next_small()
                act(junk3[:, 0:1024], x1b[tl][:, 0:1024], AF.Square, [b_x1[tl]], [b_junk3, b_ss], accum=ss[:, 0:1])
                rs, b_rs = rstd_of(ss[:, 0:1], b_ss, D)
                stt("dve", x1b[tl][:, 0:1024], x1b[tl][:, 0:1024], rs[:, 0:1], normf_t[:, 0:1024], ALU.mult, ALU.mult,
                    [b_x1[tl], b_rs, b_const], [b_x1[tl]])
                dma("pool", out_d[b, t * 128:(t + 1) * 128, :], x1b[tl][:, 0:1024], [b_x1[tl]], [b_out])
            if stop_after == "3d":
                raise StopBuild()

    try:
        for b in range(2):
            phase1(b)
            if stop_after == "1" and b == 0:
                break
            phase1b(b)
            if stop_after == "1b" and b == 0:
                break
            S.barrier()
            phase3(b)
            S.barrier()
            if stop_after == "3" and b == 0:
                break
    except StopBuild:
        pass

    if dbg_d is not None:
        S.barrier()
        dma("pool", dbg_d[:, 0:dbg_cols], mem[:, 0:dbg_cols], [], [b_out])

    import contextlib
    with contextlib.ExitStack() as es:
        sems = {e: es.enter_context(nc.semaphore("s_" + e)) for e in ["pe", "act", "dve", "pool"]}
        dsems = {e: [es.enter_context(nc.semaphore("d_%s%d" % (e, i))) for i in range(n)] for e, n in NDSEM.items()}
        S.finalize(nc, sems, dsems)
        block = es.enter_context(nc.Block())

        @block.tensor
        def _(e):
            S.emit("pe", e, sems, dsems)

        @block.scalar
        def _(e):
            S.emit("act", e, sems, dsems)

        @block.vector
        def _(e):
            S.emit("dve", e, sems, dsems)

        @block.gpsimd
        def _(e):
            S.emit("pool", e, sems, dsems)

        @block.sync
        def _(e):
            S.emit("sp", e, sems, dsems)

    layout = dict(A_end=A_end, DE_off=DE_off, D_end=D_end, E_end=E_end, R1_off=R1_off, TOTAL=TOTAL)
    _aps = dict(poolin=poolin, vaug=vaug, sog=sog, qT=qT, kT=kT, gsb=gsb, E8=E8, THR8=THR8, nlf=nlf, a8=a8, dec=dec,
                Cst=Cst, modcol=modcol, G1c=G1c, G2c=G2c, gtr0=gtr[0], gtr1=gtr[1], stR0=stR[0], stR1=stR[1],
                x1b0=x1b[0], x1b1=x1b[1], x1b2=x1b[2], x1b3=x1b[3], h2T=h2T, actT=actT, mixT=mixT[0], hT0=hT[0], hT1=hT[1], hT2=hT[2])
    layout["aps"] = {k: (int(v.offset) * (2 if v.dtype == BF16 else 4), int(v.shape[1]), "bf16" if v.dtype == BF16 else "f32")
                     for k, v in _aps.items()}
    return nc, layout


def make_in_maps(x, c, ctx, c_ctx, w_ada, b_ada, norm1, w_in, conv_qk, gate_bias, pool_w,
                 pool_scale, head_norm, w_out, norm2, w_up, w_down, norm_f):
    f32 = np.float32
    x = np.asarray(x, f32)
    c = np.asarray(c, f32)
    ctx = np.asarray(ctx, f32)
    c_ctx = np.asarray(c_ctx, f32)
    w_ada = np.asarray(w_ada, f32)[0]
    b_ada = np.asarray(b_ada, f32)[0]
    norm1 = np.asarray(norm1, f32)[0]
    w_in = np.asarray(w_in, f32)[0]
    conv_qk = np.asarray(conv_qk, f32)[0]
    gate_bias = np.asarray(gate_bias, f32)[0]
    pool_w = np.asarray(pool_w, f32)[0]
    pool_scale = np.asarray(pool_scale, f32)[0]
    head_norm = np.asarray(head_norm, f32)[0]
    w_out = np.asarray(w_out, f32)[0]
    norm2 = np.asarray(norm2, f32)[0]
    w_up = np.asarray(w_up, f32)[0]
    w_down = np.asarray(w_down, f32)[0]
    norm_f = np.asarray(norm_f, f32)

    def colform(v, nchunk):
        return np.ascontiguousarray(v.reshape(nchunk, 128).T)

    w_ada_l = np.ascontiguousarray(w_ada.reshape(8, 128, 6, 1024).transpose(2, 1, 0, 3)).reshape(6, 128, 8 * 1024)
    bseg = b_ada.reshape(6, 1024)
    b_ada_col = np.ascontiguousarray(
        np.stack([colform(bseg[s], 8) for s in (0, 1, 3, 4)], axis=1)).reshape(128, 32)
    b_ada_row = np.ascontiguousarray(np.stack([bseg[2], bseg[5]], axis=0))
    w_in_l = np.ascontiguousarray(w_in.reshape(8, 128, INW))
    conv_c = np.ascontiguousarray(conv_qk.reshape(3, 4, 128).transpose(2, 1, 0)).reshape(128, 12)
    pool_w_l = np.ascontiguousarray(pool_w.transpose(1, 0, 2)).reshape(128, 512)
    pool_sc = colform(pool_scale, 4)
    w_out_l = np.ascontiguousarray(w_out.reshape(8, 128, 1024))
    wu = w_up.reshape(8, 128, 2, NJ, 128)
    w_up_l = np.ascontiguousarray(wu.transpose(3, 1, 0, 2, 4)).reshape(NJ, 128, 2048)
    w_down_l = np.ascontiguousarray(w_down.reshape(NJ, 128, 1024))
    s_idx = np.arange(128)
    tri = np.stack([
        (s_idx[:, None] <= s_idx[None, :]).astype(f32),
        (s_idx[:, None] >= s_idx[None, :]).astype(f32),
        np.ones((128, 128), f32)], axis=0)
    ident = np.eye(128, dtype=f32)
    pm = _pool_mats()
    ahl = np.ascontiguousarray(pm.transpose(2, 0, 1, 3)).reshape(128, 1024)

    shared = dict(
        w_ada_l=w_ada_l, b_ada_col=b_ada_col, b_ada_row=b_ada_row,
        norm1c=colform(norm1, 8), norm2c=colform(norm2, 8),
        normf_row=np.ascontiguousarray(norm_f.reshape(1, 1024)), hn_row=np.ascontiguousarray(head_norm.reshape(1, 512)),
        w_in_l=w_in_l, conv_c=conv_c, gb_row=np.ascontiguousarray(gate_bias.reshape(1, 16)),
        pool_w_l=pool_w_l, pool_sc=pool_sc, w_out_l=w_out_l, w_up_l=w_up_l, w_down_l=w_down_l,
        tri=tri, ident=ident, ahl=ahl,
        hmask=np.ascontiguousarray(np.stack([(s_idx < 64), (s_idx >= 64)], axis=1).astype(f32)),
    )
    in_maps = []
    for core in range(NCORES):
        b0 = 2 * core
        cT = np.stack([colform(c[b0], 8), colform(c[b0 + 1], 8), colform(c_ctx, 8)], axis=2).reshape(128, 24)
        m = dict(shared)
        m["x"] = np.ascontiguousarray(x[b0:b0 + 2])
        m["ctx"] = np.ascontiguousarray(ctx[b0:b0 + 2])
        m["cT"] = np.ascontiguousarray(cT)
        in_maps.append(m)
    return in_maps


_PROGRAM = None


def kernel(x, c, ctx, c_ctx, w_ada, b_ada, norm1, w_in, conv_qk, gate_bias, pool_w,
           pool_scale, head_norm, w_out, norm2, w_up, w_down, norm_f):
    global _PROGRAM
    in_maps = make_in_maps(x, c, ctx, c_ctx, w_ada, b_ada, norm1, w_in, conv_qk, gate_bias, pool_w,
                           pool_scale, head_norm, w_out, norm2, w_up, w_down, norm_f)
    if _PROGRAM is None:
        _PROGRAM = build_program()[0]
    res = run_bass_kernel_spmd(_PROGRAM, in_maps, core_ids=list(range(NCORES)))
    out = np.concatenate([np.asarray(r["out"], np.float32) for r in res.results], axis=0)
    return out
```

```python
import numpy as np
import ml_dtypes
import concourse.bass as bass
import concourse.mybir as mybir
from concourse.bass_utils import run_bass_kernel_spmd

F32 = mybir.dt.float32
BF16 = mybir.dt.bfloat16
AF = mybir.ActivationFunctionType
ALU = mybir.AluOpType

NCORES = 8
D = 1024
T = 2048
TC = 256
NT = 16
NTT = 18
DFF = 2816
NJ = 22
INW = 2064
EPS = 1e-6
LN8 = float(np.log(8.0))


class Buf:
    __slots__ = ("name", "lw", "rd")

    def __init__(self, name):
        self.name = name
        self.lw = None
        self.rd = []


class Op:
    __slots__ = ("eng", "fn", "raw", "oth", "idx", "signal", "ticket", "dma", "dsem", "dticket", "waits", "extra")

    def __init__(self, eng, fn, dma):
        self.eng = eng
        self.fn = fn
        self.dma = dma
        self.raw = []
        self.oth = []
        self.signal = False
        self.ticket = 0
        self.dsem = None
        self.dticket = 0
        self.waits = []
        self.extra = []


ENGS = ["pe", "act", "dve", "pool", "sp"]
NDSEM = {"sp": 12, "pool": 8, "act": 4}


class Sched:
    def __init__(self):
        self.q = {e: [] for e in ENGS}
        self.pending = {e: [] for e in ENGS}
        self.dma_since = []
        self.final_dma = {}

    def add(self, eng, fn, reads=(), writes=(), dma=False):
        op = Op(eng, fn, dma)
        op.idx = len(self.q[eng])
        for b in reads:
            if b.lw is not None:
                op.raw.append(b.lw)
        for b in writes:
            if b.lw is not None:
                op.oth.append(b.lw)
            op.oth.extend(b.rd)
        for b in reads:
            b.rd.append(op)
        for b in writes:
            b.lw = op
            b.rd = []
        if self.pending[eng]:
            op.oth.extend(self.pending[eng])
            self.pending[eng] = []
        self.q[eng].append(op)
        if dma:
            self.dma_since.append(op)
        return op

    def barrier(self):
        lasts = []
        for e in ENGS:
            for op in reversed(self.q[e]):
                if not op.dma:
                    lasts.append(op)
                    break
        lasts.extend(self.dma_since)
        self.dma_since = []
        for e in ENGS:
            self.pending[e] = self.pending[e] + list(lasts)

    def finalize(self, nc, sems, dsems):
        for e in ENGS:
            for op in self.q[e]:
                need = {}
                dmad = []
                for kind, lst in (("raw", op.raw), ("oth", op.oth)):
                    for d in lst:
                        if d is op:
                            continue
                        if d.dma:
                            dmad.append(d)
                            continue
                        if d.eng == op.eng and not op.dma:
                            if d.eng == "pe":
                                continue
                            if kind != "raw":
                                continue
                            if op.idx - d.idx > 2:
                                continue
                        k = d.eng
                        if k not in need or need[k].idx < d.idx:
                            need[k] = d
                op.extra = (list(need.values()), dmad)
                for d in need.values():
                    d.signal = True
        for e in ENGS:
            cnt = 0
            k = 0
            hist = {}
            for op in self.q[e]:
                if op.dma:
                    n = NDSEM[e]
                    si = k % n
                    op.dsem = dsems[e][si]
                    op.dticket = 16 * (k // n + 1)
                    if k >= n:
                        op.waits.append((op.dsem, 16 * (k // n)))
                    k += 1
                elif op.signal:
                    cnt += 1
                    op.ticket = cnt
            self.final_dma[e] = k
        for e in ENGS:
            waited = {}
            for op in self.q[e]:
                need, dmad = op.extra
                ws = list(op.waits)
                for d in need:
                    ws.append((sems[d.eng], d.ticket))
                for d in dmad:
                    ws.append((d.dsem, d.dticket))
                out = []
                for s, v in ws:
                    key = id(s)
                    if waited.get(key, 0) >= v:
                        continue
                    waited[key] = v
                    out.append((s, v))
                op.waits = out

    def emit(self, eng, e, sems, dsems):
        for op in self.q[eng]:
            for s, v in op.waits:
                e.wait_ge(s, v)
            ins = op.fn(e)
            if op.dma:
                ins.then_inc(op.dsem, 16)
            elif op.signal:
                ins.then_inc(sems[eng], 1)
        if eng in NDSEM:
            k = self.final_dma.get(eng, 0)
            n = NDSEM[eng]
            for si in range(min(n, k)):
                cntd = (k - si + n - 1) // n
                e.wait_ge(dsems[eng][si], 16 * cntd)


def _pool_mats():
    wins = (2, 4, 8, 16)
    out = np.zeros((4, 2, 128, 128), np.float32)
    for gi, win in enumerate(wins):
        A = np.zeros((64, 64), np.float64)
        for pos in range(64):
            lo = min(max(pos - win // 2, 0), 63)
            hi = min(max(pos + win // 2 - 1, 0), 63)
            cnt = hi - lo + 1
            if hi >= lo:
                A[pos, lo:hi + 1] += 1.0 / cnt
            A[pos, pos] -= 1.0
        A2 = np.zeros((128, 128), np.float64)
        A2[:64, :64] = A
        A2[64:, 64:] = A
        R = A2.T.astype(np.float32)
        hi_ = R.astype(ml_dtypes.bfloat16).astype(np.float32)
        lo_ = (R - hi_).astype(ml_dtypes.bfloat16).astype(np.float32)
        out[gi, 0] = hi_
        out[gi, 1] = lo_
    return out


class StopBuild(Exception):
    pass


class Region:
    def __init__(self, mem):
        self.mem = mem
        self.off = 0
        self.hi = 0

    def take(self, nbytes, dt=F32):
        nb = (nbytes + 31) // 32 * 32
        o = self.off
        self.off += nb
        self.hi = max(self.hi, self.off)
        ap = self.mem[:, o // 4:(o + nb) // 4]
        if dt != F32:
            ap = ap.bitcast(dt)
            return ap[:, 0:nbytes // 2]
        return ap[:, 0:nbytes // 4]


def build_program(stop_after=None, dbg_cols=0):
    nc = bass.Bass("TRN2", target_bir_lowering=False)

    def din(name, shape, dt=F32):
        return nc.dram_tensor(name, list(shape), dt, kind="ExternalInput").ap()

    x_d = din("x", [2, T, D])
    ctx_d = din("ctx", [2, TC, D])
    cT_d = din("cT", [128, 24])
    wada_d = din("w_ada_l", [6, 128, 8 * 1024])
    bcol_d = din("b_ada_col", [128, 32])
    brow_d = din("b_ada_row", [2, 1024])
    n1c_d = din("norm1c", [128, 8])
    n2c_d = din("norm2c", [128, 8])
    normf_d = din("normf_row", [1, 1024])
    hn_d = din("hn_row", [1, 512])
    win_d = din("w_in_l", [8, 128, INW])
    conv_d = din("conv_c", [128, 12])
    gb_d = din("gb_row", [1, 16])
    poolw_d = din("pool_w_l", [128, 512])
    poolsc_d = din("pool_sc", [128, 4])
    wout_d = din("w_out_l", [8, 128, 1024])
    wup_d = din("w_up_l", [NJ, 128, 2048])
    wdn_d = din("w_down_l", [NJ, 128, 1024])
    tri_d = din("tri", [3, 128, 128])
    ident_d = din("ident", [128, 128])
    ahl_d = din("ahl", [128, 1024])
    hmask_d = din("hmask", [128, 2])
    out_d = nc.dram_tensor("out", [2, T, D], F32, kind="ExternalOutput").ap()
    s_up = nc.dram_tensor("s_up", [NJ, 128, 2048], BF16, kind="Internal").ap()
    s_dn = nc.dram_tensor("s_dn", [NJ, 128, 1024], BF16, kind="Internal").ap()
    s_out = nc.dram_tensor("s_out", [8, 128, 1024], BF16, kind="Internal").ap()
    s_gt = nc.dram_tensor("s_gt", [2, 128, 1024], F32, kind="Internal").ap()
    dbg_d = None
    if dbg_cols:
        dbg_d = nc.dram_tensor("dbg", [128, dbg_cols], F32, kind="ExternalOutput").ap()

    S = Sched()
    TOTAL = 206000 // 4
    mem_g = nc.sbuf_tensor("mem", [128, TOTAL], F32)
    mem = mem_g.__enter__()
    ps_g = nc.psum_tensor("ps", [128, 4096], F32)
    ps = ps_g.__enter__()
    R = Region(mem)

    def bank(i, n=1):
        return ps[:, i * 512:(i + n) * 512]

    pb = [Buf("pb%d" % i) for i in range(8)]

    ident = R.take(256, BF16)
    tri_u = R.take(512)
    tri_l = R.take(512)
    ones_f = R.take(512)
    ahl = R.take(2048, BF16)
    poolw = R.take(1024, BF16)
    normf_t = R.take(4096)
    gbias_t = R.take(64)
    convc = R.take(48)
    poolsc = R.take(16)
    bcol = R.take(128)
    n1c = R.take(32)
    n2c = R.take(32)
    cT = R.take(96)
    scb = R.take(48, BF16)
    modcol = R.take(4 * 8 * 3 * 4)
    G1c = R.take(96)
    G2c = R.take(96)
    hmask = R.take(8)
    cst_m05 = R.take(4)
    cst_one = R.take(4)
    cst_ln8 = R.take(4)
    gtr = [R.take(4096) for _ in range(2)]
    small = [R.take(64) for _ in range(8)]
    A_end = R.off

    b_const = Buf("const")
    b_mod = Buf("mod")
    b_gt = Buf("gt")
    b_gtmp = Buf("gtmp")

    poolin = R.take(NT * 512 * 2, BF16)
    vaug = R.take(NTT * 516 * 2, BF16)
    sog = R.take(NT * 512 * 2, BF16)
    qT = R.take(2 * 2304 * 2, BF16)
    kT = R.take(2 * 2304 * 2, BF16)
    gsb = R.take(NTT * 16 * 4)
    E8 = R.take(NTT * 8 * 4)
    THR8 = R.take(NTT * 8 * 4)
    nlf = R.take(NTT * 8 * 4)
    a8 = R.take(NTT * 8 * 4)
    dec = R.take(NTT * 4 * 4)
    stR = [R.take(2 * 129 * 4) for _ in range(2)]
    R1_off = R.off
    hT = [R.take(8 * 512 * 2, BF16) for _ in range(2)]
    R.off = R1_off
    Cst = R.take(NT * 2 * 2 * 129 * 2, BF16)
    upring = [R.take(4096, BF16) for _ in range(3)]
    dnring = [R.take(2048, BF16) for _ in range(3)]
    DE_off = R.off

    w_in = R.take(8 * INW * 2, BF16)
    xs = [R.take(4096) for _ in range(2)]
    xn = [R.take(2048, BF16) for _ in range(2)]
    junk = R.take(2048, BF16)
    acc = [R.take(2048) for _ in range(2)]
    kz = [[R.take(512, BF16) for _ in range(2)] for _ in range(2)]
    otmp = [R.take(2048) for _ in range(1)]
    vpI = [R.take(1056, BF16) for _ in range(2)]
    hT.append(R.take(8 * 512 * 2, BF16))
    hhn_t = R.take(2048)
    D_end = R.off
    R.off = DE_off + 8 * INW * 2
    adaB = R.take(16384, BF16)
    browt = [R.take(4096) for _ in range(2)]
    gtmp = [R.take(4096) for _ in range(2)]
    screp = R.take(8 * 2 * 128 * 2, BF16)
    P_end = R.off
    adaA = mem[:, R1_off // 4:(R1_off + 16384) // 4].bitcast(BF16)

    R.off = DE_off
    x1b = [R.take(4096) for _ in range(4)]
    xn2 = [R.take(2048, BF16) for _ in range(1)]
    h2T = R.take(8 * 512 * 2, BF16)
    actT = R.take(NJ * 512 * 2, BF16)
    wout = actT[:, 0:8 * 1024]
    mixT = [R.take(8 * 128 * 2, BF16) for _ in range(1)]
    vpf = [R.take(1056, BF16) for _ in range(2)]
    vpb = [R.take(1056, BF16) for _ in range(2)]
    Pf = [R.take(1024, BF16) for _ in range(2)]
    Pb = [R.take(1024, BF16) for _ in range(2)]
    qz = [R.take(1024, BF16) for _ in range(2)]
    t12 = R.take(4096)
    t1 = t12[:, 0:512]
    t2 = t12[:, 512:1024]
    hm = t1
    tmp4k = t12
    mo = [R.take(1024, BF16) for _ in range(1)]
    dT = [R.take(1024, BF16) for _ in range(1)]
    sgt = [R.take(2048) for _ in range(1)]
    junk3 = R.take(2048, BF16)
    sqj = junk3[:, 0:512]
    E_end = R.off
    assert max(D_end, E_end, P_end) <= TOTAL * 4, (D_end, E_end, P_end, TOTAL * 4)

    def v3(ap, a, b):
        return ap.rearrange("p (a b) -> p a b", a=a, b=b)

    ahl_v = v3(ahl, 8, 128)
    poolw_v = v3(poolw, 4, 128)
    w_in_v = v3(w_in, 8, INW)
    wout_v = v3(wout, 8, 1024)
    hT_v = [v3(h, 8, 512) for h in hT]
    h2T_v = v3(h2T, 8, 512)
    actT_v = v3(actT, NJ, 512)
    qT_v = v3(qT, 2, 2304)
    kT_v = v3(kT, 2, 2304)
    poolin_v = v3(poolin, NT, 512)
    sog_v = v3(sog, NT, 512)
    vaug_v = vaug.rearrange("p (t h c) -> p t h c", t=NTT, h=4, c=129)
    gsb_v = v3(gsb, NTT, 16)
    E8_v = v3(E8, NTT, 8)
    THR8_v = v3(THR8, NTT, 8)
    nlf_v = v3(nlf, NTT, 8)
    a8_v = v3(a8, NTT, 8)
    dec_v = dec.rearrange("p (t d h) -> p t d h", t=NTT, d=2, h=2)
    Cst_v = Cst.rearrange("p (t d h c) -> p t d h c", t=NT, d=2, h=2, c=129)
    modcol_v = modcol.rearrange("p (s c v) -> p s c v", s=4, c=8, v=3)
    G1c_v = v3(G1c, 8, 3)
    G2c_v = v3(G2c, 8, 3)
    mixT_v = [v3(m, 8, 128) for m in mixT]

    b_win = Buf("w_in")
    b_hT = [Buf("hT%d" % i) for i in range(3)]
    b_xs = [Buf("xs%d" % i) for i in range(2)]
    b_xn = [Buf("xn%d" % i) for i in range(2)]
    b_junk = Buf("junk")
    b_small = [Buf("small%d" % i) for i in range(8)]
    b_acc = [Buf("acc%d" % i) for i in range(2)]
    b_kz = [Buf("kz%d" % i) for i in range(2)]
    b_qz = [Buf("qz%d" % i) for i in range(2)]
    b_otmp = [Buf("otmp%d" % i) for i in range(1)]
    b_vpI = [Buf("vpI%d" % i) for i in range(2)]
    b_poolin = [Buf("poolin%d" % i) for i in range(NT)]
    b_vaug = [Buf("vaug%d" % i) for i in range(NTT)]
    b_sog = [Buf("sog%d" % i) for i in range(NT)]
    b_qk = Buf("qk")
    b_gsb = Buf("gsb")
    b_gate = Buf("gate")
    b_st = [Buf("stR0"), Buf("stR1")]
    b_cst = Buf("cst")
    b_up = [Buf("up%d" % i) for i in range(3)]
    b_dn = [Buf("dn%d" % i) for i in range(3)]
    b_sup = [Buf("sup%d" % i) for i in range(NJ)]
    b_sdn = [Buf("sdn%d" % i) for i in range(NJ)]
    b_sout = [Buf("sout%d" % i) for i in range(8)]
    b_adaA = Buf("adaA")
    b_adaB = Buf("adaB")
    b_brow = Buf("brow")
    b_x1 = [Buf("x1_%d" % i) for i in range(4)]
    b_xn2 = [Buf("xn2_%d" % i) for i in range(1)]
    b_h2T = Buf("h2T")
    b_actT = Buf("actT")
    b_wout = b_actT
    b_mixT = [Buf("mixT%d" % i) for i in range(1)]
    b_vpf = [Buf("vpf%d" % i) for i in range(2)]
    b_vpb = [Buf("vpb%d" % i) for i in range(2)]
    b_Pf = [Buf("Pf%d" % i) for i in range(2)]
    b_Pb = [Buf("Pb%d" % i) for i in range(2)]
    b_t12 = Buf("t12")
    b_t1 = b_t12
    b_t2 = b_t12
    b_hm = b_t12
    b_tmp4k = b_t12
    b_mo = [Buf("mo%d" % i) for i in range(1)]
    b_dT = [Buf("dT%d" % i) for i in range(1)]
    b_sgt = [Buf("sgt%d" % i) for i in range(1)]
    b_junk3 = Buf("junk3")
    b_sqj = b_junk3
    b_hhn = Buf("hhn")
    b_sgt_dram = Buf("s_gt")
    b_out = Buf("outdram")

    small_i = [0]

    def next_small():
        i = small_i[0] % 8
        small_i[0] += 1
        return small[i], b_small[i]

    def dma(q, out, in_, reads=(), writes=()):
        return S.add(q, lambda e, o=out, i=in_: e.dma_start(out=o, in_=i), reads, writes, dma=True)

    def act(out, in_, func, reads, writes, scale=1.0, bias=None, accum=None):
        kw = {}
        if bias is not None:
            kw["bias"] = bias
        if accum is not None:
            kw["accum_out"] = accum
        return S.add("act", lambda e: e.activation(out=out, in_=in_, func=func, scale=scale, **kw), reads, writes)

    def tt(eng, out, in0, in1, op, reads, writes):
        return S.add(eng, lambda e: e.tensor_tensor(out=out, in0=in0, in1=in1, op=op), reads, writes)

    def ts(eng, out, in0, s1, s2, op0, op1, reads, writes):
        if s2 is None:
            return S.add(eng, lambda e: e.tensor_scalar(out=out, in0=in0, scalar1=s1, scalar2=None, op0=op0), reads, writes)
        return S.add(eng, lambda e: e.tensor_scalar(out=out, in0=in0, scalar1=s1, scalar2=s2, op0=op0, op1=op1), reads, writes)

    def stt(eng, out, in0, sc, in1, op0, op1, reads, writes):
        return S.add(eng, lambda e: e.scalar_tensor_tensor(out=out, in0=in0, scalar=sc, in1=in1, op0=op0, op1=op1), reads, writes)

    def cp(eng, out, in_, reads, writes):
        if eng == "act":
            return S.add(eng, lambda e: e.activation(out=out, in_=in_, func=AF.Copy), reads, writes)
        return S.add(eng, lambda e: e.tensor_copy(out, in_), reads, writes)

    def mm(out, lhsT, rhs, start, stop, reads, writes, tp=None):
        if tp is None:
            return S.add("pe", lambda e: e.matmul(out, lhsT=lhsT, rhs=rhs, start=start, stop=stop), reads, writes)
        return S.add("pe", lambda e: e.matmul(out, lhsT=lhsT, rhs=rhs, start=start, stop=stop, tile_position=tp), reads, writes)

    def tr(out, in_, reads, writes):
        return S.add("pe", lambda e: e.transpose(out, in_, ident), list(reads) + [b_const], writes)

    def rstd_of(ss_ap, b_ss, n):
        ms, b_ms = next_small()
        ts("dve", ms[:, 0:1], ss_ap, 1.0 / n, EPS, ALU.mult, ALU.add, [b_ss], [b_ms])
        rs, b_rs = next_small()
        tt("pool", rs[:, 0:1], ms[:, 0:1], cst_m05[:, 0:1], ALU.pow, [b_ms, b_const], [b_rs])
        return rs, b_rs

    S.add("pool", lambda e: e.memset(cst_m05[:, 0:1], -0.5), [], [b_const])
    S.add("pool", lambda e: e.memset(cst_one[:, 0:1], 1.0), [], [b_const])
    S.add("pool", lambda e: e.memset(cst_ln8[:, 0:1], LN8), [], [b_const])
    pc_ = {n: Buf("c_" + n) for n in ["hmask", "tri_u", "tri_l", "ones", "convc", "poolsc", "bcol", "n1c", "n2c", "cT", "gbias",
                                      "normf", "brow0", "brow1", "ident", "ahl", "poolw"]}
    for nm, dst, src in [
        ("cT", cT, cT_d[:, :]), ("bcol", bcol, bcol_d[:, :]), ("n1c", n1c, n1c_d[:, :]), ("n2c", n2c, n2c_d[:, :]),
        ("tri_u", tri_u, tri_d[0]), ("tri_l", tri_l, tri_d[1]), ("ones", ones_f, tri_d[2]), ("convc", convc, conv_d[:, :]),
        ("poolsc", poolsc, poolsc_d[:, :]), ("hmask", hmask, hmask_d[:, :]),
        ("gbias", gbias_t, gb_d.partition_broadcast(128)), ("normf", normf_t, normf_d.partition_broadcast(128)),
        ("brow0", browt[0], brow_d[0:1, :].partition_broadcast(128)), ("brow1", browt[1], brow_d[1:2, :].partition_broadcast(128)),
    ]:
        dma("sp", dst, src, [], [pc_[nm]])
    dma("pool", ident, ident_d[:, :], [], [pc_["ident"]])
    dma("pool", ahl, ahl_d[:, :], [], [pc_["ahl"]])
    dma("pool", poolw, poolw_d[:, :], [], [pc_["poolw"]])
    S.add("pool", lambda e: e.memset(vaug_v[:, :, :, 128:129], 1.0), [], b_vaug)

    act(scb[:, 0:24], cT[:, 0:24], AF.Silu, [pc_["cT"]], [b_mod])
    scb_v = v3(scb[:, 0:24], 8, 3)
    screp_v = screp.rearrange("p (c b m) -> p c b m", c=8, b=2, m=128)
    cp("dve", screp_v, scb_v[:, :, 0:2].unsqueeze(3).broadcast_to([128, 8, 2, 128]), [b_mod], [b_mod])

    adaA_v = v3(adaA, 8, 1024)
    adaB_v = v3(adaB, 8, 1024)
    seg_order = [0, 1, 3, 4, 2, 5]
    col_si = {0: 0, 1: 1, 3: 2, 4: 3}
    psc = bank(0)[:, 0:96].rearrange("p (s c v) -> p s c v", s=4, c=8, v=3)
    for k, seg in enumerate(seg_order):
        stg, stg_v, b_stg = (adaA, adaA_v, b_adaA) if k % 2 == 0 else (adaB, adaB_v, b_adaB)
        dma("pool", stg, wada_d[seg], [], [b_stg])
        if k == 1:
            for c in range(8):
                dma("pool", w_in_v[:, c, :], win_d[c], [], [b_win])
        if seg in col_si:
            si = col_si[seg]
            for pc in range(8):
                for c in range(8):
                    mm(psc[:, si, pc, :], stg_v[:, c, pc * 128:(pc + 1) * 128], scb_v[:, c, :], c == 0, c == 7,
                       [b_stg, b_mod], [pb[0]])
            tt("dve", modcol_v[:, si], psc[:, si], v3(bcol[:, 0:32], 4, 8)[:, si].unsqueeze(2).broadcast_to([128, 8, 3]),
               ALU.add, [pb[0], pc_["bcol"]], [b_mod])
        else:
            gi = 0 if seg == 2 else 1
            for b in range(2):
                for half in range(2):
                    bk = 1 + (b * 2 + half) % 4
                    for c in range(8):
                        mm(bank(bk), screp_v[:, c, b, :], stg_v[:, c, half * 512:(half + 1) * 512], c == 0, c == 7,
                           [b_stg, b_mod], [pb[bk]])
                    gdst = gtr[gi] if b == 0 else gtmp[gi]
                    tt("dve", gdst[:, half * 512:(half + 1) * 512], bank(bk), browt[gi][:, half * 512:(half + 1) * 512],
                       ALU.add, [pb[bk], pc_["brow%d" % gi]], [b_gt if b == 0 else b_gtmp])
            dma("sp", s_gt[gi], gtmp[gi][:, 0:1024], [b_gtmp], [b_sgt_dram])
    stt("dve", G1c_v, modcol_v[:, 1], 1.0, n1c[:, 0:8].unsqueeze(2).broadcast_to([128, 8, 3]), ALU.add, ALU.mult,
        [b_mod, pc_["n1c"]], [b_mod])
    stt("dve", G2c_v, modcol_v[:, 3], 1.0, n2c[:, 0:8].unsqueeze(2).broadcast_to([128, 8, 3]), ALU.add, ALU.mult,
        [b_mod, pc_["n2c"]], [b_mod])
    S1c_v = modcol_v[:, 0]
    S2c_v = modcol_v[:, 2]

    S.barrier()

    conv_jobs = []

    pend_store = []

    def _mk_up(j):
        def f():
            dma("pool", upring[j % 3], wup_d[j], [], [b_up[j % 3]])
            flush_store()
            pend_store.append(lambda: dma("pool", s_up[j], upring[j % 3], [b_up[j % 3]], [b_sup[j]]))
        return f

    def _mk_dn(j):
        def f():
            dma("pool", dnring[j % 3], wdn_d[j], [], [b_dn[j % 3]])
            flush_store()
            pend_store.append(lambda: dma("pool", s_dn[j], dnring[j % 3], [b_dn[j % 3]], [b_sdn[j]]))
        return f

    def _mk_out(c):
        def f():
            dma("pool", dnring[(c + 1) % 3], wout_d[c], [], [b_dn[(c + 1) % 3]])
            flush_store()
            pend_store.append(lambda: dma("pool", s_out[c], dnring[(c + 1) % 3], [b_dn[(c + 1) % 3]], [b_sout[c]]))
        return f

    def flush_store():
        while pend_store:
            pend_store.pop(0)()

    for c in range(8):
        conv_jobs.append(_mk_out(c))
    for j in range(NJ):
        conv_jobs.append(_mk_up(j))
        conv_jobs.append(_mk_dn(j))

    def pump_conv(n):
        for _ in range(n):
            if conv_jobs:
                conv_jobs.pop(0)()
        if not conv_jobs:
            flush_store()

    tp_i = [0]
    mmb_i = [0]

    def next_mm_bank():
        i = 4 + mmb_i[0] % 3
        mmb_i[0] += 1
        return i

    xs_i = [0]

    def norm_T(src_ap_fn, ntile, Gc, Sc, vec, dst_v, b_dst, load, xs_list, b_xs_list, xn_list, b_xn_list):
        for g0 in range(0, ntile, 2):
            pr = (tp_i[0] % 2) * 2
            tp_i[0] += 1
            tpv = bank(pr, 2).bitcast(BF16)[:, 0:2048].rearrange("p (c t) -> p c t", c=8, t=256)
            for tl in range(g0, min(g0 + 2, ntile)):
                xt, b_xt = src_ap_fn(tl)
                ss, b_ss = next_small()
                act(junk[:, 0:1024], xt, AF.Square, [b_xt], [b_junk, b_ss], accum=ss[:, 0:1])
                rs, b_rs = rstd_of(ss[:, 0:1], b_ss, D)
                k = xs_i[0] % len(xn_list)
                xs_i[0] += 1
                ts("dve", xn_list[k][:, 0:1024], xt, rs[:, 0:1], None, ALU.mult, None, [b_xt, b_rs], [b_xn_list[k]])
                for c in range(8):
                    tr(tpv[:, c, (tl - g0) * 128:(tl - g0 + 1) * 128], xn_list[k][:, c * 128:(c + 1) * 128],
                       [b_xn_list[k]], [pb[pr], pb[pr + 1]])
            n = min(2, ntile - g0) * 128
            for c in range(8):
                act(dst_v[:, c, g0 * 128:g0 * 128 + n], tpv[:, c, 0:n], AF.Identity, [pb[pr], pb[pr + 1], b_mod], [b_dst],
                    scale=Gc[:, c, vec:vec + 1], bias=Sc[:, c, vec:vec + 1])

    def group_front(src_fn, ntl, xn_list, b_xn_list, junk_ap, b_junk_):
        ss, b_ss = next_small()
        srcs = []
        for i in range(ntl):
            xt, b_xt = src_fn(i)
            srcs.append((xt, b_xt))
            act(junk_ap[:, 0:1024], xt, AF.Square, [b_xt], [b_junk_, b_ss], accum=ss[:, i:i + 1])
        ms, b_ms = next_small()
        ts("dve", ms[:, 0:ntl], ss[:, 0:ntl], 1.0 / D, EPS, ALU.mult, ALU.add, [b_ss], [b_ms])
        rs, b_rs = next_small()
        tt("pool", rs[:, 0:ntl], ms[:, 0:ntl], cst_m05[:, 0:1].broadcast_to([128, ntl]), ALU.pow, [b_ms, b_const], [b_rs])
        outs = []
        for i in range(ntl):
            xt, b_xt = srcs[i]
            k = xs_i[0] % len(xn_list)
            xs_i[0] += 1
            ts("dve", xn_list[k][:, 0:1024], xt, rs[:, i:i + 1], None, ALU.mult, None, [b_xt, b_rs], [b_xn_list[k]])
            outs.append((xn_list[k], b_xn_list[k]))
        return outs

    def group_back(xns, Gc, Sc, vec, dst_v, b_dst, col0):
        pr = (tp_i[0] % 2) * 2
        tp_i[0] += 1
        tpv = bank(pr, 2).bitcast(BF16)[:, 0:2048].rearrange("p (c t) -> p c t", c=8, t=256)
        for i, (xa, b_xa) in enumerate(xns):
            for c in range(8):
                tr(tpv[:, c, i * 128:(i + 1) * 128], xa[:, c * 128:(c + 1) * 128], [b_xa], [pb[pr], pb[pr + 1]])
        n = len(xns) * 128
        for c in range(8):
            act(dst_v[:, c, col0:col0 + n], tpv[:, c, 0:n], AF.Identity, [pb[pr], pb[pr + 1], b_mod], [b_dst],
                scale=Gc[:, c, vec:vec + 1], bias=Sc[:, c, vec:vec + 1])

    def phase1(b):
        if b > 0:
            for c in range(8):
                dma("pool", w_in_v[:, c, :], win_d[c], [], [b_win])
        dma("sp", hhn_t[:, 0:512], hn_d.partition_broadcast(128), [], [b_hhn])
        ts("dve", hhn_t[:, 0:512], hhn_t[:, 0:512], 0.5, None, ALU.mult, None, [b_hhn], [b_hhn])
        blocks = [dict(name="ctx", tokoff=0, bi=0, nblk=1, tpb=2, vec=2, src=ctx_d, tt0=0)]
        for bi in range(4):
            blocks.append(dict(name="x", tokoff=256, bi=bi, nblk=4, tpb=4, vec=b, src=x_d, tt0=2))
        xload_i = [0]
        for gi_, B_ in enumerate(blocks):
            B_["slot"] = gi_ % 3

        def front(B_, g):
            ntl = min(2, B_["tpb"] - 2 * g)

            def src(i):
                pump_conv(2)
                k = xload_i[0] % 2
                xload_i[0] += 1
                r0 = (B_["bi"] * B_["tpb"] + 2 * g + i) * 128
                dma("sp", xs[k][:, 0:1024], B_["src"][b, r0:r0 + 128, :], [], [b_xs[k]])
                return xs[k][:, 0:1024], b_xs[k]
            return group_front(src, ntl, xn, b_xn, junk, b_junk)

        def back(B_, g, xns):
            sl = B_["slot"]
            group_back(xns, G1c_v, S1c_v, B_["vec"], hT_v[sl], b_hT[sl], g * 256)

        def tok_tiles(B_, tls):
            sl = B_["slot"]
            h_v = hT_v[sl]
            name, bi, tpb = B_["name"], B_["bi"], B_["tpb"]
            for tl in tls:
                tti = B_["tt0"] + bi * tpb + tl
                groups = ["v", "gates"] if name == "ctx" else ["pool", "v", "o", "gates"]
                for grp in groups:
                    c0, ncol = {"pool": (0, 512), "v": (1024, 512), "o": (1536, 512), "gates": (2048, 16)}[grp]
                    bk = next_mm_bank()
                    for c in range(8):
                        mm(bank(bk)[:, 0:ncol], h_v[:, c, tl * 128:(tl + 1) * 128], w_in_v[:, c, c0:c0 + ncol],
                           c == 0, c == 7, [b_hT[sl], b_win], [pb[bk]])
                    if grp == "pool":
                        ti = bi * tpb + tl
                        act(poolin_v[:, ti, :], bank(bk), AF.Copy, [pb[bk]], [b_poolin[ti]])
                    elif grp == "v":
                        cp("dve", vaug_v[:, tti, :, 0:128], v3(bank(bk), 4, 128), [pb[bk]], [b_vaug[tti]])
                    elif grp == "o":
                        ti = bi * tpb + tl
                        act(otmp[0][:, 0:512], bank(bk), AF.Tanh, [pb[bk]], [b_otmp[0]], scale=0.5)
                        stt("dve", sog_v[:, ti, :], otmp[0][:, 0:512], 1.0, hhn_t[:, 0:512], ALU.add, ALU.mult,
                            [b_otmp[0], b_hhn], [b_sog[ti]])
                    else:
                        tt("dve", gsb_v[:, tti, :], bank(bk)[:, 0:16], gbias_t[:, 0:16], ALU.add,
                           [pb[bk], b_const], [b_gsb])

        def qk_block(B_, prevB, nextB):
            sl = B_["slot"]
            h_v = hT_v[sl]
            bi, tpb, nblk = B_["bi"], B_["tpb"], B_["nblk"]
            n = tpb * 128
            left = bi > 0
            right = bi < nblk - 1
            hb = 7
            halo = bank(hb)[:, 0:8]
            for qc in range(4):
                bk = next_mm_bank()
                cw = 512 + qc * 128
                for c in range(8):
                    mm(bank(bk)[:, 0:n], w_in_v[:, c, cw:cw + 128], h_v[:, c, 0:n], c == 0, c == 7,
                       [b_hT[sl], b_win], [pb[bk]])
                if left:
                    hp = hT_v[prevB["slot"]]
                    for c in range(8):
                        mm(halo[:, 2 * qc:2 * qc + 1], w_in_v[:, c, cw:cw + 128], hp[:, c, 511:512], c == 0, c == 7,
                           [b_hT[prevB["slot"]], b_win], [pb[hb]])
                if right:
                    hn_ = hT_v[nextB["slot"]]
                    for c in range(8):
                        mm(halo[:, 2 * qc + 1:2 * qc + 2], w_in_v[:, c, cw:cw + 128], hn_[:, c, 0:1], c == 0, c == 7,
                           [b_hT[nextB["slot"]], b_win], [pb[hb]])
                k = qc % 2
                a = acc[k]
                psb = bank(bk)
                act(a[:, 0:n], psb[:, 0:n], AF.Identity, [pb[bk], b_const], [b_acc[k]], scale=convc[:, qc * 3 + 1:qc * 3 + 2])
                stt("dve", a[:, 1:n], psb[:, 0:n - 1], convc[:, qc * 3:qc * 3 + 1], a[:, 1:n], ALU.mult, ALU.add,
                    [pb[bk], b_const, b_acc[k]], [b_acc[k]])
                stt("dve", a[:, 0:n - 1], psb[:, 1:n], convc[:, qc * 3 + 2:qc * 3 + 3], a[:, 0:n - 1], ALU.mult, ALU.add,
                    [pb[bk], b_const, b_acc[k]], [b_acc[k]])
                if left:
                    stt("dve", a[:, 0:1], halo[:, 2 * qc:2 * qc + 1], convc[:, qc * 3:qc * 3 + 1], a[:, 0:1], ALU.mult, ALU.add,
                        [pb[hb], b_const, b_acc[k]], [b_acc[k]])
                if right:
                    stt("dve", a[:, n - 1:n], halo[:, 2 * qc + 1:2 * qc + 2], convc[:, qc * 3 + 2:qc * 3 + 3], a[:, n - 1:n],
                        ALU.mult, ALU.add, [pb[hb], b_const, b_acc[k]], [b_acc[k]])
                dstv = qT_v if qc < 2 else kT_v
                t0 = B_["tokoff"] + bi * n
                act(dstv[:, qc % 2, t0:t0 + n], a[:, 0:n], AF.Silu, [b_acc[k]], [b_qk])

        B0 = blocks[0]
        for g in range((B0["tpb"] + 1) // 2):
            back(B0, g, front(B0, g))
        for i_, B_ in enumerate(blocks):
            nxt = blocks[i_ + 1] if i_ + 1 < len(blocks) else None
            prv = blocks[i_ - 1] if i_ > 0 else None
            ng = (B_["tpb"] + 1) // 2
            nng = (nxt["tpb"] + 1) // 2 if nxt else 0
            for g in range(max(ng, nng)):
                xns = front(nxt, g) if (nxt and g < nng) else None
                if g < ng:
                    tok_tiles(B_, list(range(2 * g, min(2 * g + 2, B_["tpb"]))))
                if xns is not None:
                    back(nxt, g, xns)
            qk_block(B_, prv if (prv and prv["name"] == B_["name"]) else None,
                     nxt if (nxt and nxt["name"] == B_["name"]) else None)
        pump_conv(1000)

    def phase1b(b):
        act(a8_v[:, :, :], gsb_v[:, :, 8:16], AF.Exp, [b_gsb], [b_gate], scale=-1.0)
        act(nlf_v[:, :, :], a8_v[:, :, :], AF.Ln, [b_gate, b_const], [b_gate], bias=cst_one[:, 0:1])
        cs = bank(0)[:, 0:NTT * 8].rearrange("p (t g) -> p t g", t=NTT, g=8)
        bl = bank(1)[:, 0:NTT * 8].rearrange("p (t g) -> p t g", t=NTT, g=8)
        for tti in range(NTT):
            mm(cs[:, tti, 0:4], tri_u[:, 0:128], nlf_v[:, tti, 0:4], True, True, [b_gate, b_const], [pb[0]])
            mm(cs[:, tti, 4:8], tri_l[:, 0:128], nlf_v[:, tti, 4:8], True, True, [b_gate, b_const], [pb[0]])
            mm(bl[:, tti, :], ones_f[:, 0:128], nlf_v[:, tti, :], True, True, [b_gate, b_const], [pb[1]])
        tt("dve", a8_v[:, :, :], gsb_v[:, :, 0:8], cs, ALU.add, [b_gsb, pb[0], b_gate], [b_gate])
        act(E8_v[:, :, :], a8_v[:, :, :], AF.Exp, [b_gate], [b_gate])
        act(THR8_v[:, :, :], cs, AF.Exp, [pb[0], b_const], [b_gate], bias=cst_ln8[:, 0:1])
        blv = bl.rearrange("p t (d h) -> p t d h", d=2, h=4)
        act(dec_v[0:64], blv[0:64, :, :, 0:4:2], AF.Exp, [pb[1]], [b_gate], scale=-1.0)
        act(dec_v[64:128], blv[64:128, :, :, 1:4:2], AF.Exp, [pb[1]], [b_gate], scale=-1.0)

        def tok0(tti):
            return tti * 128 if tti < 2 else 256 + (tti - 2) * 128

        for k_ in range(2):
            for hf in range(2):
                S.add("pool", lambda e, a=kz[k_][hf]: e.memset(a[:, 0:256], 0.0), [], [b_kz[k_]])
        for d, order in ((0, list(range(NTT))), (1, [1, 0] + list(range(NTT - 1, 1, -1)))):
            Rv = v3(stR[d][:, 0:258], 2, 129)
            prev = None
            for step, tti in enumerate(order):
                k = step % 2
                kb = 2 + k
                ktp = bank(kb).bitcast(BF16)[:, 0:256]
                t0 = tok0(tti)
                for pr in range(2):
                    tr(ktp[:, pr * 128:(pr + 1) * 128], kT_v[:, pr, t0:t0 + 128], [b_qk], [pb[kb]])
                ktp_v = v3(ktp, 2, 128)
                cp("act", v3(kz[k][0][:, 0:256], 2, 128)[:, :, 0:64], ktp_v[:, :, 0:64], [pb[kb]], [b_kz[k]])
                cp("act", v3(kz[k][1][:, 0:256], 2, 128)[:, :, 64:128], ktp_v[:, :, 64:128], [pb[kb]], [b_kz[k]])
                vp = vpI[k][:, 0:516].rearrange("p (h c) -> p h c", h=4, c=129)
                tt("dve", vp, vaug_v[:, tti], E8_v[:, tti, d * 4:d * 4 + 4].unsqueeze(2).broadcast_to([128, 4, 129]), ALU.mult,
                   [b_vaug[tti], b_gate], [b_vpI[k]])
                ub = 4 + d * 2 + k
                U = bank(ub)[:, 0:258].rearrange("p (h c) -> p h c", h=2, c=129)
                for pr in range(2):
                    mm(U[:, pr, :], kz[k][0][:, pr * 128:(pr + 1) * 128], vp[:, 2 * pr, :], True, False,
                       [b_kz[k], b_vpI[k]], [pb[ub]])
                    mm(U[:, pr, :], kz[k][1][:, pr * 128:(pr + 1) * 128], vp[:, 2 * pr + 1, :], False, True,
                       [b_kz[k], b_vpI[k]], [pb[ub]])
                if prev is None:
                    cp("dve", Rv, U, [pb[ub]], [b_st[d]])
                else:
                    for hh in range(2):
                        dcol = dec_v[:, prev, d, hh:hh + 1]
                        if tti >= 2:
                            act(Cst_v[:, tti - 2, d, hh, :], Rv[:, hh, :], AF.Identity, [b_st[d], b_gate], [b_cst], scale=dcol)
                        stt("dve", Rv[:, hh, :], Rv[:, hh, :], dcol, U[:, hh, :], ALU.mult, ALU.add,
                            [b_st[d], b_gate, pb[ub]], [b_st[d]])
                prev = tti

    def phase3(b):
        if b > 0:
            for gi in range(2):
                dma("sp", gtr[gi][:, 0:1024], s_gt[gi], [b_sgt_dram], [b_gt])
        for nb in range(4):
            for c in range(8):
                dma("sp", wout_v[:, c, :], s_out[c], [b_sout[c]], [b_wout])
            for tl in range(4):
                r0 = (nb * 4 + tl) * 128
                dma("sp", x1b[tl][:, 0:1024], x_d[b, r0:r0 + 128, :], [], [b_x1[tl]])
            for tl in range(4):
                t = nb * 4 + tl
                tti = t + 2
                t0 = 256 + t * 128
                k = t % 2
                if stop_after == "3a0":
                    raise StopBuild()
                Sps = v3(bank(0), 4, 128)
                qz_v = qz[k][:, 0:512].rearrange("p (f r c) -> p f r c", f=2, r=2, c=128)
                for hf in range(2):
                    act(qz_v[:, hf], qT_v[:, :, t0:t0 + 128], AF.Copy, [b_qk, b_const], [b_qz[k]], scale=hmask[:, hf:hf + 1])
                for h in range(4):
                    mm(Sps[:, h, :], kT_v[:, h // 2, t0:t0 + 128], qz_v[:, h % 2, h // 2, :], True, True,
                       [b_qk, b_qz[k]], [pb[0]])
                if stop_after == "3a1":
                    raise StopBuild()
                Pf_v = v3(Pf[k][:, 0:512], 4, 128)
                Pb_v = v3(Pb[k][:, 0:512], 4, 128)
                tt("dve", Pf_v, Sps, tri_u[:, 0:128].unsqueeze(1).broadcast_to([128, 4, 128]), ALU.mult, [pb[0], b_const], [b_Pf[k]])
                tt("dve", Pb_v, Sps, tri_l[:, 0:128].unsqueeze(1).broadcast_to([128, 4, 128]), ALU.mult, [pb[0], b_const], [b_Pb[k]])
                vf = vpf[k][:, 0:516].rearrange("p (h c) -> p h c", h=4, c=129)
                vb = vpb[k][:, 0:516].rearrange("p (h c) -> p h c", h=4, c=129)
                if stop_after == "3a2":
                    raise StopBuild()
                tt("pool", vf, vaug_v[:, tti], E8_v[:, tti, 0:4].unsqueeze(2).broadcast_to([128, 4, 129]), ALU.mult,
                   [b_vaug[tti], b_gate], [b_vpf[k]])
                tt("pool", vb, vaug_v[:, tti], E8_v[:, tti, 4:8].unsqueeze(2).broadcast_to([128, 4, 129]), ALU.mult,
                   [b_vaug[tti], b_gate], [b_vpb[k]])
                if stop_after == "3a":
                    raise StopBuild()
                den = bank(3)[:, 0:8]
                for d, (Pv, vv, bP, bV) in enumerate(((Pf_v, vf, b_Pf[k], b_vpf[k]), (Pb_v, vb, b_Pb[k], b_vpb[k]))):
                    NUM = v3(bank(1 + d), 4, 128)
                    for h in range(4):
                        qh = qz_v[:, h % 2, h // 2, :]
                        mm(NUM[:, h, :], Pv[:, h, :], vv[:, h, 0:128], True, False, [bP, bV], [pb[1 + d]])
                        mm(NUM[:, h, :], qh, Cst_v[:, t, d, h // 2, 0:128], False, True,
                           [b_qz[k], b_cst], [pb[1 + d]])
                        mm(den[:, d * 4 + h:d * 4 + h + 1], Pv[:, h, :], vv[:, h, 128:129], True, False, [bP, bV], [pb[3]])
                        mm(den[:, d * 4 + h:d * 4 + h + 1], qh, Cst_v[:, t, d, h // 2, 128:129], False, True,
                           [b_qz[k], b_cst], [pb[3]])
                if stop_after == "3b0":
                    raise StopBuild()
                dabs, b_dabs = next_small()
                act(dabs[:, 0:8], den, AF.Abs, [pb[3]], [b_dabs])
                dm, b_dm = next_small()
                tt("dve", dm[:, 0:8], dabs[:, 0:8], THR8_v[:, tti, :], ALU.max, [b_dabs, b_gate], [b_dm])
                r8, b_r8 = next_small()
                S.add("dve", lambda e, o=r8, i=dm: e.reciprocal(out=o[:, 0:8], in_=i[:, 0:8]), [b_dm], [b_r8])
                if stop_after == "3b":
                    raise StopBuild()
                tt("dve", v3(t1[:, 0:512], 4, 128), v3(bank(1), 4, 128), r8[:, 0:4].unsqueeze(2).broadcast_to([128, 4, 128]), ALU.mult,
                   [pb[1], b_r8], [b_t1])
                tt("dve", v3(t2[:, 0:512], 4, 128), v3(bank(2), 4, 128), r8[:, 4:8].unsqueeze(2).broadcast_to([128, 4, 128]), ALU.mult,
                   [pb[2], b_r8], [b_t2])
                tt("pool", hm[:, 0:512], t1[:, 0:512], t2[:, 0:512], ALU.add, [b_t1, b_t2], [b_hm])
                ss4, b_ss4 = next_small()
                for h in range(4):
                    act(sqj[:, h * 128:(h + 1) * 128], hm[:, h * 128:(h + 1) * 128], AF.Square, [b_hm], [b_sqj, b_ss4],
                        accum=ss4[:, h:h + 1])
                k1 = 0
                ms4, b_ms4 = next_small()
                ts("dve", ms4[:, 0:4], ss4[:, 0:4], 1.0 / 128, EPS, ALU.mult, ALU.add, [b_ss4], [b_ms4])
                rs4, b_rs4 = next_small()
                tt("pool", rs4[:, 0:4], ms4[:, 0:4], cst_m05[:, 0:1].broadcast_to([128, 4]), ALU.pow, [b_ms4, b_const], [b_rs4])
                for h in range(4):
                    stt("dve", mo[k1][:, h * 128:(h + 1) * 128], hm[:, h * 128:(h + 1) * 128], rs4[:, h:h + 1],
                        sog_v[:, t, h * 128:(h + 1) * 128], ALU.mult, ALU.mult, [b_hm, b_rs4, b_sog[t]], [b_mo[k1]])
                if stop_after == "3c":
                    raise StopBuild()
                moT = bank(3).bitcast(BF16)[:, 512:1024].rearrange("p (h c) -> p h c", h=4, c=128)
                for h in range(4):
                    tr(moT[:, h, :], mo[k1][:, h * 128:(h + 1) * 128], [b_mo[k1]], [pb[3]])
                cp("act", mixT_v[k1][:, 4:8, :], moT, [pb[3]], [b_mixT[k1]])
                if stop_after == "3c2":
                    raise StopBuild()
                dps = v3(bank(4), 4, 128)
                for g in range(4):
                    mm(dps[:, g, :], poolin_v[:, t, g * 128:(g + 1) * 128], ahl_v[:, 2 * g, :], True, False,
                       [b_poolin[t], b_const], [pb[4]])
                    mm(dps[:, g, :], poolin_v[:, t, g * 128:(g + 1) * 128], ahl_v[:, 2 * g + 1, :], False, True,
                       [b_poolin[t], b_const], [pb[4]])
                cp("act", dT[k1][:, 0:512], bank(4), [pb[4]], [b_dT[k1]])
                pops = v3(bank(5), 4, 128)
                for g in range(4):
                    mm(pops[:, g, :], poolw_v[:, g, :], dT[k1][:, g * 128:(g + 1) * 128], True, True, [b_dT[k1], b_const], [pb[5]])
                tt("dve", mixT_v[k1][:, 0:4, :], pops, poolsc[:, 0:4].unsqueeze(2).broadcast_to([128, 4, 128]), ALU.mult,
                   [pb[5], b_const], [b_mixT[k1]])
                if stop_after == "3e":
                    raise StopBuild()
                for half in range(2):
                    for kc in range(8):
                        mm(bank(6 + half), mixT_v[k1][:, kc, :], wout_v[:, kc, half * 512:(half + 1) * 512], kc == 0, kc == 7,
                           [b_mixT[k1], b_wout], [pb[6 + half]])
                tt("dve", tmp4k[:, 0:1024], bank(6, 2), gtr[0][:, 0:1024], ALU.mult, [pb[6], pb[7], b_gt], [b_tmp4k])
                tt("pool", x1b[tl][:, 0:1024], tmp4k[:, 0:1024], x1b[tl][:, 0:1024], ALU.add, [b_tmp4k, b_x1[tl]], [b_x1[tl]])
                if stop_after == "3m1":
                    raise StopBuild()
            if stop_after == "3m":
                raise StopBuild()
            def src2(tl):
                return x1b[tl][:, 0:1024], b_x1[tl]
            norm_T(src2, 4, G2c_v, S2c_v, b, h2T_v, b_h2T, None, None, None, xn2, b_xn2)
            if stop_after == "3n":
                raise StopBuild()
            for j in range(min(2, NJ)):
                dma("sp", upring[j % 3], s_up[j], [b_sup[j]], [b_up[j % 3]])
            for j in range(NJ):
                if j + 2 < NJ:
                    dma("sp", upring[(j + 2) % 3], s_up[j + 2], [b_sup[j + 2]], [b_up[(j + 2) % 3]])
                if j == NJ - 2:
                    for jj in range(2):
                        dma("sp", dnring[jj % 3], s_dn[jj], [b_sdn[jj]], [b_dn[jj % 3]])
                upv = v3(upring[j % 3], 8, 256)
                gb_, ab_ = 2 + (j % 2), 4 + (j % 2)
                for c in range(8):
                    mm(bank(gb_), upv[:, c, 0:128], h2T_v[:, c, :], c == 0, c == 7, [b_up[j % 3], b_h2T], [pb[gb_]])
                for c in range(8):
                    mm(bank(ab_), upv[:, c, 128:256], h2T_v[:, c, :], c == 0, c == 7, [b_up[j % 3], b_h2T], [pb[ab_]])
                kk = 0
                act(sgt[kk][:, 0:512], bank(gb_), AF.Silu, [pb[gb_]], [b_sgt[kk]])
                tt("dve", actT_v[:, j, :], bank(ab_), sgt[kk][:, 0:512], ALU.mult, [pb[ab_], b_sgt[kk]], [b_actT])
            if stop_after == "3u":
                raise StopBuild()
            for j in range(NJ):
                if j + 2 < NJ:
                    dma("sp", dnring[(j + 2) % 3], s_dn[j + 2], [b_sdn[j + 2]], [b_dn[(j + 2) % 3]])
                for tl in range(4):
                    for half in range(2):
                        bk = tl * 2 + half
                        mm(bank(bk), actT_v[:, j, tl * 128:(tl + 1) * 128], dnring[j % 3][:, half * 512:(half + 1) * 512],
                           j == 0, j == NJ - 1, [b_actT, b_dn[j % 3]], [pb[bk]])
            for tl in range(4):
                t = nb * 4 + tl
                tt("dve", tmp4k[:, 0:1024], bank(tl * 2, 2), gtr[1][:, 0:1024], ALU.mult, [pb[tl * 2], pb[tl * 2 + 1], b_gt], [b_tmp4k])
                tt("pool", x1b[tl][:, 0:1024], tmp4k[:, 0:1024], x1b[tl][:, 0:1024], ALU.add, [b_tmp4k, b_x1[tl]], [b_x1[tl]])
                ss, b_ss = next_small()
                act(junk3[:, 0:1024], x1b[tl][:, 0:1024], AF.Square, [b_x1[tl]], [b_junk3, b_ss], accum=ss[:, 0:1])
                rs, b_rs = rstd_of(ss[:, 0:1], b_ss, D)
                stt("dve", x1b[tl][:, 0:1024], x1b[tl][:, 0:1024], rs[:, 0:1], normf_t[:, 0:1024], ALU.mult, ALU.mult,
                    [b_x1[tl], b_rs, b_const], [b_x1[tl]])
                dma("pool", out_d[b, t * 128:(t + 1) * 128, :], x1b[tl][:, 0:1024], [b_x1[tl]], [b_out])
            if stop_after == "3d":
                raise StopBuild()

    try:
        for b in range(2):
            phase1(b)
            if stop_after == "1" and b == 0:
                break
            phase1b(b)
            if stop_after == "1b" and b == 0:
                break
            S.barrier()
            phase3(b)
            S.barrier()
            if stop_after == "3" and b == 0:
                break
    except StopBuild:
        pass

    if dbg_d is not None:
        S.barrier()
        dma("pool", dbg_d[:, 0:dbg_cols], mem[:, 0:dbg_cols], [], [b_out])

    import contextlib
    with contextlib.ExitStack() as es:
        sems = {e: es.enter_context(nc.semaphore("s_" + e)) for e in ["pe", "act", "dve", "pool"]}
        dsems = {e: [es.enter_context(nc.semaphore("d_%s%d" % (e, i))) for i in range(n)] for e, n in NDSEM.items()}
        S.finalize(nc, sems, dsems)
        block = es.enter_context(nc.Block())

        @block.tensor
        def _(e):
            S.emit("pe", e, sems, dsems)

        @block.scalar
        def _(e):
            S.emit("act", e, sems, dsems)

        @block.vector
        def _(e):
            S.emit("dve", e, sems, dsems)

        @block.gpsimd
        def _(e):
            S.emit("pool", e, sems, dsems)

        @block.sync
        def _(e):
            S.emit("sp", e, sems, dsems)

    layout = dict(A_end=A_end, DE_off=DE_off, D_end=D_end, E_end=E_end, R1_off=R1_off, TOTAL=TOTAL)
    _aps = dict(poolin=poolin, vaug=vaug, sog=sog, qT=qT, kT=kT, gsb=gsb, E8=E8, THR8=THR8, nlf=nlf, a8=a8, dec=dec,
                Cst=Cst, modcol=modcol, G1c=G1c, G2c=G2c, gtr0=gtr[0], gtr1=gtr[1], stR0=stR[0], stR1=stR[1],
                x1b0=x1b[0], x1b1=x1b[1], x1b2=x1b[2], x1b3=x1b[3], h2T=h2T, actT=actT, mixT=mixT[0], hT0=hT[0], hT1=hT[1], hT2=hT[2])
    layout["aps"] = {k: (int(v.offset) * (2 if v.dtype == BF16 else 4), int(v.shape[1]), "bf16" if v.dtype == BF16 else "f32")
                     for k, v in _aps.items()}
    return nc, layout


def make_in_maps(x, c, ctx, c_ctx, w_ada, b_ada, norm1, w_in, conv_qk, gate_bias, pool_w,
                 pool_scale, head_norm, w_out, norm2, w_up, w_down, norm_f):
    f32 = np.float32
    x = np.asarray(x, f32)
    c = np.asarray(c, f32)
    ctx = np.asarray(ctx, f32)
    c_ctx = np.asarray(c_ctx, f32)
    w_ada = np.asarray(w_ada, f32)[0]
    b_ada = np.asarray(b_ada, f32)[0]
    norm1 = np.asarray(norm1, f32)[0]
    w_in = np.asarray(w_in, f32)[0]
    conv_qk = np.asarray(conv_qk, f32)[0]
    gate_bias = np.asarray(gate_bias, f32)[0]
    pool_w = np.asarray(pool_w, f32)[0]
    pool_scale = np.asarray(pool_scale, f32)[0]
    head_norm = np.asarray(head_norm, f32)[0]
    w_out = np.asarray(w_out, f32)[0]
    norm2 = np.asarray(norm2, f32)[0]
    w_up = np.asarray(w_up, f32)[0]
    w_down = np.asarray(w_down, f32)[0]
    norm_f = np.asarray(norm_f, f32)

    def colform(v, nchunk):
        return np.ascontiguousarray(v.reshape(nchunk, 128).T)

    w_ada_l = np.ascontiguousarray(w_ada.reshape(8, 128, 6, 1024).transpose(2, 1, 0, 3)).reshape(6, 128, 8 * 1024)
    bseg = b_ada.reshape(6, 1024)
    b_ada_col = np.ascontiguousarray(
        np.stack([colform(bseg[s], 8) for s in (0, 1, 3, 4)], axis=1)).reshape(128, 32)
    b_ada_row = np.ascontiguousarray(np.stack([bseg[2], bseg[5]], axis=0))
    w_in_l = np.ascontiguousarray(w_in.reshape(8, 128, INW))
    conv_c = np.ascontiguousarray(conv_qk.reshape(3, 4, 128).transpose(2, 1, 0)).reshape(128, 12)
    pool_w_l = np.ascontiguousarray(pool_w.transpose(1, 0, 2)).reshape(128, 512)
    pool_sc = colform(pool_scale, 4)
    w_out_l = np.ascontiguousarray(w_out.reshape(8, 128, 1024))
    wu = w_up.reshape(8, 128, 2, NJ, 128)
    w_up_l = np.ascontiguousarray(wu.transpose(3, 1, 0, 2, 4)).reshape(NJ, 128, 2048)
    w_down_l = np.ascontiguousarray(w_down.reshape(NJ, 128, 1024))
    s_idx = np.arange(128)
    tri = np.stack([
        (s_idx[:, None] <= s_idx[None, :]).astype(f32),
        (s_idx[:, None] >= s_idx[None, :]).astype(f32),
        np.ones((128, 128), f32)], axis=0)
    ident = np.eye(128, dtype=f32)
    pm = _pool_mats()
    ahl = np.ascontiguousarray(pm.transpose(2, 0, 1, 3)).reshape(128, 1024)

    shared = dict(
        w_ada_l=w_ada_l, b_ada_col=b_ada_col, b_ada_row=b_ada_row,
        norm1c=colform(norm1, 8), norm2c=colform(norm2, 8),
        normf_row=np.ascontiguousarray(norm_f.reshape(1, 1024)), hn_row=np.ascontiguousarray(head_norm.reshape(1, 512)),
        w_in_l=w_in_l, conv_c=conv_c, gb_row=np.ascontiguousarray(gate_bias.reshape(1, 16)),
        pool_w_l=pool_w_l, pool_sc=pool_sc, w_out_l=w_out_l, w_up_l=w_up_l, w_down_l=w_down_l,
        tri=tri, ident=ident, ahl=ahl,
        hmask=np.ascontiguousarray(np.stack([(s_idx < 64), (s_idx >= 64)], axis=1).astype(f32)),
    )
    in_maps = []
    for core in range(NCORES):
        b0 = 2 * core
        cT = np.stack([colform(c[b0], 8), colform(c[b0 + 1], 8), colform(c_ctx, 8)], axis=2).reshape(128, 24)
        m = dict(shared)
        m["x"] = np.ascontiguousarray(x[b0:b0 + 2])
        m["ctx"] = np.ascontiguousarray(ctx[b0:b0 + 2])
        m["cT"] = np.ascontiguousarray(cT)
        in_maps.append(m)
    return in_maps


_PROGRAM = None


def kernel(x, c, ctx, c_ctx, w_ada, b_ada, norm1, w_in, conv_qk, gate_bias, pool_w,
           pool_scale, head_norm, w_out, norm2, w_up, w_down, norm_f):
    global _PROGRAM
    in_maps = make_in_maps(x, c, ctx, c_ctx, w_ada, b_ada, norm1, w_in, conv_qk, gate_bias, pool_w,
                           pool_scale, head_norm, w_out, norm2, w_up, w_down, norm_f)
    if _PROGRAM is None:
        _PROGRAM = build_program()[0]
    res = run_bass_kernel_spmd(_PROGRAM, in_maps, core_ids=list(range(NCORES)))
    out = np.concatenate([np.asarray(r["out"], np.float32) for r in res.results], axis=0)
    return out
```

```python
import numpy as np
import ml_dtypes
import concourse.bass as bass
import concourse.mybir as mybir
from concourse.bass_utils import run_bass_kernel_spmd

F32 = mybir.dt.float32
BF16 = mybir.dt.bfloat16
AF = mybir.ActivationFunctionType
ALU = mybir.AluOpType

NCORES = 8
D = 1024
T = 2048
TC = 256
NT = 16
NTT = 18
DFF = 2816
NJ = 22
INW = 2064
EPS = 1e-6
LN8 = float(np.log(8.0))


class Buf:
    __slots__ = ("name", "lw", "rd")

    def __init__(self, name):
        self.name = name
        self.lw = None
        self.rd = []


class Op:
    __slots__ = ("eng", "fn", "raw", "oth", "idx", "signal", "ticket", "dma", "dsem", "dticket", "waits", "extra")

    def __init__(self, eng, fn, dma):
        self.eng = eng
        self.fn = fn
        self.dma = dma
        self.raw = []
        self.oth = []
        self.signal = False
        self.ticket = 0
        self.dsem = None
        self.dticket = 0
        self.waits = []
        self.extra = []


ENGS = ["pe", "act", "dve", "pool", "sp"]
NDSEM = {"sp": 12, "pool": 8, "act": 4}


class Sched:
    def __init__(self):
        self.q = {e: [] for e in ENGS}
        self.pending = {e: [] for e in ENGS}
        self.dma_since = []
        self.final_dma = {}

    def add(self, eng, fn, reads=(), writes=(), dma=False):
        op = Op(eng, fn, dma)
        op.idx = len(self.q[eng])
        for b in reads:
            if b.lw is not None:
                op.raw.append(b.lw)
        for b in writes:
            if b.lw is not None:
                op.oth.append(b.lw)
            op.oth.extend(b.rd)
        for b in reads:
            b.rd.append(op)
        for b in writes:
            b.lw = op
            b.rd = []
        if self.pending[eng]:
            op.oth.extend(self.pending[eng])
            self.pending[eng] = []
        self.q[eng].append(op)
        if dma:
            self.dma_since.append(op)
        return op

    def barrier(self):
        lasts = []
        for e in ENGS:
            for op in reversed(self.q[e]):
                if not op.dma:
                    lasts.append(op)
                    break
        lasts.extend(self.dma_since)
        self.dma_since = []
        for e in ENGS:
            self.pending[e] = self.pending[e] + list(lasts)

    def finalize(self, nc, sems, dsems):
        for e in ENGS:
            for op in self.q[e]:
                need = {}
                dmad = []
                for kind, lst in (("raw", op.raw), ("oth", op.oth)):
                    for d in lst:
                        if d is op:
                            continue
                        if d.dma:
                            dmad.append(d)
                            continue
                        if d.eng == op.eng and not op.dma:
                            if d.eng == "pe":
                                continue
                            if kind != "raw":
                                continue
                            if op.idx - d.idx > 2:
                                continue
                        k = d.eng
                        if k not in need or need[k].idx < d.idx:
                            need[k] = d
                op.extra = (list(need.values()), dmad)
                for d in need.values():
                    d.signal = True
        for e in ENGS:
            cnt = 0
            k = 0
            hist = {}
            for op in self.q[e]:
                if op.dma:
                    n = NDSEM[e]
                    si = k % n
                    op.dsem = dsems[e][si]
                    op.dticket = 16 * (k // n + 1)
                    if k >= n:
                        op.waits.append((op.dsem, 16 * (k // n)))
                    k += 1
                elif op.signal:
                    cnt += 1
                    op.ticket = cnt
            self.final_dma[e] = k
        for e in ENGS:
            waited = {}
            for op in self.q[e]:
                need, dmad = op.extra
                ws = list(op.waits)
                for d in need:
                    ws.append((sems[d.eng], d.ticket))
                for d in dmad:
                    ws.append((d.dsem, d.dticket))
                out = []
                for s, v in ws:
                    key = id(s)
                    if waited.get(key, 0) >= v:
                        continue
                    waited[key] = v
                    out.append((s, v))
                op.waits = out

    def emit(self, eng, e, sems, dsems):
        for op in self.q[eng]:
            for s, v in op.waits:
                e.wait_ge(s, v)
            ins = op.fn(e)
            if op.dma:
                ins.then_inc(op.dsem, 16)
            elif op.signal:
                ins.then_inc(sems[eng], 1)
        if eng in NDSEM:
            k = self.final_dma.get(eng, 0)
            n = NDSEM[eng]
            for si in range(min(n, k)):
                cntd = (k - si + n - 1) // n
                e.wait_ge(dsems[eng][si], 16 * cntd)


def _pool_mats():
    wins = (2, 4, 8, 16)
    out = np.zeros((4, 2, 128, 128), np.float32)
    for gi, win in enumerate(wins):
        A = np.zeros((64, 64), np.float64)
        for pos in range(64):
            lo = min(max(pos - win // 2, 0), 63)
            hi = min(max(pos + win // 2 - 1, 0), 63)
            cnt = hi - lo + 1
            if hi >= lo:
                A[pos, lo:hi + 1] += 1.0 / cnt
            A[pos, pos] -= 1.0
        A2 = np.zeros((128, 128), np.float64)
        A2[:64, :64] = A
        A2[64:, 64:] = A
        R = A2.T.astype(np.float32)
        hi_ = R.astype(ml_dtypes.bfloat16).astype(np.float32)
        lo_ = (R - hi_).astype(ml_dtypes.bfloat16).astype(np.float32)
        out[gi, 0] = hi_
        out[gi, 1] = lo_
    return out


class StopBuild(Exception):
    pass


class Region:
    def __init__(self, mem):
        self.mem = mem
        self.off = 0
        self.hi = 0

    def take(self, nbytes, dt=F32):
        nb = (nbytes + 31) // 32 * 32
        o = self.off
        self.off += nb
        self.hi = max(self.hi, self.off)
        ap = self.mem[:, o // 4:(o + nb) // 4]
        if dt != F32:
            ap = ap.bitcast(dt)
            return ap[:, 0:nbytes // 2]
        return ap[:, 0:nbytes // 4]


def build_program(stop_after=None, dbg_cols=0):
    nc = bass.Bass("TRN2", target_bir_lowering=False)

    def din(name, shape, dt=F32):
        return nc.dram_tensor(name, list(shape), dt, kind="ExternalInput").ap()

    x_d = din("x", [2, T, D])
    ctx_d = din("ctx", [2, TC, D])
    cT_d = din("cT", [128, 24])
    wada_d = din("w_ada_l", [6, 128, 8 * 1024])
    bcol_d = din("b_ada_col", [128, 32])
    brow_d = din("b_ada_row", [2, 1024])
    n1c_d = din("norm1c", [128, 8])
    n2c_d = din("norm2c", [128, 8])
    normf_d = din("normf_row", [1, 1024])
    hn_d = din("hn_row", [1, 512])
    win_d = din("w_in_l", [8, 128, INW])
    conv_d = din("conv_c", [128, 12])
    gb_d = din("gb_row", [1, 16])
    poolw_d = din("pool_w_l", [128, 512])
    poolsc_d = din("pool_sc", [128, 4])
    wout_d = din("w_out_l", [8, 128, 1024])
    wup_d = din("w_up_l", [NJ, 128, 2048])
    wdn_d = din("w_down_l", [NJ, 128, 1024])
    tri_d = din("tri", [3, 128, 128])
    ident_d = din("ident", [128, 128])
    ahl_d = din("ahl", [128, 1024])
    hmask_d = din("hmask", [128, 2])
    out_d = nc.dram_tensor("out", [2, T, D], F32, kind="ExternalOutput").ap()
    s_up = nc.dram_tensor("s_up", [NJ, 128, 2048], BF16, kind="Internal").ap()
    s_dn = nc.dram_tensor("s_dn", [NJ, 128, 1024], BF16, kind="Internal").ap()
    s_out = nc.dram_tensor("s_out", [8, 128, 1024], BF16, kind="Internal").ap()
    s_gt = nc.dram_tensor("s_gt", [2, 128, 1024], F32, kind="Internal").ap()
    dbg_d = None
    if dbg_cols:
        dbg_d = nc.dram_tensor("dbg", [128, dbg_cols], F32, kind="ExternalOutput").ap()

    S = Sched()
    TOTAL = 210000 // 4
    mem_g = nc.sbuf_tensor("mem", [128, TOTAL], F32)
    mem = mem_g.__enter__()
    ps_g = nc.psum_tensor("ps", [128, 4096], F32)
    ps = ps_g.__enter__()
    R = Region(mem)

    def bank(i, n=1):
        return ps[:, i * 512:(i + n) * 512]

    pb = [Buf("pb%d" % i) for i in range(8)]

    ident = R.take(256, BF16)
    tri_u = R.take(512)
    tri_l = R.take(512)
    ones_f = R.take(512)
    ahl = R.take(2048, BF16)
    poolw = R.take(1024, BF16)
    normf_t = R.take(4096)
    gbias_t = R.take(64)
    convc = R.take(48)
    poolsc = R.take(16)
    bcol = R.take(128)
    n1c = R.take(32)
    n2c = R.take(32)
    cT = R.take(96)
    scb = R.take(48, BF16)
    modcol = R.take(4 * 8 * 3 * 4)
    G1c = R.take(96)
    G2c = R.take(96)
    hmask = R.take(8)
    cst_m05 = R.take(4)
    cst_one = R.take(4)
    cst_ln8 = R.take(4)
    gtr = [R.take(4096) for _ in range(2)]
    small = [R.take(64) for _ in range(8)]
    A_end = R.off

    b_const = Buf("const")
    b_mod = Buf("mod")
    b_gt = Buf("gt")
    b_gtmp = Buf("gtmp")

    poolin = R.take(NT * 512 * 2, BF16)
    vaug = R.take(NTT * 516 * 2, BF16)
    sog = R.take(NT * 512 * 2, BF16)
    qT = R.take(2 * 2304 * 2, BF16)
    kT = R.take(2 * 2304 * 2, BF16)
    gsb = R.take(NTT * 16 * 4)
    E8 = R.take(NTT * 8 * 4)
    THR8 = R.take(NTT * 8 * 4)
    nlf = R.take(NTT * 8 * 4)
    a8 = R.take(NTT * 8 * 4)
    dec = R.take(NTT * 4 * 4)
    stR = [R.take(2 * 129 * 4) for _ in range(2)]
    R1_off = R.off
    hT = [R.take(8 * 512 * 2, BF16) for _ in range(2)]
    R.off = R1_off
    Cst = R.take(NT * 2 * 2 * 129 * 2, BF16)
    upring = [R.take(4096, BF16) for _ in range(3)]
    dnring = [R.take(2048, BF16) for _ in range(3)]
    DE_off = R.off

    w_in = R.take(8 * INW * 2, BF16)
    xs = [R.take(4096) for _ in range(2)]
    xn = [R.take(2048, BF16) for _ in range(2)]
    junk = R.take(2048, BF16)
    acc = [R.take(2048) for _ in range(2)]
    kz = [[R.take(512, BF16) for _ in range(2)] for _ in range(4)]
    otmp = [R.take(2048) for _ in range(1)]
    vpI = [R.take(1056, BF16) for _ in range(4)]
    hT.append(R.take(8 * 512 * 2, BF16))
    hhn_t = R.take(2048)
    D_end = R.off
    R.off = DE_off + 8 * INW * 2
    adaB = R.take(16384, BF16)
    browt = [R.take(4096) for _ in range(2)]
    gtmp = [R.take(4096) for _ in range(2)]
    screp = R.take(8 * 2 * 128 * 2, BF16)
    P_end = R.off
    adaA = mem[:, R1_off // 4:(R1_off + 16384) // 4].bitcast(BF16)

    R.off = DE_off
    x1b = [R.take(4096) for _ in range(4)]
    xn2 = [R.take(2048, BF16) for _ in range(2)]
    h2T = R.take(8 * 512 * 2, BF16)
    actT = R.take(NJ * 512 * 2, BF16)
    wout = actT[:, 0:8 * 1024]
    mixT = [R.take(8 * 128 * 2, BF16) for _ in range(4)]
    vpf = [R.take(1056, BF16) for _ in range(2)]
    vpb = [R.take(1056, BF16) for _ in range(2)]
    Pf = [R.take(1024, BF16) for _ in range(2)]
    Pb = [R.take(1024, BF16) for _ in range(2)]
    qz = [R.take(1024, BF16) for _ in range(2)]
    t12 = R.take(4096)
    t1 = t12[:, 0:512]
    t2 = t12[:, 512:1024]
    hm = t1
    tmp4k = t12
    mo = [R.take(1024, BF16) for _ in range(2)]
    dT = [R.take(1024, BF16) for _ in range(2)]
    sgt = [R.take(2048) for _ in range(1)]
    E_end = R.off
    assert max(D_end, E_end, P_end) <= TOTAL * 4, (D_end, E_end, P_end, TOTAL * 4)

    def v3(ap, a, b):
        return ap.rearrange("p (a b) -> p a b", a=a, b=b)

    ahl_v = v3(ahl, 8, 128)
    poolw_v = v3(poolw, 4, 128)
    w_in_v = v3(w_in, 8, INW)
    wout_v = v3(wout, 8, 1024)
    hT_v = [v3(h, 8, 512) for h in hT]
    h2T_v = v3(h2T, 8, 512)
    actT_v = v3(actT, NJ, 512)
    qT_v = v3(qT, 2, 2304)
    kT_v = v3(kT, 2, 2304)
    poolin_v = v3(poolin, NT, 512)
    sog_v = v3(sog, NT, 512)
    vaug_v = vaug.rearrange("p (t h c) -> p t h c", t=NTT, h=4, c=129)
    gsb_v = v3(gsb, NTT, 16)
    E8_v = v3(E8, NTT, 8)
    THR8_v = v3(THR8, NTT, 8)
    nlf_v = v3(nlf, NTT, 8)
    a8_v = v3(a8, NTT, 8)
    dec_v = dec.rearrange("p (t d h) -> p t d h", t=NTT, d=2, h=2)
    Cst_v = Cst.rearrange("p (t d h c) -> p t d h c", t=NT, d=2, h=2, c=129)
    modcol_v = modcol.rearrange("p (s c v) -> p s c v", s=4, c=8, v=3)
    G1c_v = v3(G1c, 8, 3)
    G2c_v = v3(G2c, 8, 3)
    mixT_v = [v3(m, 8, 128) for m in mixT]

    b_win = Buf("w_in")
    b_hT = [Buf("hT%d" % i) for i in range(3)]
    b_xs = [Buf("xs%d" % i) for i in range(2)]
    b_xn = [Buf("xn%d" % i) for i in range(2)]
    b_junk = Buf("junk")
    b_small = [Buf("small%d" % i) for i in range(8)]
    b_acc = [Buf("acc%d" % i) for i in range(2)]
    b_kz = [Buf("kz%d" % i) for i in range(4)]
    b_qz = [Buf("qz%d" % i) for i in range(2)]
    b_otmp = [Buf("otmp%d" % i) for i in range(1)]
    b_vpI = [Buf("vpI%d" % i) for i in range(4)]
    b_poolin = [Buf("poolin%d" % i) for i in range(NT)]
    b_vaug = [Buf("vaug%d" % i) for i in range(NTT)]
    b_sog = [Buf("sog%d" % i) for i in range(NT)]
    b_qk = Buf("qk")
    b_gsb = Buf("gsb")
    b_gate = Buf("gate")
    b_st = [Buf("stR0"), Buf("stR1")]
    b_cst = Buf("cst")
    b_up = [Buf("up%d" % i) for i in range(3)]
    b_dn = [Buf("dn%d" % i) for i in range(3)]
    b_sup = [Buf("sup%d" % i) for i in range(NJ)]
    b_sdn = [Buf("sdn%d" % i) for i in range(NJ)]
    b_sout = [Buf("sout%d" % i) for i in range(8)]
    b_adaA = Buf("adaA")
    b_adaB = Buf("adaB")
    b_brow = Buf("brow")
    b_x1 = [Buf("x1_%d" % i) for i in range(4)]
    b_xn2 = [Buf("xn2_%d" % i) for i in range(2)]
    b_h2T = Buf("h2T")
    b_actT_lo = Buf("actT_lo")
    b_actT_hi = Buf("actT_hi")
    b_wout = b_actT_lo
    b_mixT = [Buf("mixT%d" % i) for i in range(4)]
    b_vpf = [Buf("vpf%d" % i) for i in range(2)]
    b_vpb = [Buf("vpb%d" % i) for i in range(2)]
    b_Pf = [Buf("Pf%d" % i) for i in range(2)]
    b_Pb = [Buf("Pb%d" % i) for i in range(2)]
    b_t12 = Buf("t12")
    b_t1 = b_t12
    b_t2 = b_t12
    b_hm = b_t12
    b_tmp4k = b_t12
    b_mo = [Buf("mo%d" % i) for i in range(2)]
    b_dT = [Buf("dT%d" % i) for i in range(2)]
    b_sgt = [Buf("sgt%d" % i) for i in range(1)]
    b_hhn = Buf("hhn")
    b_sgt_dram = Buf("s_gt")
    b_out = Buf("outdram")

    small_i = [0]

    def next_small():
        i = small_i[0] % 8
        small_i[0] += 1
        return small[i], b_small[i]

    def dma(q, out, in_, reads=(), writes=()):
        return S.add(q, lambda e, o=out, i=in_: e.dma_start(out=o, in_=i), reads, writes, dma=True)

    def act(out, in_, func, reads, writes, scale=1.0, bias=None, accum=None):
        kw = {}
        if bias is not None:
            kw["bias"] = bias
        if accum is not None:
            kw["accum_out"] = accum
        return S.add("act", lambda e: e.activation(out=out, in_=in_, func=func, scale=scale, **kw), reads, writes)

    def tt(eng, out, in0, in1, op, reads, writes):
        return S.add(eng, lambda e: e.tensor_tensor(out=out, in0=in0, in1=in1, op=op), reads, writes)

    def ts(eng, out, in0, s1, s2, op0, op1, reads, writes):
        if s2 is None:
            return S.add(eng, lambda e: e.tensor_scalar(out=out, in0=in0, scalar1=s1, scalar2=None, op0=op0), reads, writes)
        return S.add(eng, lambda e: e.tensor_scalar(out=out, in0=in0, scalar1=s1, scalar2=s2, op0=op0, op1=op1), reads, writes)

    def stt(eng, out, in0, sc, in1, op0, op1, reads, writes):
        return S.add(eng, lambda e: e.scalar_tensor_tensor(out=out, in0=in0, scalar=sc, in1=in1, op0=op0, op1=op1), reads, writes)

    def cp(eng, out, in_, reads, writes):
        if eng == "act":
            return S.add(eng, lambda e: e.activation(out=out, in_=in_, func=AF.Copy), reads, writes)
        return S.add(eng, lambda e: e.tensor_copy(out, in_), reads, writes)

    def mm(out, lhsT, rhs, start, stop, reads, writes, tp=None):
        if tp is None:
            return S.add("pe", lambda e: e.matmul(out, lhsT=lhsT, rhs=rhs, start=start, stop=stop), reads, writes)
        return S.add("pe", lambda e: e.matmul(out, lhsT=lhsT, rhs=rhs, start=start, stop=stop, tile_position=tp), reads, writes)

    def tr(out, in_, reads, writes):
        return S.add("pe", lambda e: e.transpose(out, in_, ident), list(reads) + [b_const], writes)

    def rstd_of(ss_ap, b_ss, n):
        ms, b_ms = next_small()
        ts("dve", ms[:, 0:1], ss_ap, 1.0 / n, EPS, ALU.mult, ALU.add, [b_ss], [b_ms])
        rs, b_rs = next_small()
        tt("pool", rs[:, 0:1], ms[:, 0:1], cst_m05[:, 0:1], ALU.pow, [b_ms, b_const], [b_rs])
        return rs, b_rs

    S.add("pool", lambda e: e.memset(cst_m05[:, 0:1], -0.5), [], [b_const])
    S.add("pool", lambda e: e.memset(cst_one[:, 0:1], 1.0), [], [b_const])
    S.add("pool", lambda e: e.memset(cst_ln8[:, 0:1], LN8), [], [b_const])
    pc_ = {n: Buf("c_" + n) for n in ["hmask", "tri_u", "tri_l", "ones", "convc", "poolsc", "bcol", "n1c", "n2c", "cT", "gbias",
                                      "normf", "brow0", "brow1", "ident", "ahl", "poolw"]}
    for nm, dst, src in [
        ("cT", cT, cT_d[:, :]), ("bcol", bcol, bcol_d[:, :]), ("n1c", n1c, n1c_d[:, :]), ("n2c", n2c, n2c_d[:, :]),
        ("tri_u", tri_u, tri_d[0]), ("tri_l", tri_l, tri_d[1]), ("ones", ones_f, tri_d[2]), ("convc", convc, conv_d[:, :]),
        ("poolsc", poolsc, poolsc_d[:, :]), ("hmask", hmask, hmask_d[:, :]),
        ("gbias", gbias_t, gb_d.partition_broadcast(128)), ("normf", normf_t, normf_d.partition_broadcast(128)),
        ("brow0", browt[0], brow_d[0:1, :].partition_broadcast(128)), ("brow1", browt[1], brow_d[1:2, :].partition_broadcast(128)),
    ]:
        dma("sp", dst, src, [], [pc_[nm]])
    dma("pool", ident, ident_d[:, :], [], [pc_["ident"]])
    dma("pool", ahl, ahl_d[:, :], [], [pc_["ahl"]])
    dma("pool", poolw, poolw_d[:, :], [], [pc_["poolw"]])
    S.add("pool", lambda e: e.memset(vaug_v[:, :, :, 128:129], 1.0), [], b_vaug)

    act(scb[:, 0:24], cT[:, 0:24], AF.Silu, [pc_["cT"]], [b_mod])
    scb_v = v3(scb[:, 0:24], 8, 3)
    screp_v = screp.rearrange("p (c b m) -> p c b m", c=8, b=2, m=128)
    cp("dve", screp_v, scb_v[:, :, 0:2].unsqueeze(3).broadcast_to([128, 8, 2, 128]), [b_mod], [b_mod])

    adaA_v = v3(adaA, 8, 1024)
    adaB_v = v3(adaB, 8, 1024)
    seg_order = [0, 1, 3, 4, 2, 5]
    col_si = {0: 0, 1: 1, 3: 2, 4: 3}
    psc = bank(0)[:, 0:96].rearrange("p (s c v) -> p s c v", s=4, c=8, v=3)
    for k, seg in enumerate(seg_order):
        stg, stg_v, b_stg = (adaA, adaA_v, b_adaA) if k % 2 == 0 else (adaB, adaB_v, b_adaB)
        dma("pool", stg, wada_d[seg], [], [b_stg])
        if k == 1:
            for c in range(8):
                dma("pool", w_in_v[:, c, :], win_d[c], [], [b_win])
        if seg in col_si:
            si = col_si[seg]
            for pc in range(8):
                for c in range(8):
                    mm(psc[:, si, pc, :], stg_v[:, c, pc * 128:(pc + 1) * 128], scb_v[:, c, :], c == 0, c == 7,
                       [b_stg, b_mod], [pb[0]])
            tt("dve", modcol_v[:, si], psc[:, si], v3(bcol[:, 0:32], 4, 8)[:, si].unsqueeze(2).broadcast_to([128, 8, 3]),
               ALU.add, [pb[0], pc_["bcol"]], [b_mod])
        else:
            gi = 0 if seg == 2 else 1
            for b in range(2):
                for half in range(2):
                    bk = 1 + (b * 2 + half) % 4
                    for c in range(8):
                        mm(bank(bk), screp_v[:, c, b, :], stg_v[:, c, half * 512:(half + 1) * 512], c == 0, c == 7,
                           [b_stg, b_mod], [pb[bk]])
                    gdst = gtr[gi] if b == 0 else gtmp[gi]
                    tt("dve", gdst[:, half * 512:(half + 1) * 512], bank(bk), browt[gi][:, half * 512:(half + 1) * 512],
                       ALU.add, [pb[bk], pc_["brow%d" % gi]], [b_gt if b == 0 else b_gtmp])
            dma("sp", s_gt[gi], gtmp[gi][:, 0:1024], [b_gtmp], [b_sgt_dram])
    stt("dve", G1c_v, modcol_v[:, 1], 1.0, n1c[:, 0:8].unsqueeze(2).broadcast_to([128, 8, 3]), ALU.add, ALU.mult,
        [b_mod, pc_["n1c"]], [b_mod])
    stt("dve", G2c_v, modcol_v[:, 3], 1.0, n2c[:, 0:8].unsqueeze(2).broadcast_to([128, 8, 3]), ALU.add, ALU.mult,
        [b_mod, pc_["n2c"]], [b_mod])
    S1c_v = modcol_v[:, 0]
    S2c_v = modcol_v[:, 2]

    S.barrier()

    conv_jobs = []

    pend_store = []

    def _mk_up(j):
        def f():
            dma("pool", upring[j % 3], wup_d[j], [], [b_up[j % 3]])
            flush_store()
            pend_store.append(lambda: dma("pool", s_up[j], upring[j % 3], [b_up[j % 3]], [b_sup[j]]))
        return f

    def _mk_dn(j):
        def f():
            dma("pool", dnring[j % 3], wdn_d[j], [], [b_dn[j % 3]])
            flush_store()
            pend_store.append(lambda: dma("pool", s_dn[j], dnring[j % 3], [b_dn[j % 3]], [b_sdn[j]]))
        return f

    def _mk_out(c):
        def f():
            dma("pool", dnring[(c + 1) % 3], wout_d[c], [], [b_dn[(c + 1) % 3]])
            flush_store()
            pend_store.append(lambda: dma("pool", s_out[c], dnring[(c + 1) % 3], [b_dn[(c + 1) % 3]], [b_sout[c]]))
        return f

    def flush_store():
        while pend_store:
            pend_store.pop(0)()

    for c in range(8):
        conv_jobs.append(_mk_out(c))
    for j in range(NJ):
        conv_jobs.append(_mk_up(j))
        conv_jobs.append(_mk_dn(j))

    def pump_conv(n):
        for _ in range(n):
            if conv_jobs:
                conv_jobs.pop(0)()
        if not conv_jobs:
            flush_store()

    tp_i = [0]
    mmb_i = [0]

    def next_mm_bank():
        i = 4 + mmb_i[0] % 3
        mmb_i[0] += 1
        return i

    xs_i = [0]

    def norm_T(src_ap_fn, ntile, Gc, Sc, vec, dst_v, b_dst, load, xs_list, b_xs_list, xn_list, b_xn_list):
        for g0 in range(0, ntile, 2):
            pr = (tp_i[0] % 2) * 2
            tp_i[0] += 1
            tpv = bank(pr, 2).bitcast(BF16)[:, 0:2048].rearrange("p (c t) -> p c t", c=8, t=256)
            for tl in range(g0, min(g0 + 2, ntile)):
                xt, b_xt = src_ap_fn(tl)
                ss, b_ss = next_small()
                act(junk[:, 0:1024], xt, AF.Square, [b_xt], [b_junk, b_ss], accum=ss[:, 0:1])
                rs, b_rs = rstd_of(ss[:, 0:1], b_ss, D)
                k = xs_i[0] % len(xn_list)
                xs_i[0] += 1
                ts("dve", xn_list[k][:, 0:1024], xt, rs[:, 0:1], None, ALU.mult, None, [b_xt, b_rs], [b_xn_list[k]])
                for c in range(8):
                    tr(tpv[:, c, (tl - g0) * 128:(tl - g0 + 1) * 128], xn_list[k][:, c * 128:(c + 1) * 128],
                       [b_xn_list[k]], [pb[pr], pb[pr + 1]])
            n = min(2, ntile - g0) * 128
            for c in range(8):
                act(dst_v[:, c, g0 * 128:g0 * 128 + n], tpv[:, c, 0:n], AF.Identity, [pb[pr], pb[pr + 1], b_mod], [b_dst],
                    scale=Gc[:, c, vec:vec + 1], bias=Sc[:, c, vec:vec + 1])

    def group_front(src_fn, ntl, xn_list, b_xn_list, junk_ap, b_junk_):
        ss, b_ss = next_small()
        srcs = []
        for i in range(ntl):
            xt, b_xt = src_fn(i)
            srcs.append((xt, b_xt))
            act(junk_ap[:, 0:1024], xt, AF.Square, [b_xt], [b_junk_, b_ss], accum=ss[:, i:i + 1])
        ms, b_ms = next_small()
        ts("dve", ms[:, 0:ntl], ss[:, 0:ntl], 1.0 / D, EPS, ALU.mult, ALU.add, [b_ss], [b_ms])
        rs, b_rs = next_small()
        tt("pool", rs[:, 0:ntl], ms[:, 0:ntl], cst_m05[:, 0:1].broadcast_to([128, ntl]), ALU.pow, [b_ms, b_const], [b_rs])
        outs = []
        for i in range(ntl):
            xt, b_xt = srcs[i]
            k = xs_i[0] % len(xn_list)
            xs_i[0] += 1
            ts("dve", xn_list[k][:, 0:1024], xt, rs[:, i:i + 1], None, ALU.mult, None, [b_xt, b_rs], [b_xn_list[k]])
            outs.append((xn_list[k], b_xn_list[k]))
        return outs

    def group_back(xns, Gc, Sc, vec, dst_v, b_dst, col0):
        pr = (tp_i[0] % 2) * 2
        tp_i[0] += 1
        tpv = bank(pr, 2).bitcast(BF16)[:, 0:2048].rearrange("p (c t) -> p c t", c=8, t=256)
        for i, (xa, b_xa) in enumerate(xns):
            for c in range(8):
                tr(tpv[:, c, i * 128:(i + 1) * 128], xa[:, c * 128:(c + 1) * 128], [b_xa], [pb[pr], pb[pr + 1]])
        n = len(xns) * 128
        for c in range(8):
            act(dst_v[:, c, col0:col0 + n], tpv[:, c, 0:n], AF.Identity, [pb[pr], pb[pr + 1], b_mod], [b_dst],
                scale=Gc[:, c, vec:vec + 1], bias=Sc[:, c, vec:vec + 1])

    def phase1(b):
        if b > 0:
            for c in range(8):
                dma("pool", w_in_v[:, c, :], win_d[c], [], [b_win])
        dma("sp", hhn_t[:, 0:512], hn_d.partition_broadcast(128), [], [b_hhn])
        ts("dve", hhn_t[:, 0:512], hhn_t[:, 0:512], 0.5, None, ALU.mult, None, [b_hhn], [b_hhn])
        blocks = [dict(name="ctx", tokoff=0, bi=0, nblk=1, tpb=2, vec=2, src=ctx_d, tt0=0)]
        for bi in range(4):
            blocks.append(dict(name="x", tokoff=256, bi=bi, nblk=4, tpb=4, vec=b, src=x_d, tt0=2))
        xload_i = [0]
        for gi_, B_ in enumerate(blocks):
            B_["slot"] = gi_ % 3

        def front(B_, g):
            ntl = min(2, B_["tpb"] - 2 * g)

            def src(i):
                pump_conv(2)
                k = xload_i[0] % 2
                xload_i[0] += 1
                r0 = (B_["bi"] * B_["tpb"] + 2 * g + i) * 128
                dma("sp", xs[k][:, 0:1024], B_["src"][b, r0:r0 + 128, :], [], [b_xs[k]])
                return xs[k][:, 0:1024], b_xs[k]
            return group_front(src, ntl, xn, b_xn, junk, b_junk)

        def back(B_, g, xns):
            sl = B_["slot"]
            group_back(xns, G1c_v, S1c_v, B_["vec"], hT_v[sl], b_hT[sl], g * 256)

        def tok_tiles(B_, tls):
            sl = B_["slot"]
            h_v = hT_v[sl]
            name, bi, tpb = B_["name"], B_["bi"], B_["tpb"]
            for tl in tls:
                tti = B_["tt0"] + bi * tpb + tl
                groups = ["v", "gates"] if name == "ctx" else ["pool", "v", "o", "gates"]
                for grp in groups:
                    c0, ncol = {"pool": (0, 512), "v": (1024, 512), "o": (1536, 512), "gates": (2048, 16)}[grp]
                    bk = next_mm_bank()
                    for c in range(8):
                        mm(bank(bk)[:, 0:ncol], h_v[:, c, tl * 128:(tl + 1) * 128], w_in_v[:, c, c0:c0 + ncol],
                           c == 0, c == 7, [b_hT[sl], b_win], [pb[bk]])
                    if grp == "pool":
                        ti = bi * tpb + tl
                        act(poolin_v[:, ti, :], bank(bk), AF.Copy, [pb[bk]], [b_poolin[ti]])
                    elif grp == "v":
                        cp("dve", vaug_v[:, tti, :, 0:128], v3(bank(bk), 4, 128), [pb[bk]], [b_vaug[tti]])
                    elif grp == "o":
                        ti = bi * tpb + tl
                        act(otmp[0][:, 0:512], bank(bk), AF.Tanh, [pb[bk]], [b_otmp[0]], scale=0.5)
                        stt("dve", sog_v[:, ti, :], otmp[0][:, 0:512], 1.0, hhn_t[:, 0:512], ALU.add, ALU.mult,
                            [b_otmp[0], b_hhn], [b_sog[ti]])
                    else:
                        tt("dve", gsb_v[:, tti, :], bank(bk)[:, 0:16], gbias_t[:, 0:16], ALU.add,
                           [pb[bk], b_const], [b_gsb])

        def qk_block(B_, prevB, nextB):
            sl = B_["slot"]
            h_v = hT_v[sl]
            bi, tpb, nblk = B_["bi"], B_["tpb"], B_["nblk"]
            n = tpb * 128
            left = bi > 0
            right = bi < nblk - 1
            hb = 7
            halo = bank(hb)[:, 0:8]
            for qc in range(4):
                bk = next_mm_bank()
                cw = 512 + qc * 128
                for c in range(8):
                    mm(bank(bk)[:, 0:n], w_in_v[:, c, cw:cw + 128], h_v[:, c, 0:n], c == 0, c == 7,
                       [b_hT[sl], b_win], [pb[bk]])
                if left:
                    hp = hT_v[prevB["slot"]]
                    for c in range(8):
                        mm(halo[:, 2 * qc:2 * qc + 1], w_in_v[:, c, cw:cw + 128], hp[:, c, 511:512], c == 0, c == 7,
                           [b_hT[prevB["slot"]], b_win], [pb[hb]])
                if right:
                    hn_ = hT_v[nextB["slot"]]
                    for c in range(8):
                        mm(halo[:, 2 * qc + 1:2 * qc + 2], w_in_v[:, c, cw:cw + 128], hn_[:, c, 0:1], c == 0, c == 7,
                           [b_hT[nextB["slot"]], b_win], [pb[hb]])
                k = qc % 2
                a = acc[k]
                psb = bank(bk)
                act(a[:, 0:n], psb[:, 0:n], AF.Identity, [pb[bk], b_const], [b_acc[k]], scale=convc[:, qc * 3 + 1:qc * 3 + 2])
                stt("dve", a[:, 1:n], psb[:, 0:n - 1], convc[:, qc * 3:qc * 3 + 1], a[:, 1:n], ALU.mult, ALU.add,
                    [pb[bk], b_const, b_acc[k]], [b_acc[k]])
                stt("dve", a[:, 0:n - 1], psb[:, 1:n], convc[:, qc * 3 + 2:qc * 3 + 3], a[:, 0:n - 1], ALU.mult, ALU.add,
                    [pb[bk], b_const, b_acc[k]], [b_acc[k]])
                if left:
                    stt("dve", a[:, 0:1], halo[:, 2 * qc:2 * qc + 1], convc[:, qc * 3:qc * 3 + 1], a[:, 0:1], ALU.mult, ALU.add,
                        [pb[hb], b_const, b_acc[k]], [b_acc[k]])
                if right:
                    stt("dve", a[:, n - 1:n], halo[:, 2 * qc + 1:2 * qc + 2], convc[:, qc * 3 + 2:qc * 3 + 3], a[:, n - 1:n],
                        ALU.mult, ALU.add, [pb[hb], b_const, b_acc[k]], [b_acc[k]])
                dstv = qT_v if qc < 2 else kT_v
                t0 = B_["tokoff"] + bi * n
                act(dstv[:, qc % 2, t0:t0 + n], a[:, 0:n], AF.Silu, [b_acc[k]], [b_qk])

        B0 = blocks[0]
        for g in range((B0["tpb"] + 1) // 2):
            back(B0, g, front(B0, g))
        for i_, B_ in enumerate(blocks):
            nxt = blocks[i_ + 1] if i_ + 1 < len(blocks) else None
            prv = blocks[i_ - 1] if i_ > 0 else None
            ng = (B_["tpb"] + 1) // 2
            nng = (nxt["tpb"] + 1) // 2 if nxt else 0
            for g in range(max(ng, nng)):
                xns = front(nxt, g) if (nxt and g < nng) else None
                if g < ng:
                    tok_tiles(B_, list(range(2 * g, min(2 * g + 2, B_["tpb"]))))
                if xns is not None:
                    back(nxt, g, xns)
            qk_block(B_, prv if (prv and prv["name"] == B_["name"]) else None,
                     nxt if (nxt and nxt["name"] == B_["name"]) else None)
        pump_conv(1000)

    def phase1b(b):
        act(a8_v[:, :, :], gsb_v[:, :, 8:16], AF.Exp, [b_gsb], [b_gate], scale=-1.0)
        act(nlf_v[:, :, :], a8_v[:, :, :], AF.Ln, [b_gate, b_const], [b_gate], bias=cst_one[:, 0:1])
        cs = bank(0)[:, 0:NTT * 8].rearrange("p (t g) -> p t g", t=NTT, g=8)
        bl = bank(1)[:, 0:NTT * 8].rearrange("p (t g) -> p t g", t=NTT, g=8)
        for tti in range(NTT):
            mm(cs[:, tti, 0:4], tri_u[:, 0:128], nlf_v[:, tti, 0:4], True, True, [b_gate, b_const], [pb[0]])
            mm(cs[:, tti, 4:8], tri_l[:, 0:128], nlf_v[:, tti, 4:8], True, True, [b_gate, b_const], [pb[0]])
            mm(bl[:, tti, :], ones_f[:, 0:128], nlf_v[:, tti, :], True, True, [b_gate, b_const], [pb[1]])
        tt("dve", a8_v[:, :, :], gsb_v[:, :, 0:8], cs, ALU.add, [b_gsb, pb[0], b_gate], [b_gate])
        act(E8_v[:, :, :], a8_v[:, :, :], AF.Exp, [b_gate], [b_gate])
        act(THR8_v[:, :, :], cs, AF.Exp, [pb[0], b_const], [b_gate], bias=cst_ln8[:, 0:1])
        blv = bl.rearrange("p t (d h) -> p t d h", d=2, h=4)
        act(dec_v[0:64], blv[0:64, :, :, 0:4:2], AF.Exp, [pb[1]], [b_gate], scale=-1.0)
        act(dec_v[64:128], blv[64:128, :, :, 1:4:2], AF.Exp, [pb[1]], [b_gate], scale=-1.0)

        def tok0(tti):
            return tti * 128 if tti < 2 else 256 + (tti - 2) * 128

        for k_ in range(4):
            for hf in range(2):
                S.add("pool", lambda e, a=kz[k_][hf]: e.memset(a[:, 0:256], 0.0), [], [b_kz[k_]])
        orders = (list(range(NTT)), [1, 0] + list(range(NTT - 1, 1, -1)))
        prevs = [None, None]
        for step in range(NTT):
            for d in range(2):
                tti = orders[d][step]
                Rv = v3(stR[d][:, 0:258], 2, 129)
                prev = prevs[d]
                k = d * 2 + step % 2
                kb = d + 2 * (step % 2)
                ktp = bank(kb).bitcast(BF16)[:, 0:256]
                t0 = tok0(tti)
                for pr in range(2):
                    tr(ktp[:, pr * 128:(pr + 1) * 128], kT_v[:, pr, t0:t0 + 128], [b_qk], [pb[kb]])
                ktp_v = v3(ktp, 2, 128)
                cp("act", v3(kz[k][0][:, 0:256], 2, 128)[:, :, 0:64], ktp_v[:, :, 0:64], [pb[kb]], [b_kz[k]])
                cp("act", v3(kz[k][1][:, 0:256], 2, 128)[:, :, 64:128], ktp_v[:, :, 64:128], [pb[kb]], [b_kz[k]])
                vp = vpI[k][:, 0:516].rearrange("p (h c) -> p h c", h=4, c=129)
                tt("pool", vp, vaug_v[:, tti], E8_v[:, tti, d * 4:d * 4 + 4].unsqueeze(2).broadcast_to([128, 4, 129]), ALU.mult,
                   [b_vaug[tti], b_gate], [b_vpI[k]])
                ub = 4 + d * 2 + step % 2
                U = bank(ub)[:, 0:258].rearrange("p (h c) -> p h c", h=2, c=129)
                for pr in range(2):
                    mm(U[:, pr, :], kz[k][0][:, pr * 128:(pr + 1) * 128], vp[:, 2 * pr, :], True, False,
                       [b_kz[k], b_vpI[k]], [pb[ub]])
                    mm(U[:, pr, :], kz[k][1][:, pr * 128:(pr + 1) * 128], vp[:, 2 * pr + 1, :], False, True,
                       [b_kz[k], b_vpI[k]], [pb[ub]])
                if prev is None:
                    cp("dve", Rv, U, [pb[ub]], [b_st[d]])
                else:
                    for hh in range(2):
                        dcol = dec_v[:, prev, d, hh:hh + 1]
                        if tti >= 2:
                            act(Cst_v[:, tti - 2, d, hh, :], Rv[:, hh, :], AF.Identity, [b_st[d], b_gate], [b_cst], scale=dcol)
                        stt("dve", Rv[:, hh, :], Rv[:, hh, :], dcol, U[:, hh, :], ALU.mult, ALU.add,
                            [b_st[d], b_gate, pb[ub]], [b_st[d]])
                prevs[d] = tti

    def phase3(b):
        if b > 0:
            for gi in range(2):
                dma("sp", gtr[gi][:, 0:1024], s_gt[gi], [b_sgt_dram], [b_gt])

        def load_wout():
            for c in range(8):
                dma("sp", wout_v[:, c, :], s_out[c], [b_sout[c]], [b_wout])

        def h1(nb, tl):
            t = nb * 4 + tl
            tti = t + 2
            t0 = 256 + t * 128
            k = t % 2
            Sps = v3(bank(0), 4, 128)
            qz_v = qz[k][:, 0:512].rearrange("p (f r c) -> p f r c", f=2, r=2, c=128)
            for hf in range(2):
                act(qz_v[:, hf], qT_v[:, :, t0:t0 + 128], AF.Copy, [b_qk, b_const], [b_qz[k]], scale=hmask[:, hf:hf + 1])
            for h in range(4):
                mm(Sps[:, h, :], kT_v[:, h // 2, t0:t0 + 128], qz_v[:, h % 2, h // 2, :], True, True,
                   [b_qk, b_qz[k]], [pb[0]])
            Pf_v = v3(Pf[k][:, 0:512], 4, 128)
            Pb_v = v3(Pb[k][:, 0:512], 4, 128)
            tt("dve", Pf_v, Sps, tri_u[:, 0:128].unsqueeze(1).broadcast_to([128, 4, 128]), ALU.mult, [pb[0], b_const], [b_Pf[k]])
            tt("dve", Pb_v, Sps, tri_l[:, 0:128].unsqueeze(1).broadcast_to([128, 4, 128]), ALU.mult, [pb[0], b_const], [b_Pb[k]])
            vf = vpf[k][:, 0:516].rearrange("p (h c) -> p h c", h=4, c=129)
            vb = vpb[k][:, 0:516].rearrange("p (h c) -> p h c", h=4, c=129)
            tt("pool", vf, vaug_v[:, tti], E8_v[:, tti, 0:4].unsqueeze(2).broadcast_to([128, 4, 129]), ALU.mult,
               [b_vaug[tti], b_gate], [b_vpf[k]])
            tt("pool", vb, vaug_v[:, tti], E8_v[:, tti, 4:8].unsqueeze(2).broadcast_to([128, 4, 129]), ALU.mult,
               [b_vaug[tti], b_gate], [b_vpb[k]])

        def h2(nb, tl):
            t = nb * 4 + tl
            k = t % 2
            dps = v3(bank(0), 4, 128)
            for g in range(4):
                mm(dps[:, g, :], poolin_v[:, t, g * 128:(g + 1) * 128], ahl_v[:, 2 * g, :], True, False,
                   [b_poolin[t], b_const], [pb[0]])
                mm(dps[:, g, :], poolin_v[:, t, g * 128:(g + 1) * 128], ahl_v[:, 2 * g + 1, :], False, True,
                   [b_poolin[t], b_const], [pb[0]])
            cp("act", dT[k][:, 0:512], bank(0), [pb[0]], [b_dT[k]])

        def h3(nb, tl):
            t = nb * 4 + tl
            tti = t + 2
            k = t % 2
            qz_v = qz[k][:, 0:512].rearrange("p (f r c) -> p f r c", f=2, r=2, c=128)
            Pf_v = v3(Pf[k][:, 0:512], 4, 128)
            Pb_v = v3(Pb[k][:, 0:512], 4, 128)
            vf = vpf[k][:, 0:516].rearrange("p (h c) -> p h c", h=4, c=129)
            vb = vpb[k][:, 0:516].rearrange("p (h c) -> p h c", h=4, c=129)
            den = bank(3)[:, 0:8]
            for d, (Pv, vv, bP, bV) in enumerate(((Pf_v, vf, b_Pf[k], b_vpf[k]), (Pb_v, vb, b_Pb[k], b_vpb[k]))):
                NUM = v3(bank(1 + d), 4, 128)
                for h in range(4):
                    qh = qz_v[:, h % 2, h // 2, :]
                    mm(NUM[:, h, :], Pv[:, h, :], vv[:, h, 0:128], True, False, [bP, bV], [pb[1 + d]])
                    mm(NUM[:, h, :], qh, Cst_v[:, t, d, h // 2, 0:128], False, True, [b_qz[k], b_cst], [pb[1 + d]])
                    mm(den[:, d * 4 + h:d * 4 + h + 1], Pv[:, h, :], vv[:, h, 128:129], True, False, [bP, bV], [pb[3]])
                    mm(den[:, d * 4 + h:d * 4 + h + 1], qh, Cst_v[:, t, d, h // 2, 128:129], False, True,
                       [b_qz[k], b_cst], [pb[3]])
            dabs, b_dabs = next_small()
            act(dabs[:, 0:8], den, AF.Abs, [pb[3]], [b_dabs])
            dm, b_dm = next_small()
            tt("dve", dm[:, 0:8], dabs[:, 0:8], THR8_v[:, tti, :], ALU.max, [b_dabs, b_gate], [b_dm])
            r8, b_r8 = next_small()
            S.add("dve", lambda e, o=r8, i=dm: e.reciprocal(out=o[:, 0:8], in_=i[:, 0:8]), [b_dm], [b_r8])
            tt("dve", v3(t1[:, 0:512], 4, 128), v3(bank(1), 4, 128), r8[:, 0:4].unsqueeze(2).broadcast_to([128, 4, 128]), ALU.mult,
               [pb[1], b_r8], [b_t12])
            ss4, b_ss4 = next_small()
            for h in range(4):
                stt("dve", t1[:, h * 128:(h + 1) * 128], bank(2)[:, h * 128:(h + 1) * 128], r8[:, 4 + h:5 + h],
                    t1[:, h * 128:(h + 1) * 128], ALU.mult, ALU.add, [pb[2], b_r8, b_t12], [b_t12])
            for h in range(4):
                act(t2[:, h * 128:(h + 1) * 128], t1[:, h * 128:(h + 1) * 128], AF.Square, [b_t12], [b_t12, b_ss4],
                    accum=ss4[:, h:h + 1])
            ms4, b_ms4 = next_small()
            ts("dve", ms4[:, 0:4], ss4[:, 0:4], 1.0 / 128, EPS, ALU.mult, ALU.add, [b_ss4], [b_ms4])
            rs4, b_rs4 = next_small()
            tt("pool", rs4[:, 0:4], ms4[:, 0:4], cst_m05[:, 0:1].broadcast_to([128, 4]), ALU.pow, [b_ms4, b_const], [b_rs4])
            for h in range(4):
                stt("dve", mo[k][:, h * 128:(h + 1) * 128], t1[:, h * 128:(h + 1) * 128], rs4[:, h:h + 1],
                    sog_v[:, t, h * 128:(h + 1) * 128], ALU.mult, ALU.mult, [b_t12, b_rs4, b_sog[t]], [b_mo[k]])

        def h4(nb, tl):
            t = nb * 4 + tl
            k = t % 2
            pops = v3(bank(0), 4, 128)
            for g in range(4):
                mm(pops[:, g, :], poolw_v[:, g, :], dT[k][:, g * 128:(g + 1) * 128], True, True, [b_dT[k], b_const], [pb[0]])
            tt("dve", mixT_v[tl][:, 0:4, :], pops, poolsc[:, 0:4].unsqueeze(2).broadcast_to([128, 4, 128]), ALU.mult,
               [pb[0], b_const], [b_mixT[tl]])

        def h5(nb, tl):
            t = nb * 4 + tl
            k = t % 2
            moT = bank(3).bitcast(BF16)[:, 512:1024].rearrange("p (h c) -> p h c", h=4, c=128)
            for h in range(4):
                tr(moT[:, h, :], mo[k][:, h * 128:(h + 1) * 128], [b_mo[k]], [pb[3]])
            cp("act", mixT_v[tl][:, 4:8, :], moT, [pb[3]], [b_mixT[tl]])

        def head_pieces(nb):
            seq = [(h1, 0), (h2, 0), (h1, 1), (h2, 1), (h3, 0), (h4, 0), (h3, 1), (h4, 1), (h1, 2), (h2, 2), (h5, 0),
                   (h1, 3), (h2, 3), (h5, 1), (h3, 2), (h4, 2), (h3, 3), (h4, 3), (h5, 2), (h5, 3)]
            return [(lambda f=f, tl=tl: f(nb, tl)) for f, tl in seq]

        def tail(nb):
            for tl in range(4):
                r0 = (nb * 4 + tl) * 128
                dma("sp", x1b[tl][:, 0:1024], x_d[b, r0:r0 + 128, :], [], [b_x1[tl]])
            for tl in range(4):
                bp = 4 + 2 * (tl % 2)
                for half in range(2):
                    for kc in range(8):
                        mm(bank(bp + half), mixT_v[tl][:, kc, :], wout_v[:, kc, half * 512:(half + 1) * 512], kc == 0, kc == 7,
                           [b_mixT[tl], b_wout], [pb[bp + half]])
                tt("dve", tmp4k[:, 0:1024], bank(bp, 2), gtr[0][:, 0:1024], ALU.mult, [pb[bp], pb[bp + 1], b_gt], [b_tmp4k])
                tt("pool", x1b[tl][:, 0:1024], tmp4k[:, 0:1024], x1b[tl][:, 0:1024], ALU.add, [b_tmp4k, b_x1[tl]], [b_x1[tl]])
            sgt_j = sgt[0].bitcast(BF16)
            for g in range(2):
                def src2(i, g=g):
                    return x1b[2 * g + i][:, 0:1024], b_x1[2 * g + i]
                xg = group_front(src2, 2, xn2, b_xn2, sgt_j, b_sgt[0])
                group_back(xg, G2c_v, S2c_v, b, h2T_v, b_h2T, g * 256)

        def ffn_up(nb, pieces):
            for j in range(min(2, NJ)):
                dma("sp", upring[j % 3], s_up[j], [b_sup[j]], [b_up[j % 3]])
            for j in range(NJ):
                if j + 2 < NJ:
                    dma("sp", upring[(j + 2) % 3], s_up[j + 2], [b_sup[j + 2]], [b_up[(j + 2) % 3]])
                if j == NJ - 2:
                    for jj in range(2):
                        dma("sp", dnring[jj % 3], s_dn[jj], [b_sdn[jj]], [b_dn[jj % 3]])
                upv = v3(upring[j % 3], 8, 256)
                gb_, ab_ = 4 + (j % 2), 6 + (j % 2)
                for c in range(8):
                    mm(bank(gb_), upv[:, c, 0:128], h2T_v[:, c, :], c == 0, c == 7, [b_up[j % 3], b_h2T], [pb[gb_]])
                for c in range(8):
                    mm(bank(ab_), upv[:, c, 128:256], h2T_v[:, c, :], c == 0, c == 7, [b_up[j % 3], b_h2T], [pb[ab_]])
                act(sgt[0][:, 0:512], bank(gb_), AF.Silu, [pb[gb_]], [b_sgt[0]])
                tt("dve", actT_v[:, j, :], bank(ab_), sgt[0][:, 0:512], ALU.mult, [pb[ab_], b_sgt[0]],
                   [b_actT_lo if j < 16 else b_actT_hi])
                if pieces:
                    pieces.pop(0)()
            while pieces:
                pieces.pop(0)()

        def ffn_down(nb, last):
            for j in range(NJ):
                if j + 2 < NJ:
                    dma("sp", dnring[(j + 2) % 3], s_dn[j + 2], [b_sdn[j + 2]], [b_dn[(j + 2) % 3]])
                if j == 16 and not last:
                    load_wout()
                for tl in range(4):
                    for half in range(2):
                        bk = tl * 2 + half
                        mm(bank(bk), actT_v[:, j, tl * 128:(tl + 1) * 128], dnring[j % 3][:, half * 512:(half + 1) * 512],
                           j == 0, j == NJ - 1, [b_actT_lo if j < 16 else b_actT_hi, b_dn[j % 3]], [pb[bk]])
            for tl in range(4):
                t = nb * 4 + tl
                tt("dve", tmp4k[:, 0:1024], bank(tl * 2, 2), gtr[1][:, 0:1024], ALU.mult, [pb[tl * 2], pb[tl * 2 + 1], b_gt], [b_tmp4k])
                tt("pool", x1b[tl][:, 0:1024], tmp4k[:, 0:1024], x1b[tl][:, 0:1024], ALU.add, [b_tmp4k, b_x1[tl]], [b_x1[tl]])
                ss, b_ss = next_small()
                act(xn2[tl % 2][:, 0:1024], x1b[tl][:, 0:1024], AF.Square, [b_x1[tl]], [b_xn2[tl % 2], b_ss], accum=ss[:, 0:1])
                rs, b_rs = rstd_of(ss[:, 0:1], b_ss, D)
                stt("dve", x1b[tl][:, 0:1024], x1b[tl][:, 0:1024], rs[:, 0:1], normf_t[:, 0:1024], ALU.mult, ALU.mult,
                    [b_x1[tl], b_rs, b_const], [b_x1[tl]])
                dma("pool", out_d[b, t * 128:(t + 1) * 128, :], x1b[tl][:, 0:1024], [b_x1[tl]], [b_out])

        load_wout()
        for pc in head_pieces(0):
            pc()
        for nb in range(4):
            tail(nb)
            ffn_up(nb, head_pieces(nb + 1) if nb < 3 else [])
            ffn_down(nb, nb == 3)

    try:
        for b in range(2):
            phase1(b)
            if stop_after == "1" and b == 0:
                break
            phase1b(b)
            if stop_after == "1b" and b == 0:
                break
            S.barrier()
            phase3(b)
            S.barrier()
            if stop_after == "3" and b == 0:
                break
    except StopBuild:
        pass

    if dbg_d is not None:
        S.barrier()
        dma("pool", dbg_d[:, 0:dbg_cols], mem[:, 0:dbg_cols], [], [b_out])

    import contextlib
    with contextlib.ExitStack() as es:
        sems = {e: es.enter_context(nc.semaphore("s_" + e)) for e in ["pe", "act", "dve", "pool"]}
        dsems = {e: [es.enter_context(nc.semaphore("d_%s%d" % (e, i))) for i in range(n)] for e, n in NDSEM.items()}
        S.finalize(nc, sems, dsems)
        block = es.enter_context(nc.Block())

        @block.tensor
        def _(e):
            S.emit("pe", e, sems, dsems)

        @block.scalar
        def _(e):
            S.emit("act", e, sems, dsems)

        @block.vector
        def _(e):
            S.emit("dve", e, sems, dsems)

        @block.gpsimd
        def _(e):
            S.emit("pool", e, sems, dsems)

        @block.sync
        def _(e):
            S.emit("sp", e, sems, dsems)

    layout = dict(A_end=A_end, DE_off=DE_off, D_end=D_end, E_end=E_end, R1_off=R1_off, TOTAL=TOTAL)
    _aps = dict(poolin=poolin, vaug=vaug, sog=sog, qT=qT, kT=kT, gsb=gsb, E8=E8, THR8=THR8, nlf=nlf, a8=a8, dec=dec,
                Cst=Cst, modcol=modcol, G1c=G1c, G2c=G2c, gtr0=gtr[0], gtr1=gtr[1], stR0=stR[0], stR1=stR[1],
                x1b0=x1b[0], x1b1=x1b[1], x1b2=x1b[2], x1b3=x1b[3], h2T=h2T, actT=actT, mixT=mixT[0], hT0=hT[0], hT1=hT[1], hT2=hT[2])
    layout["aps"] = {k: (int(v.offset) * (2 if v.dtype == BF16 else 4), int(v.shape[1]), "bf16" if v.dtype == BF16 else "f32")
                     for k, v in _aps.items()}
    return nc, layout


def make_in_maps(x, c, ctx, c_ctx, w_ada, b_ada, norm1, w_in, conv_qk, gate_bias, pool_w,
                 pool_scale, head_norm, w_out, norm2, w_up, w_down, norm_f):
    f32 = np.float32
    x = np.asarray(x, f32)
    c = np.asarray(c, f32)
    ctx = np.asarray(ctx, f32)
    c_ctx = np.asarray(c_ctx, f32)
    w_ada = np.asarray(w_ada, f32)[0]
    b_ada = np.asarray(b_ada, f32)[0]
    norm1 = np.asarray(norm1, f32)[0]
    w_in = np.asarray(w_in, f32)[0]
    conv_qk = np.asarray(conv_qk, f32)[0]
    gate_bias = np.asarray(gate_bias, f32)[0]
    pool_w = np.asarray(pool_w, f32)[0]
    pool_scale = np.asarray(pool_scale, f32)[0]
    head_norm = np.asarray(head_norm, f32)[0]
    w_out = np.asarray(w_out, f32)[0]
    norm2 = np.asarray(norm2, f32)[0]
    w_up = np.asarray(w_up, f32)[0]
    w_down = np.asarray(w_down, f32)[0]
    norm_f = np.asarray(norm_f, f32)

    def colform(v, nchunk):
        return np.ascontiguousarray(v.reshape(nchunk, 128).T)

    w_ada_l = np.ascontiguousarray(w_ada.reshape(8, 128, 6, 1024).transpose(2, 1, 0, 3)).reshape(6, 128, 8 * 1024)
    bseg = b_ada.reshape(6, 1024)
    b_ada_col = np.ascontiguousarray(
        np.stack([colform(bseg[s], 8) for s in (0, 1, 3, 4)], axis=1)).reshape(128, 32)
    b_ada_row = np.ascontiguousarray(np.stack([bseg[2], bseg[5]], axis=0))
    w_in_l = np.ascontiguousarray(w_in.reshape(8, 128, INW))
    conv_c = np.ascontiguousarray(conv_qk.reshape(3, 4, 128).transpose(2, 1, 0)).reshape(128, 12)
    pool_w_l = np.ascontiguousarray(pool_w.transpose(1, 0, 2)).reshape(128, 512)
    pool_sc = colform(pool_scale, 4)
    w_out_l = np.ascontiguousarray(w_out.reshape(8, 128, 1024))
    wu = w_up.reshape(8, 128, 2, NJ, 128)
    w_up_l = np.ascontiguousarray(wu.transpose(3, 1, 0, 2, 4)).reshape(NJ, 128, 2048)
    w_down_l = np.ascontiguousarray(w_down.reshape(NJ, 128, 1024))
    s_idx = np.arange(128)
    tri = np.stack([
        (s_idx[:, None] <= s_idx[None, :]).astype(f32),
        (s_idx[:, None] >= s_idx[None, :]).astype(f32),
        np.ones((128, 128), f32)], axis=0)
    ident = np.eye(128, dtype=f32)
    pm = _pool_mats()
    ahl = np.ascontiguousarray(pm.transpose(2, 0, 1, 3)).reshape(128, 1024)

    shared = dict(
        w_ada_l=w_ada_l, b_ada_col=b_ada_col, b_ada_row=b_ada_row,
        norm1c=colform(norm1, 8), norm2c=colform(norm2, 8),
        normf_row=np.ascontiguousarray(norm_f.reshape(1, 1024)), hn_row=np.ascontiguousarray(head_norm.reshape(1, 512)),
        w_in_l=w_in_l, conv_c=conv_c, gb_row=np.ascontiguousarray(gate_bias.reshape(1, 16)),
        pool_w_l=pool_w_l, pool_sc=pool_sc, w_out_l=w_out_l, w_up_l=w_up_l, w_down_l=w_down_l,
        tri=tri, ident=ident, ahl=ahl,
        hmask=np.ascontiguousarray(np.stack([(s_idx < 64), (s_idx >= 64)], axis=1).astype(f32)),
    )
    in_maps = []
    for core in range(NCORES):
        b0 = 2 * core
        cT = np.stack([colform(c[b0], 8), colform(c[b0 + 1], 8), colform(c_ctx, 8)], axis=2).reshape(128, 24)
        m = dict(shared)
        m["x"] = np.ascontiguousarray(x[b0:b0 + 2])
        m["ctx"] = np.ascontiguousarray(ctx[b0:b0 + 2])
        m["cT"] = np.ascontiguousarray(cT)
        in_maps.append(m)
    return in_maps


_PROGRAM = None


def kernel(x, c, ctx, c_ctx, w_ada, b_ada, norm1, w_in, conv_qk, gate_bias, pool_w,
           pool_scale, head_norm, w_out, norm2, w_up, w_down, norm_f):
    global _PROGRAM
    in_maps = make_in_maps(x, c, ctx, c_ctx, w_ada, b_ada, norm1, w_in, conv_qk, gate_bias, pool_w,
                           pool_scale, head_norm, w_out, norm2, w_up, w_down, norm_f)
    if _PROGRAM is None:
        _PROGRAM = build_program()[0]
    res = run_bass_kernel_spmd(_PROGRAM, in_maps, core_ids=list(range(NCORES)))
    out = np.concatenate([np.asarray(r["out"], np.float32) for r in res.results], axis=0)
    return out
```

```python
import numpy as np
import ml_dtypes
import concourse.bass as bass
import concourse.mybir as mybir
from concourse.bass_utils import run_bass_kernel_spmd

F32 = mybir.dt.float32
BF16 = mybir.dt.bfloat16
AF = mybir.ActivationFunctionType
ALU = mybir.AluOpType

NCORES = 8
D = 1024
T = 2048
TC = 256
NT = 16
NTT = 18
DFF = 2816
NJ = 22
INW = 2064
EPS = 1e-6
LN8 = float(np.log(8.0))


class Buf:
    __slots__ = ("name", "lw", "rd")

    def __init__(self, name):
        self.name = name
        self.lw = None
        self.rd = []


class Op:
    __slots__ = ("eng", "fn", "raw", "oth", "idx", "signal", "ticket", "dma", "dsem", "dticket", "waits", "extra")

    def __init__(self, eng, fn, dma):
        self.eng = eng
        self.fn = fn
        self.dma = dma
        self.raw = []
        self.oth = []
        self.signal = False
        self.ticket = 0
        self.dsem = None
        self.dticket = 0
        self.waits = []
        self.extra = []


ENGS = ["pe", "act", "dve", "pool", "sp"]
NDSEM = {"sp": 12, "pool": 8, "act": 4}


class Sched:
    def __init__(self):
        self.q = {e: [] for e in ENGS}
        self.pending = {e: [] for e in ENGS}
        self.dma_since = []
        self.final_dma = {}

    def add(self, eng, fn, reads=(), writes=(), dma=False):
        op = Op(eng, fn, dma)
        op.idx = len(self.q[eng])
        for b in reads:
            if b.lw is not None:
                op.raw.append(b.lw)
        for b in writes:
            if b.lw is not None:
                op.oth.append(b.lw)
            op.oth.extend(b.rd)
        for b in reads:
            b.rd.append(op)
        for b in writes:
            b.lw = op
            b.rd = []
        if self.pending[eng]:
            op.oth.extend(self.pending[eng])
            self.pending[eng] = []
        self.q[eng].append(op)
        if dma:
            self.dma_since.append(op)
        return op

    def barrier(self):
        lasts = []
        for e in ENGS:
            for op in reversed(self.q[e]):
                if not op.dma:
                    lasts.append(op)
                    break
        lasts.extend(self.dma_since)
        self.dma_since = []
        for e in ENGS:
            self.pending[e] = self.pending[e] + list(lasts)

    def finalize(self, nc, sems, dsems):
        for e in ENGS:
            for op in self.q[e]:
                need = {}
                dmad = []
                for kind, lst in (("raw", op.raw), ("oth", op.oth)):
                    for d in lst:
                        if d is op:
                            continue
                        if d.dma:
                            dmad.append(d)
                            continue
                        if d.eng == op.eng and not op.dma:
                            if d.eng == "pe":
                                continue
                            if kind != "raw":
                                continue
                            if op.idx - d.idx > 2:
                                continue
                        k = d.eng
                        if k not in need or need[k].idx < d.idx:
                            need[k] = d
                op.extra = (list(need.values()), dmad)
                for d in need.values():
                    d.signal = True
        for e in ENGS:
            cnt = 0
            k = 0
            hist = {}
            for op in self.q[e]:
                if op.dma:
                    n = NDSEM[e]
                    si = k % n
                    op.dsem = dsems[e][si]
                    op.dticket = 16 * (k // n + 1)
                    if k >= n:
                        op.waits.append((op.dsem, 16 * (k // n)))
                    k += 1
                elif op.signal:
                    cnt += 1
                    op.ticket = cnt
            self.final_dma[e] = k
        for e in ENGS:
            waited = {}
            for op in self.q[e]:
                need, dmad = op.extra
                ws = list(op.waits)
                for d in need:
                    ws.append((sems[d.eng], d.ticket))
                for d in dmad:
                    ws.append((d.dsem, d.dticket))
                out = []
                for s, v in ws:
                    key = id(s)
                    if waited.get(key, 0) >= v:
                        continue
                    waited[key] = v
                    out.append((s, v))
                op.waits = out

    def emit(self, eng, e, sems, dsems):
        for op in self.q[eng]:
            for s, v in op.waits:
                e.wait_ge(s, v)
            ins = op.fn(e)
            if op.dma:
                ins.then_inc(op.dsem, 16)
            elif op.signal:
                ins.then_inc(sems[eng], 1)
        if eng in NDSEM:
            k = self.final_dma.get(eng, 0)
            n = NDSEM[eng]
            for si in range(min(n, k)):
                cntd = (k - si + n - 1) // n
                e.wait_ge(dsems[eng][si], 16 * cntd)


def _pool_mats():
    wins = (2, 4, 8, 16)
    out = np.zeros((4, 2, 128, 128), np.float32)
    for gi, win in enumerate(wins):
        A = np.zeros((64, 64), np.float64)
        for pos in range(64):
            lo = min(max(pos - win // 2, 0), 63)
            hi = min(max(pos + win // 2 - 1, 0), 63)
            cnt = hi - lo + 1
            if hi >= lo:
                A[pos, lo:hi + 1] += 1.0 / cnt
            A[pos, pos] -= 1.0
        A2 = np.zeros((128, 128), np.float64)
        A2[:64, :64] = A
        A2[64:, 64:] = A
        R = A2.T.astype(np.float32)
        hi_ = R.astype(ml_dtypes.bfloat16).astype(np.float32)
        lo_ = (R - hi_).astype(ml_dtypes.bfloat16).astype(np.float32)
        out[gi, 0] = hi_
        out[gi, 1] = lo_
    return out


class StopBuild(Exception):
    pass


class Region:
    def __init__(self, mem):
        self.mem = mem
        self.off = 0
        self.hi = 0

    def take(self, nbytes, dt=F32):
        nb = (nbytes + 31) // 32 * 32
        o = self.off
        self.off += nb
        self.hi = max(self.hi, self.off)
        ap = self.mem[:, o // 4:(o + nb) // 4]
        if dt != F32:
            ap = ap.bitcast(dt)
            return ap[:, 0:nbytes // 2]
        return ap[:, 0:nbytes // 4]


def build_program(stop_after=None, dbg_cols=0):
    nc = bass.Bass("TRN2", target_bir_lowering=False)

    def din(name, shape, dt=F32):
        return nc.dram_tensor(name, list(shape), dt, kind="ExternalInput").ap()

    x_d = din("x", [2, T, D])
    ctx_d = din("ctx", [2, TC, D])
    cT_d = din("cT", [128, 24])
    wada_d = din("w_ada_l", [6, 128, 8 * 1024])
    bcol_d = din("b_ada_col", [128, 32])
    brow_d = din("b_ada_row", [2, 1024])
    n1c_d = din("norm1c", [128, 8])
    n2c_d = din("norm2c", [128, 8])
    normf_d = din("normf_row", [1, 1024])
    hn_d = din("hn_row", [1, 512])
    win_d = din("w_in_l", [8, 128, INW])
    conv_d = din("conv_c", [128, 12])
    gb_d = din("gb_row", [1, 16])
    poolw_d = din("pool_w_l", [128, 512])
    poolsc_d = din("pool_sc", [128, 4])
    wout_d = din("w_out_l", [8, 128, 1024])
    wup_d = din("w_up_l", [NJ, 128, 2048])
    wdn_d = din("w_down_l", [NJ, 128, 1024])
    tri_d = din("tri", [3, 128, 128])
    ident_d = din("ident", [128, 128])
    ahl_d = din("ahl", [128, 1024])
    hmask_d = din("hmask", [128, 2])
    out_d = nc.dram_tensor("out", [2, T, D], F32, kind="ExternalOutput").ap()
    s_up = nc.dram_tensor("s_up", [NJ, 128, 2048], BF16, kind="Internal").ap()
    s_dn = nc.dram_tensor("s_dn", [NJ, 128, 1024], BF16, kind="Internal").ap()
    s_out = nc.dram_tensor("s_out", [8, 128, 1024], BF16, kind="Internal").ap()
    s_gt = nc.dram_tensor("s_gt", [2, 128, 1024], F32, kind="Internal").ap()
    dbg_d = None
    if dbg_cols:
        dbg_d = nc.dram_tensor("dbg", [128, dbg_cols], F32, kind="ExternalOutput").ap()

    S = Sched()
    TOTAL = 210000 // 4
    mem_g = nc.sbuf_tensor("mem", [128, TOTAL], F32)
    mem = mem_g.__enter__()
    ps_g = nc.psum_tensor("ps", [128, 4096], F32)
    ps = ps_g.__enter__()
    R = Region(mem)

    def bank(i, n=1):
        return ps[:, i * 512:(i + n) * 512]

    pb = [Buf("pb%d" % i) for i in range(8)]

    ident = R.take(256, BF16)
    tri_u = R.take(512)
    tri_l = R.take(512)
    ones_f = R.take(512)
    ahl = R.take(2048, BF16)
    poolw = R.take(1024, BF16)
    normf_t = R.take(4096)
    gbias_t = R.take(64)
    convc = R.take(48)
    poolsc = R.take(16)
    bcol = R.take(128)
    n1c = R.take(32)
    n2c = R.take(32)
    cT = R.take(96)
    scb = R.take(48, BF16)
    modcol = R.take(4 * 8 * 3 * 4)
    G1c = R.take(96)
    G2c = R.take(96)
    hmask = R.take(8)
    cst_m05 = R.take(4)
    cst_one = R.take(4)
    cst_ln8 = R.take(4)
    gtr = [R.take(4096) for _ in range(2)]
    small = [R.take(64) for _ in range(8)]
    A_end = R.off

    b_const = Buf("const")
    b_mod = Buf("mod")
    b_gt = Buf("gt")
    b_gtmp = Buf("gtmp")

    poolin = R.take(NT * 512 * 2, BF16)
    vaug = R.take(NTT * 516 * 2, BF16)
    sog = R.take(NT * 512 * 2, BF16)
    qT = R.take(2 * 2304 * 2, BF16)
    kT = R.take(2 * 2304 * 2, BF16)
    gsb = R.take(NTT * 16 * 4)
    E8 = R.take(NTT * 8 * 4)
    THR8 = R.take(NTT * 8 * 4)
    nlf = R.take(NTT * 8 * 4)
    a8 = R.take(NTT * 8 * 4)
    dec = R.take(NTT * 4 * 4)
    stR = [R.take(2 * 129 * 4) for _ in range(2)]
    R1_off = R.off
    hT = [R.take(8 * 512 * 2, BF16) for _ in range(2)]
    R.off = R1_off
    Cst = R.take(NT * 2 * 2 * 129 * 2, BF16)
    upring = [R.take(4096, BF16) for _ in range(3)]
    dnring = [R.take(2048, BF16) for _ in range(3)]
    DE_off = R.off

    w_in = R.take(8 * INW * 2, BF16)
    xs = [R.take(4096) for _ in range(2)]
    xn = [R.take(2048, BF16) for _ in range(2)]
    junk = R.take(2048, BF16)
    acc = [R.take(2048) for _ in range(2)]
    kz = [[R.take(512, BF16) for _ in range(2)] for _ in range(4)]
    otmp = [R.take(2048) for _ in range(1)]
    vpI = [R.take(1056, BF16) for _ in range(4)]
    hT.append(R.take(8 * 512 * 2, BF16))
    hhn_t = R.take(2048)
    D_end = R.off
    R.off = DE_off + 8 * INW * 2
    adaB = R.take(16384, BF16)
    browt = [R.take(4096) for _ in range(2)]
    gtmp = [R.take(4096) for _ in range(2)]
    screp = R.take(8 * 2 * 128 * 2, BF16)
    P_end = R.off
    adaA = mem[:, R1_off // 4:(R1_off + 16384) // 4].bitcast(BF16)

    R.off = DE_off
    x1b = [R.take(4096) for _ in range(4)]
    xn2 = [R.take(2048, BF16) for _ in range(2)]
    h2T = R.take(8 * 512 * 2, BF16)
    actT = R.take(NJ * 512 * 2, BF16)
    wout = actT[:, 0:8 * 1024]
    mixT = [R.take(8 * 128 * 2, BF16) for _ in range(4)]
    vpf = [R.take(1056, BF16) for _ in range(2)]
    vpb = [R.take(1056, BF16) for _ in range(2)]
    Pf = [R.take(1024, BF16) for _ in range(2)]
    Pb = [R.take(1024, BF16) for _ in range(2)]
    qz = [R.take(1024, BF16) for _ in range(2)]
    t12 = R.take(4096)
    t1 = t12[:, 0:512]
    t2 = t12[:, 512:1024]
    hm = t1
    tmp4k = t12
    mo = [R.take(1024, BF16) for _ in range(2)]
    dT = [R.take(1024, BF16) for _ in range(2)]
    sgt = [R.take(2048) for _ in range(1)]
    E_end = R.off
    assert max(D_end, E_end, P_end) <= TOTAL * 4, (D_end, E_end, P_end, TOTAL * 4)

    def v3(ap, a, b):
        return ap.rearrange("p (a b) -> p a b", a=a, b=b)

    ahl_v = v3(ahl, 8, 128)
    poolw_v = v3(poolw, 4, 128)
    w_in_v = v3(w_in, 8, INW)
    wout_v = v3(wout, 8, 1024)
    hT_v = [v3(h, 8, 512) for h in hT]
    h2T_v = v3(h2T, 8, 512)
    actT_v = v3(actT, NJ, 512)
    qT_v = v3(qT, 2, 2304)
    kT_v = v3(kT, 2, 2304)
    poolin_v = v3(poolin, NT, 512)
    sog_v = v3(sog, NT, 512)
    vaug_v = vaug.rearrange("p (t h c) -> p t h c", t=NTT, h=4, c=129)
    gsb_v = v3(gsb, NTT, 16)
    E8_v = v3(E8, NTT, 8)
    THR8_v = v3(THR8, NTT, 8)
    nlf_v = v3(nlf, NTT, 8)
    a8_v = v3(a8, NTT, 8)
    dec_v = dec.rearrange("p (t d h) -> p t d h", t=NTT, d=2, h=2)
    Cst_v = Cst.rearrange("p (t d h c) -> p t d h c", t=NT, d=2, h=2, c=129)
    modcol_v = modcol.rearrange("p (s c v) -> p s c v", s=4, c=8, v=3)
    G1c_v = v3(G1c, 8, 3)
    G2c_v = v3(G2c, 8, 3)
    mixT_v = [v3(m, 8, 128) for m in mixT]

    b_win = Buf("w_in")
    b_hT = [Buf("hT%d" % i) for i in range(3)]
    b_xs = [Buf("xs%d" % i) for i in range(2)]
    b_xn = [Buf("xn%d" % i) for i in range(2)]
    b_junk = Buf("junk")
    b_small = [Buf("small%d" % i) for i in range(8)]
    b_acc = [Buf("acc%d" % i) for i in range(2)]
    b_kz = [Buf("kz%d" % i) for i in range(4)]
    b_qz = [Buf("qz%d" % i) for i in range(2)]
    b_otmp = [Buf("otmp%d" % i) for i in range(1)]
    b_vpI = [Buf("vpI%d" % i) for i in range(4)]
    b_poolin = [Buf("poolin%d" % i) for i in range(NT)]
    b_vaug = [Buf("vaug%d" % i) for i in range(NTT)]
    b_sog = [Buf("sog%d" % i) for i in range(NT)]
    b_qk = Buf("qk")
    b_gsb = Buf("gsb")
    b_gate = Buf("gate")
    b_st = [Buf("stR0"), Buf("stR1")]
    b_cst = Buf("cst")
    b_up = [Buf("up%d" % i) for i in range(3)]
    b_dn = [Buf("dn%d" % i) for i in range(3)]
    b_sup = [Buf("sup%d" % i) for i in range(NJ)]
    b_sdn = [Buf("sdn%d" % i) for i in range(NJ)]
    b_sout = [Buf("sout%d" % i) for i in range(8)]
    b_adaA = Buf("adaA")
    b_adaB = Buf("adaB")
    b_brow = Buf("brow")
    b_x1 = [Buf("x1_%d" % i) for i in range(4)]
    b_xn2 = [Buf("xn2_%d" % i) for i in range(2)]
    b_h2T = Buf("h2T")
    b_actT_lo = Buf("actT_lo")
    b_actT_hi = Buf("actT_hi")
    b_wout = b_actT_lo
    b_mixT = [Buf("mixT%d" % i) for i in range(4)]
    b_vpf = [Buf("vpf%d" % i) for i in range(2)]
    b_vpb = [Buf("vpb%d" % i) for i in range(2)]
    b_Pf = [Buf("Pf%d" % i) for i in range(2)]
    b_Pb = [Buf("Pb%d" % i) for i in range(2)]
    b_t12 = Buf("t12")
    b_t1 = b_t12
    b_t2 = b_t12
    b_hm = b_t12
    b_tmp4k = b_t12
    b_mo = [Buf("mo%d" % i) for i in range(2)]
    b_dT = [Buf("dT%d" % i) for i in range(2)]
    b_sgt = [Buf("sgt%d" % i) for i in range(1)]
    b_hhn = Buf("hhn")
    b_sgt_dram = Buf("s_gt")
    b_out = Buf("outdram")

    small_i = [0]

    def next_small():
        i = small_i[0] % 8
        small_i[0] += 1
        return small[i], b_small[i]

    def dma(q, out, in_, reads=(), writes=()):
        return S.add(q, lambda e, o=out, i=in_: e.dma_start(out=o, in_=i), reads, writes, dma=True)

    def act(out, in_, func, reads, writes, scale=1.0, bias=None, accum=None):
        kw = {}
        if bias is not None:
            kw["bias"] = bias
        if accum is not None:
            kw["accum_out"] = accum
        return S.add("act", lambda e: e.activation(out=out, in_=in_, func=func, scale=scale, **kw), reads, writes)

    def tt(eng, out, in0, in1, op, reads, writes):
        return S.add(eng, lambda e: e.tensor_tensor(out=out, in0=in0, in1=in1, op=op), reads, writes)

    def ts(eng, out, in0, s1, s2, op0, op1, reads, writes):
        if s2 is None:
            return S.add(eng, lambda e: e.tensor_scalar(out=out, in0=in0, scalar1=s1, scalar2=None, op0=op0), reads, writes)
        return S.add(eng, lambda e: e.tensor_scalar(out=out, in0=in0, scalar1=s1, scalar2=s2, op0=op0, op1=op1), reads, writes)

    def stt(eng, out, in0, sc, in1, op0, op1, reads, writes):
        return S.add(eng, lambda e: e.scalar_tensor_tensor(out=out, in0=in0, scalar=sc, in1=in1, op0=op0, op1=op1), reads, writes)

    def cp(eng, out, in_, reads, writes):
        if eng == "act":
            return S.add(eng, lambda e: e.activation(out=out, in_=in_, func=AF.Copy), reads, writes)
        return S.add(eng, lambda e: e.tensor_copy(out, in_), reads, writes)

    def mm(out, lhsT, rhs, start, stop, reads, writes, tp=None):
        if tp is None:
            return S.add("pe", lambda e: e.matmul(out, lhsT=lhsT, rhs=rhs, start=start, stop=stop), reads, writes)
        return S.add("pe", lambda e: e.matmul(out, lhsT=lhsT, rhs=rhs, start=start, stop=stop, tile_position=tp), reads, writes)

    def tr(out, in_, reads, writes):
        return S.add("pe", lambda e: e.transpose(out, in_, ident), list(reads) + [b_const], writes)

    def rstd_of(ss_ap, b_ss, n):
        ms, b_ms = next_small()
        ts("dve", ms[:, 0:1], ss_ap, 1.0 / n, EPS, ALU.mult, ALU.add, [b_ss], [b_ms])
        rs, b_rs = next_small()
        tt("pool", rs[:, 0:1], ms[:, 0:1], cst_m05[:, 0:1], ALU.pow, [b_ms, b_const], [b_rs])
        return rs, b_rs

    S.add("pool", lambda e: e.memset(cst_m05[:, 0:1], -0.5), [], [b_const])
    S.add("pool", lambda e: e.memset(cst_one[:, 0:1], 1.0), [], [b_const])
    S.add("pool", lambda e: e.memset(cst_ln8[:, 0:1], LN8), [], [b_const])
    pc_ = {n: Buf("c_" + n) for n in ["hmask", "tri_u", "tri_l", "ones", "convc", "poolsc", "bcol", "n1c", "n2c", "cT", "gbias",
                                      "normf", "brow0", "brow1", "ident", "ahl", "poolw"]}
    for nm, dst, src in [
        ("cT", cT, cT_d[:, :]), ("bcol", bcol, bcol_d[:, :]), ("n1c", n1c, n1c_d[:, :]), ("n2c", n2c, n2c_d[:, :]),
        ("tri_u", tri_u, tri_d[0]), ("tri_l", tri_l, tri_d[1]), ("ones", ones_f, tri_d[2]), ("convc", convc, conv_d[:, :]),
        ("poolsc", poolsc, poolsc_d[:, :]), ("hmask", hmask, hmask_d[:, :]),
        ("gbias", gbias_t, gb_d.partition_broadcast(128)), ("normf", normf_t, normf_d.partition_broadcast(128)),
        ("brow0", browt[0], brow_d[0:1, :].partition_broadcast(128)), ("brow1", browt[1], brow_d[1:2, :].partition_broadcast(128)),
    ]:
        dma("sp", dst, src, [], [pc_[nm]])
    dma("pool", ident, ident_d[:, :], [], [pc_["ident"]])
    dma("pool", ahl, ahl_d[:, :], [], [pc_["ahl"]])
    dma("pool", poolw, poolw_d[:, :], [], [pc_["poolw"]])
    S.add("pool", lambda e: e.memset(vaug_v[:, :, :, 128:129], 1.0), [], b_vaug)

    act(scb[:, 0:24], cT[:, 0:24], AF.Silu, [pc_["cT"]], [b_mod])
    scb_v = v3(scb[:, 0:24], 8, 3)
    screp_v = screp.rearrange("p (c b m) -> p c b m", c=8, b=2, m=128)
    cp("dve", screp_v, scb_v[:, :, 0:2].unsqueeze(3).broadcast_to([128, 8, 2, 128]), [b_mod], [b_mod])

    adaA_v = v3(adaA, 8, 1024)
    adaB_v = v3(adaB, 8, 1024)
    seg_order = [0, 1, 3, 4, 2, 5]
    col_si = {0: 0, 1: 1, 3: 2, 4: 3}
    psc = bank(0)[:, 0:96].rearrange("p (s c v) -> p s c v", s=4, c=8, v=3)
    for k, seg in enumerate(seg_order):
        stg, stg_v, b_stg = (adaA, adaA_v, b_adaA) if k % 2 == 0 else (adaB, adaB_v, b_adaB)
        dma("pool", stg, wada_d[seg], [], [b_stg])
        if k == 1:
            for c in range(8):
                dma("pool", w_in_v[:, c, :], win_d[c], [], [b_win])
        if seg in col_si:
            si = col_si[seg]
            for pc in range(8):
                for c in range(8):
                    mm(psc[:, si, pc, :], stg_v[:, c, pc * 128:(pc + 1) * 128], scb_v[:, c, :], c == 0, c == 7,
                       [b_stg, b_mod], [pb[0]])
            tt("dve", modcol_v[:, si], psc[:, si], v3(bcol[:, 0:32], 4, 8)[:, si].unsqueeze(2).broadcast_to([128, 8, 3]),
               ALU.add, [pb[0], pc_["bcol"]], [b_mod])
        else:
            gi = 0 if seg == 2 else 1
            for b in range(2):
                for half in range(2):
                    bk = 1 + (b * 2 + half) % 4
                    for c in range(8):
                        mm(bank(bk), screp_v[:, c, b, :], stg_v[:, c, half * 512:(half + 1) * 512], c == 0, c == 7,
                           [b_stg, b_mod], [pb[bk]])
                    gdst = gtr[gi] if b == 0 else gtmp[gi]
                    tt("dve", gdst[:, half * 512:(half + 1) * 512], bank(bk), browt[gi][:, half * 512:(half + 1) * 512],
                       ALU.add, [pb[bk], pc_["brow%d" % gi]], [b_gt if b == 0 else b_gtmp])
            dma("sp", s_gt[gi], gtmp[gi][:, 0:1024], [b_gtmp], [b_sgt_dram])
    stt("dve", G1c_v, modcol_v[:, 1], 1.0, n1c[:, 0:8].unsqueeze(2).broadcast_to([128, 8, 3]), ALU.add, ALU.mult,
        [b_mod, pc_["n1c"]], [b_mod])
    stt("dve", G2c_v, modcol_v[:, 3], 1.0, n2c[:, 0:8].unsqueeze(2).broadcast_to([128, 8, 3]), ALU.add, ALU.mult,
        [b_mod, pc_["n2c"]], [b_mod])
    S1c_v = modcol_v[:, 0]
    S2c_v = modcol_v[:, 2]

    S.barrier()

    conv_jobs = []

    pend_store = []

    def _mk_up(j):
        def f():
            dma("pool", upring[j % 3], wup_d[j], [], [b_up[j % 3]])
            flush_store()
            pend_store.append(lambda: dma("pool", s_up[j], upring[j % 3], [b_up[j % 3]], [b_sup[j]]))
        return f

    def _mk_dn(j):
        def f():
            dma("pool", dnring[j % 3], wdn_d[j], [], [b_dn[j % 3]])
            flush_store()
            pend_store.append(lambda: dma("pool", s_dn[j], dnring[j % 3], [b_dn[j % 3]], [b_sdn[j]]))
        return f

    def _mk_out(c):
        def f():
            dma("pool", dnring[(c + 1) % 3], wout_d[c], [], [b_dn[(c + 1) % 3]])
            flush_store()
            pend_store.append(lambda: dma("pool", s_out[c], dnring[(c + 1) % 3], [b_dn[(c + 1) % 3]], [b_sout[c]]))
        return f

    def flush_store():
        while pend_store:
            pend_store.pop(0)()

    for c in range(8):
        conv_jobs.append(_mk_out(c))
    for j in range(NJ):
        conv_jobs.append(_mk_up(j))
        conv_jobs.append(_mk_dn(j))

    def pump_conv(n):
        for _ in range(n):
            if conv_jobs:
                conv_jobs.pop(0)()
        if not conv_jobs:
            flush_store()

    tp_i = [0]
    mmb_i = [0]

    def next_mm_bank():
        i = 4 + mmb_i[0] % 3
        mmb_i[0] += 1
        return i

    xs_i = [0]

    def norm_T(src_ap_fn, ntile, Gc, Sc, vec, dst_v, b_dst, load, xs_list, b_xs_list, xn_list, b_xn_list):
        for g0 in range(0, ntile, 2):
            pr = (tp_i[0] % 2) * 2
            tp_i[0] += 1
            tpv = bank(pr, 2).bitcast(BF16)[:, 0:2048].rearrange("p (c t) -> p c t", c=8, t=256)
            for tl in range(g0, min(g0 + 2, ntile)):
                xt, b_xt = src_ap_fn(tl)
                ss, b_ss = next_small()
                act(junk[:, 0:1024], xt, AF.Square, [b_xt], [b_junk, b_ss], accum=ss[:, 0:1])
                rs, b_rs = rstd_of(ss[:, 0:1], b_ss, D)
                k = xs_i[0] % len(xn_list)
                xs_i[0] += 1
                ts("dve", xn_list[k][:, 0:1024], xt, rs[:, 0:1], None, ALU.mult, None, [b_xt, b_rs], [b_xn_list[k]])
                for c in range(8):
                    tr(tpv[:, c, (tl - g0) * 128:(tl - g0 + 1) * 128], xn_list[k][:, c * 128:(c + 1) * 128],
                       [b_xn_list[k]], [pb[pr], pb[pr + 1]])
            n = min(2, ntile - g0) * 128
            for c in range(8):
                act(dst_v[:, c, g0 * 128:g0 * 128 + n], tpv[:, c, 0:n], AF.Identity, [pb[pr], pb[pr + 1], b_mod], [b_dst],
                    scale=Gc[:, c, vec:vec + 1], bias=Sc[:, c, vec:vec + 1])

    def group_front(src_fn, ntl, xn_list, b_xn_list, junk_ap, b_junk_):
        ss, b_ss = next_small()
        srcs = []
        for i in range(ntl):
            xt, b_xt = src_fn(i)
            srcs.append((xt, b_xt))
            act(junk_ap[:, 0:1024], xt, AF.Square, [b_xt], [b_junk_, b_ss], accum=ss[:, i:i + 1])
        ms, b_ms = next_small()
        ts("dve", ms[:, 0:ntl], ss[:, 0:ntl], 1.0 / D, EPS, ALU.mult, ALU.add, [b_ss], [b_ms])
        rs, b_rs = next_small()
        tt("pool", rs[:, 0:ntl], ms[:, 0:ntl], cst_m05[:, 0:1].broadcast_to([128, ntl]), ALU.pow, [b_ms, b_const], [b_rs])
        outs = []
        for i in range(ntl):
            xt, b_xt = srcs[i]
            k = xs_i[0] % len(xn_list)
            xs_i[0] += 1
            ts("dve", xn_list[k][:, 0:1024], xt, rs[:, i:i + 1], None, ALU.mult, None, [b_xt, b_rs], [b_xn_list[k]])
            outs.append((xn_list[k], b_xn_list[k]))
        return outs

    def group_back(xns, Gc, Sc, vec, dst_v, b_dst, col0):
        pr = (tp_i[0] % 2) * 2
        tp_i[0] += 1
        tpv = bank(pr, 2).bitcast(BF16)[:, 0:2048].rearrange("p (c t) -> p c t", c=8, t=256)
        for i, (xa, b_xa) in enumerate(xns):
            for c in range(8):
                tr(tpv[:, c, i * 128:(i + 1) * 128], xa[:, c * 128:(c + 1) * 128], [b_xa], [pb[pr], pb[pr + 1]])
        n = len(xns) * 128
        for c in range(8):
            act(dst_v[:, c, col0:col0 + n], tpv[:, c, 0:n], AF.Identity, [pb[pr], pb[pr + 1], b_mod], [b_dst],
                scale=Gc[:, c, vec:vec + 1], bias=Sc[:, c, vec:vec + 1])

    def phase1(b):
        if b > 0:
            for c in range(8):
                dma("pool", w_in_v[:, c, :], win_d[c], [], [b_win])
        dma("sp", hhn_t[:, 0:512], hn_d.partition_broadcast(128), [], [b_hhn])
        ts("dve", hhn_t[:, 0:512], hhn_t[:, 0:512], 0.5, None, ALU.mult, None, [b_hhn], [b_hhn])
        blocks = [dict(name="ctx", tokoff=0, bi=0, nblk=1, tpb=2, vec=2, src=ctx_d, tt0=0)]
        for bi in range(4):
            blocks.append(dict(name="x", tokoff=256, bi=bi, nblk=4, tpb=4, vec=b, src=x_d, tt0=2))
        xload_i = [0]
        for gi_, B_ in enumerate(blocks):
            B_["slot"] = gi_ % 3

        def front(B_, g):
            ntl = min(2, B_["tpb"] - 2 * g)

            def src(i):
                pump_conv(2)
                k = xload_i[0] % 2
                xload_i[0] += 1
                r0 = (B_["bi"] * B_["tpb"] + 2 * g + i) * 128
                dma("sp", xs[k][:, 0:1024], B_["src"][b, r0:r0 + 128, :], [], [b_xs[k]])
                return xs[k][:, 0:1024], b_xs[k]
            return group_front(src, ntl, xn, b_xn, junk, b_junk)

        def back(B_, g, xns):
            sl = B_["slot"]
            group_back(xns, G1c_v, S1c_v, B_["vec"], hT_v[sl], b_hT[sl], g * 256)

        def tok_tiles(B_, tls):
            sl = B_["slot"]
            h_v = hT_v[sl]
            name, bi, tpb = B_["name"], B_["bi"], B_["tpb"]
            for tl in tls:
                tti = B_["tt0"] + bi * tpb + tl
                groups = ["v", "gates"] if name == "ctx" else ["pool", "v", "o", "gates"]
                for grp in groups:
                    c0, ncol = {"pool": (0, 512), "v": (1024, 512), "o": (1536, 512), "gates": (2048, 16)}[grp]
                    bk = next_mm_bank()
                    for c in range(8):
                        mm(bank(bk)[:, 0:ncol], h_v[:, c, tl * 128:(tl + 1) * 128], w_in_v[:, c, c0:c0 + ncol],
                           c == 0, c == 7, [b_hT[sl], b_win], [pb[bk]])
                    if grp == "pool":
                        ti = bi * tpb + tl
                        act(poolin_v[:, ti, :], bank(bk), AF.Copy, [pb[bk]], [b_poolin[ti]])
                    elif grp == "v":
                        cp("dve", vaug_v[:, tti, :, 0:128], v3(bank(bk), 4, 128), [pb[bk]], [b_vaug[tti]])
                    elif grp == "o":
                        ti = bi * tpb + tl
                        act(otmp[0][:, 0:512], bank(bk), AF.Tanh, [pb[bk]], [b_otmp[0]], scale=0.5)
                        stt("dve", sog_v[:, ti, :], otmp[0][:, 0:512], 1.0, hhn_t[:, 0:512], ALU.add, ALU.mult,
                            [b_otmp[0], b_hhn], [b_sog[ti]])
                    else:
                        tt("dve", gsb_v[:, tti, :], bank(bk)[:, 0:16], gbias_t[:, 0:16], ALU.add,
                           [pb[bk], b_const], [b_gsb])

        def qk_block(B_, prevB, nextB):
            sl = B_["slot"]
            h_v = hT_v[sl]
            bi, tpb, nblk = B_["bi"], B_["tpb"], B_["nblk"]
            n = tpb * 128
            left = bi > 0
            right = bi < nblk - 1
            hb = 7
            halo = bank(hb)[:, 0:8]
            for qc in range(4):
                bk = next_mm_bank()
                cw = 512 + qc * 128
                for c in range(8):
                    mm(bank(bk)[:, 0:n], w_in_v[:, c, cw:cw + 128], h_v[:, c, 0:n], c == 0, c == 7,
                       [b_hT[sl], b_win], [pb[bk]])
                if left:
                    hp = hT_v[prevB["slot"]]
                    for c in range(8):
                        mm(halo[:, 2 * qc:2 * qc + 1], w_in_v[:, c, cw:cw + 128], hp[:, c, 511:512], c == 0, c == 7,
                           [b_hT[prevB["slot"]], b_win], [pb[hb]])
                if right:
                    hn_ = hT_v[nextB["slot"]]
                    for c in range(8):
                        mm(halo[:, 2 * qc + 1:2 * qc + 2], w_in_v[:, c, cw:cw + 128], hn_[:, c, 0:1], c == 0, c == 7,
                           [b_hT[nextB["slot"]], b_win], [pb[hb]])
                k = qc % 2
                a = acc[k]
                psb = bank(bk)
                act(a[:, 0:n], psb[:, 0:n], AF.Identity, [pb[bk], b_const], [b_acc[k]], scale=convc[:, qc * 3 + 1:qc * 3 + 2])
                stt("dve", a[:, 1:n], psb[:, 0:n - 1], convc[:, qc * 3:qc * 3 + 1], a[:, 1:n], ALU.mult, ALU.add,
                    [pb[bk], b_const, b_acc[k]], [b_acc[k]])
                stt("dve", a[:, 0:n - 1], psb[:, 1:n], convc[:, qc * 3 + 2:qc * 3 + 3], a[:, 0:n - 1], ALU.mult, ALU.add,
                    [pb[bk], b_const, b_acc[k]], [b_acc[k]])
                if left:
                    stt("dve", a[:, 0:1], halo[:, 2 * qc:2 * qc + 1], convc[:, qc * 3:qc * 3 + 1], a[:, 0:1], ALU.mult, ALU.add,
                        [pb[hb], b_const, b_acc[k]], [b_acc[k]])
                if right:
                    stt("dve", a[:, n - 1:n], halo[:, 2 * qc + 1:2 * qc + 2], convc[:, qc * 3 + 2:qc * 3 + 3], a[:, n - 1:n],
                        ALU.mult, ALU.add, [pb[hb], b_const, b_acc[k]], [b_acc[k]])
                dstv = qT_v if qc < 2 else kT_v
                t0 = B_["tokoff"] + bi * n
                act(dstv[:, qc % 2, t0:t0 + n], a[:, 0:n], AF.Silu, [b_acc[k]], [b_qk])

        B0 = blocks[0]
        for g in range((B0["tpb"] + 1) // 2):
            back(B0, g, front(B0, g))
        for i_, B_ in enumerate(blocks):
            nxt = blocks[i_ + 1] if i_ + 1 < len(blocks) else None
            prv = blocks[i_ - 1] if i_ > 0 else None
            ng = (B_["tpb"] + 1) // 2
            nng = (nxt["tpb"] + 1) // 2 if nxt else 0
            for g in range(max(ng, nng)):
                xns = front(nxt, g) if (nxt and g < nng) else None
                if g < ng:
                    tok_tiles(B_, list(range(2 * g, min(2 * g + 2, B_["tpb"]))))
                if xns is not None:
                    back(nxt, g, xns)
            qk_block(B_, prv if (prv and prv["name"] == B_["name"]) else None,
                     nxt if (nxt and nxt["name"] == B_["name"]) else None)
        pump_conv(1000)

    def phase1b(b):
        act(a8_v[:, :, :], gsb_v[:, :, 8:16], AF.Exp, [b_gsb], [b_gate], scale=-1.0)
        act(nlf_v[:, :, :], a8_v[:, :, :], AF.Ln, [b_gate, b_const], [b_gate], bias=cst_one[:, 0:1])
        cs = bank(0)[:, 0:NTT * 8].rearrange("p (t g) -> p t g", t=NTT, g=8)
        bl = bank(1)[:, 0:NTT * 8].rearrange("p (t g) -> p t g", t=NTT, g=8)
        for tti in range(NTT):
            mm(cs[:, tti, 0:4], tri_u[:, 0:128], nlf_v[:, tti, 0:4], True, True, [b_gate, b_const], [pb[0]])
            mm(cs[:, tti, 4:8], tri_l[:, 0:128], nlf_v[:, tti, 4:8], True, True, [b_gate, b_const], [pb[0]])
            mm(bl[:, tti, :], ones_f[:, 0:128], nlf_v[:, tti, :], True, True, [b_gate, b_const], [pb[1]])
        tt("dve", a8_v[:, :, :], gsb_v[:, :, 0:8], cs, ALU.add, [b_gsb, pb[0], b_gate], [b_gate])
        act(E8_v[:, :, :], a8_v[:, :, :], AF.Exp, [b_gate], [b_gate])
        act(THR8_v[:, :, :], cs, AF.Exp, [pb[0], b_const], [b_gate], bias=cst_ln8[:, 0:1])
        blv = bl.rearrange("p t (d h) -> p t d h", d=2, h=4)
        act(dec_v[0:64], blv[0:64, :, :, 0:4:2], AF.Exp, [pb[1]], [b_gate], scale=-1.0)
        act(dec_v[64:128], blv[64:128, :, :, 1:4:2], AF.Exp, [pb[1]], [b_gate], scale=-1.0)

        def tok0(tti):
            return tti * 128 if tti < 2 else 256 + (tti - 2) * 128

        for k_ in range(4):
            for hf in range(2):
                S.add("pool", lambda e, a=kz[k_][hf]: e.memset(a[:, 0:256], 0.0), [], [b_kz[k_]])
        orders = (list(range(NTT)), [1, 0] + list(range(NTT - 1, 1, -1)))
        prevs = [None, None]
        for step in range(NTT):
            for d in range(2):
                tti = orders[d][step]
                Rv = v3(stR[d][:, 0:258], 2, 129)
                prev = prevs[d]
                k = d * 2 + step % 2
                kb = d + 2 * (step % 2)
                ktp = bank(kb).bitcast(BF16)[:, 0:256]
                t0 = tok0(tti)
                for pr in range(2):
                    tr(ktp[:, pr * 128:(pr + 1) * 128], kT_v[:, pr, t0:t0 + 128], [b_qk], [pb[kb]])
                ktp_v = v3(ktp, 2, 128)
                cp("act", v3(kz[k][0][:, 0:256], 2, 128)[:, :, 0:64], ktp_v[:, :, 0:64], [pb[kb]], [b_kz[k]])
                cp("act", v3(kz[k][1][:, 0:256], 2, 128)[:, :, 64:128], ktp_v[:, :, 64:128], [pb[kb]], [b_kz[k]])
                vp = vpI[k][:, 0:516].rearrange("p (h c) -> p h c", h=4, c=129)
                tt("pool", vp, vaug_v[:, tti], E8_v[:, tti, d * 4:d * 4 + 4].unsqueeze(2).broadcast_to([128, 4, 129]), ALU.mult,
                   [b_vaug[tti], b_gate], [b_vpI[k]])
                ub = 4 + d * 2 + step % 2
                U = bank(ub)[:, 0:258].rearrange("p (h c) -> p h c", h=2, c=129)
                for pr in range(2):
                    mm(U[:, pr, :], kz[k][0][:, pr * 128:(pr + 1) * 128], vp[:, 2 * pr, :], True, False,
                       [b_kz[k], b_vpI[k]], [pb[ub]])
                    mm(U[:, pr, :], kz[k][1][:, pr * 128:(pr + 1) * 128], vp[:, 2 * pr + 1, :], False, True,
                       [b_kz[k], b_vpI[k]], [pb[ub]])
                if prev is None:
                    cp("dve", Rv, U, [pb[ub]], [b_st[d]])
                else:
                    for hh in range(2):
                        dcol = dec_v[:, prev, d, hh:hh + 1]
                        if tti >= 2:
                            act(Cst_v[:, tti - 2, d, hh, :], Rv[:, hh, :], AF.Identity, [b_st[d], b_gate], [b_cst], scale=dcol)
                        stt("dve", Rv[:, hh, :], Rv[:, hh, :], dcol, U[:, hh, :], ALU.mult, ALU.add,
                            [b_st[d], b_gate, pb[ub]], [b_st[d]])
                prevs[d] = tti

    def phase3(b):
        if b > 0:
            for gi in range(2):
                dma("sp", gtr[gi][:, 0:1024], s_gt[gi], [b_sgt_dram], [b_gt])

        def load_wout():
            for c in range(8):
                dma("sp", wout_v[:, c, :], s_out[c], [b_sout[c]], [b_wout])

        def h1(nb, tl):
            t = nb * 4 + tl
            tti = t + 2
            t0 = 256 + t * 128
            k = t % 2
            Sps = v3(bank(0), 4, 128)
            qz_v = qz[k][:, 0:512].rearrange("p (f r c) -> p f r c", f=2, r=2, c=128)
            for hf in range(2):
                act(qz_v[:, hf], qT_v[:, :, t0:t0 + 128], AF.Copy, [b_qk, b_const], [b_qz[k]], scale=hmask[:, hf:hf + 1])
            for h in range(4):
                mm(Sps[:, h, :], kT_v[:, h // 2, t0:t0 + 128], qz_v[:, h % 2, h // 2, :], True, True,
                   [b_qk, b_qz[k]], [pb[0]])
            Pf_v = v3(Pf[k][:, 0:512], 4, 128)
            Pb_v = v3(Pb[k][:, 0:512], 4, 128)
            tt("dve", Pf_v, Sps, tri_u[:, 0:128].unsqueeze(1).broadcast_to([128, 4, 128]), ALU.mult, [pb[0], b_const], [b_Pf[k]])
            tt("dve", Pb_v, Sps, tri_l[:, 0:128].unsqueeze(1).broadcast_to([128, 4, 128]), ALU.mult, [pb[0], b_const], [b_Pb[k]])
            vf = vpf[k][:, 0:516].rearrange("p (h c) -> p h c", h=4, c=129)
            vb = vpb[k][:, 0:516].rearrange("p (h c) -> p h c", h=4, c=129)
            tt("pool", vf, vaug_v[:, tti], E8_v[:, tti, 0:4].unsqueeze(2).broadcast_to([128, 4, 129]), ALU.mult,
               [b_vaug[tti], b_gate], [b_vpf[k]])
            tt("pool", vb, vaug_v[:, tti], E8_v[:, tti, 4:8].unsqueeze(2).broadcast_to([128, 4, 129]), ALU.mult,
               [b_vaug[tti], b_gate], [b_vpb[k]])

        def h2(nb, tl):
            t = nb * 4 + tl
            k = t % 2
            dps = v3(bank(0), 4, 128)
            for g in range(4):
                mm(dps[:, g, :], poolin_v[:, t, g * 128:(g + 1) * 128], ahl_v[:, 2 * g, :], True, False,
                   [b_poolin[t], b_const], [pb[0]])
                mm(dps[:, g, :], poolin_v[:, t, g * 128:(g + 1) * 128], ahl_v[:, 2 * g + 1, :], False, True,
                   [b_poolin[t], b_const], [pb[0]])
            cp("act", dT[k][:, 0:512], bank(0), [pb[0]], [b_dT[k]])

        def h3(nb, tl):
            t = nb * 4 + tl
            tti = t + 2
            k = t % 2
            qz_v = qz[k][:, 0:512].rearrange("p (f r c) -> p f r c", f=2, r=2, c=128)
            Pf_v = v3(Pf[k][:, 0:512], 4, 128)
            Pb_v = v3(Pb[k][:, 0:512], 4, 128)
            vf = vpf[k][:, 0:516].rearrange("p (h c) -> p h c", h=4, c=129)
            vb = vpb[k][:, 0:516].rearrange("p (h c) -> p h c", h=4, c=129)
            den = bank(3)[:, 0:8]
            for d, (Pv, vv, bP, bV) in enumerate(((Pf_v, vf, b_Pf[k], b_vpf[k]), (Pb_v, vb, b_Pb[k], b_vpb[k]))):
                NUM = v3(bank(1 + d), 4, 128)
                for h in range(4):
                    qh = qz_v[:, h % 2, h // 2, :]
                    mm(NUM[:, h, :], Pv[:, h, :], vv[:, h, 0:128], True, False, [bP, bV], [pb[1 + d]])
                    mm(NUM[:, h, :], qh, Cst_v[:, t, d, h // 2, 0:128], False, True, [b_qz[k], b_cst], [pb[1 + d]])
                    mm(den[:, d * 4 + h:d * 4 + h + 1], Pv[:, h, :], vv[:, h, 128:129], True, False, [bP, bV], [pb[3]])
                    mm(den[:, d * 4 + h:d * 4 + h + 1], qh, Cst_v[:, t, d, h // 2, 128:129], False, True,
                       [b_qz[k], b_cst], [pb[3]])
            dabs, b_dabs = next_small()
            act(dabs[:, 0:8], den, AF.Abs, [pb[3]], [b_dabs])
            dm, b_dm = next_small()
            tt("dve", dm[:, 0:8], dabs[:, 0:8], THR8_v[:, tti, :], ALU.max, [b_dabs, b_gate], [b_dm])
            r8, b_r8 = next_small()
            S.add("dve", lambda e, o=r8, i=dm: e.reciprocal(out=o[:, 0:8], in_=i[:, 0:8]), [b_dm], [b_r8])
            tt("dve", v3(t1[:, 0:512], 4, 128), v3(bank(1), 4, 128), r8[:, 0:4].unsqueeze(2).broadcast_to([128, 4, 128]), ALU.mult,
               [pb[1], b_r8], [b_t12])
            ss4, b_ss4 = next_small()
            for h in range(4):
                stt("dve", t1[:, h * 128:(h + 1) * 128], bank(2)[:, h * 128:(h + 1) * 128], r8[:, 4 + h:5 + h],
                    t1[:, h * 128:(h + 1) * 128], ALU.mult, ALU.add, [pb[2], b_r8, b_t12], [b_t12])
            for h in range(4):
                act(t2[:, h * 128:(h + 1) * 128], t1[:, h * 128:(h + 1) * 128], AF.Square, [b_t12], [b_t12, b_ss4],
                    accum=ss4[:, h:h + 1])
            ms4, b_ms4 = next_small()
            ts("dve", ms4[:, 0:4], ss4[:, 0:4], 1.0 / 128, EPS, ALU.mult, ALU.add, [b_ss4], [b_ms4])
            rs4, b_rs4 = next_small()
            tt("pool", rs4[:, 0:4], ms4[:, 0:4], cst_m05[:, 0:1].broadcast_to([128, 4]), ALU.pow, [b_ms4, b_const], [b_rs4])
            for h in range(4):
                stt("dve", mo[k][:, h * 128:(h + 1) * 128], t1[:, h * 128:(h + 1) * 128], rs4[:, h:h + 1],
                    sog_v[:, t, h * 128:(h + 1) * 128], ALU.mult, ALU.mult, [b_t12, b_rs4, b_sog[t]], [b_mo[k]])

        def h4(nb, tl):
            t = nb * 4 + tl
            k = t % 2
            pops = v3(bank(0), 4, 128)
            for g in range(4):
                mm(pops[:, g, :], poolw_v[:, g, :], dT[k][:, g * 128:(g + 1) * 128], True, True, [b_dT[k], b_const], [pb[0]])
            tt("dve", mixT_v[tl][:, 0:4, :], pops, poolsc[:, 0:4].unsqueeze(2).broadcast_to([128, 4, 128]), ALU.mult,
               [pb[0], b_const], [b_mixT[tl]])

        def h5(nb, tl):
            t = nb * 4 + tl
            k = t % 2
            moT = bank(3).bitcast(BF16)[:, 512:1024].rearrange("p (h c) -> p h c", h=4, c=128)
            for h in range(4):
                tr(moT[:, h, :], mo[k][:, h * 128:(h + 1) * 128], [b_mo[k]], [pb[3]])
            cp("act", mixT_v[tl][:, 4:8, :], moT, [pb[3]], [b_mixT[tl]])

        def head_pieces(nb):
            seq = [(h1, 0), (h2, 0), (h1, 1), (h2, 1), (h3, 0), (h4, 0), (h3, 1), (h4, 1), (h1, 2), (h2, 2), (h5, 0),
                   (h1, 3), (h2, 3), (h5, 1), (h3, 2), (h4, 2), (h3, 3), (h4, 3), (h5, 2), (h5, 3)]
            return [(lambda f=f, tl=tl: f(nb, tl)) for f, tl in seq]

        def tail(nb):
            for tl in range(4):
                r0 = (nb * 4 + tl) * 128
                dma("sp", x1b[tl][:, 0:1024], x_d[b, r0:r0 + 128, :], [], [b_x1[tl]])
            sgt_j = sgt[0].bitcast(BF16)

            def wout_tile(tl):
                bp = 2 * tl
                for half in range(2):
                    for kc in range(8):
                        mm(bank(bp + half), mixT_v[tl][:, kc, :], wout_v[:, kc, half * 512:(half + 1) * 512], kc == 0, kc == 7,
                           [b_mixT[tl], b_wout], [pb[bp + half]])
                tt("dve", tmp4k[:, 0:1024], bank(bp, 2), gtr[0][:, 0:1024], ALU.mult, [pb[bp], pb[bp + 1], b_gt], [b_tmp4k])
                tt("dve", x1b[tl][:, 0:1024], tmp4k[:, 0:1024], x1b[tl][:, 0:1024], ALU.add, [b_tmp4k, b_x1[tl]], [b_x1[tl]])

            def front2(g):
                def src2(i, g=g):
                    return x1b[2 * g + i][:, 0:1024], b_x1[2 * g + i]
                return group_front(src2, 2, xn2, b_xn2, sgt_j, b_sgt[0])

            wout_tile(0)
            wout_tile(1)
            wout_tile(2)
            xg0 = front2(0)
            wout_tile(3)
            group_back(xg0, G2c_v, S2c_v, b, h2T_v, b_h2T, 0)
            xg1 = front2(1)
            group_back(xg1, G2c_v, S2c_v, b, h2T_v, b_h2T, 256)

        def ffn_up(nb, pieces):
            for j in range(min(2, NJ)):
                dma("sp", upring[j % 3], s_up[j], [b_sup[j]], [b_up[j % 3]])
            for _ in range(2):
                if pieces:
                    pieces.pop(0)()
            for j in range(NJ):
                if j + 2 < NJ:
                    dma("sp", upring[(j + 2) % 3], s_up[j + 2], [b_sup[j + 2]], [b_up[(j + 2) % 3]])
                if j == NJ - 2:
                    for jj in range(2):
                        dma("sp", dnring[jj % 3], s_dn[jj], [b_sdn[jj]], [b_dn[jj % 3]])
                upv = v3(upring[j % 3], 8, 256)
                gb_, ab_ = 4 + (j % 2), 6 + (j % 2)
                for c in range(8):
                    mm(bank(gb_), upv[:, c, 0:128], h2T_v[:, c, :], c == 0, c == 7, [b_up[j % 3], b_h2T], [pb[gb_]])
                for c in range(8):
                    mm(bank(ab_), upv[:, c, 128:256], h2T_v[:, c, :], c == 0, c == 7, [b_up[j % 3], b_h2T], [pb[ab_]])
                act(sgt[0][:, 0:512], bank(gb_), AF.Silu, [pb[gb_]], [b_sgt[0]])
                tt("dve", actT_v[:, j, :], bank(ab_), sgt[0][:, 0:512], ALU.mult, [pb[ab_], b_sgt[0]],
                   [b_actT_lo if j < 16 else b_actT_hi])
                if pieces:
                    pieces.pop(0)()
            while pieces:
                pieces.pop(0)()

        def ffn_down(nb, last):
            for j in range(NJ):
                if j + 2 < NJ:
                    dma("sp", dnring[(j + 2) % 3], s_dn[j + 2], [b_sdn[j + 2]], [b_dn[(j + 2) % 3]])
                if j == 16 and not last:
                    load_wout()
                for tl in range(4):
                    for half in range(2):
                        bk = tl * 2 + half
                        mm(bank(bk), actT_v[:, j, tl * 128:(tl + 1) * 128], dnring[j % 3][:, half * 512:(half + 1) * 512],
                           j == 0, j == NJ - 1, [b_actT_lo if j < 16 else b_actT_hi, b_dn[j % 3]], [pb[bk]])
            for tl in range(4):
                t = nb * 4 + tl
                tt("dve", tmp4k[:, 0:1024], bank(tl * 2, 2), gtr[1][:, 0:1024], ALU.mult, [pb[tl * 2], pb[tl * 2 + 1], b_gt], [b_tmp4k])
                tt("dve", x1b[tl][:, 0:1024], tmp4k[:, 0:1024], x1b[tl][:, 0:1024], ALU.add, [b_tmp4k, b_x1[tl]], [b_x1[tl]])
                ss, b_ss = next_small()
                act(xn2[tl % 2][:, 0:1024], x1b[tl][:, 0:1024], AF.Square, [b_x1[tl]], [b_xn2[tl % 2], b_ss], accum=ss[:, 0:1])
                rs, b_rs = rstd_of(ss[:, 0:1], b_ss, D)
                stt("dve", x1b[tl][:, 0:1024], x1b[tl][:, 0:1024], rs[:, 0:1], normf_t[:, 0:1024], ALU.mult, ALU.mult,
                    [b_x1[tl], b_rs, b_const], [b_x1[tl]])
                dma("pool", out_d[b, t * 128:(t + 1) * 128, :], x1b[tl][:, 0:1024], [b_x1[tl]], [b_out])

        load_wout()
        for pc in head_pieces(0):
            pc()
        for nb in range(4):
            tail(nb)
            ffn_up(nb, head_pieces(nb + 1) if nb < 3 else [])
            ffn_down(nb, nb == 3)

    try:
        for b in range(2):
            phase1(b)
            if stop_after == "1" and b == 0:
                break
            phase1b(b)
            if stop_after == "1b" and b == 0:
                break
            S.barrier()
            phase3(b)
            S.barrier()
            if stop_after == "3" and b == 0:
                break
    except StopBuild:
        pass

    if dbg_d is not None:
        S.barrier()
        dma("pool", dbg_d[:, 0:dbg_cols], mem[:, 0:dbg_cols], [], [b_out])

    import contextlib
    with contextlib.ExitStack() as es:
        sems = {e: es.enter_context(nc.semaphore("s_" + e)) for e in ["pe", "act", "dve", "pool"]}
        dsems = {e: [es.enter_context(nc.semaphore("d_%s%d" % (e, i))) for i in range(n)] for e, n in NDSEM.items()}
        S.finalize(nc, sems, dsems)
        block = es.enter_context(nc.Block())

        @block.tensor
        def _(e):
            S.emit("pe", e, sems, dsems)

        @block.scalar
        def _(e):
            S.emit("act", e, sems, dsems)

        @block.vector
        def _(e):
            S.emit("dve", e, sems, dsems)

        @block.gpsimd
        def _(e):
            S.emit("pool", e, sems, dsems)

        @block.sync
        def _(e):
            S.emit("sp", e, sems, dsems)

    layout = dict(A_end=A_end, DE_off=DE_off, D_end=D_end, E_end=E_end, R1_off=R1_off, TOTAL=TOTAL)
    _aps = dict(poolin=poolin, vaug=vaug, sog=sog, qT=qT, kT=kT, gsb=gsb, E8=E8, THR8=THR8, nlf=nlf, a8=a8, dec=dec,
                Cst=Cst, modcol=modcol, G1c=G1c, G2c=G2c, gtr0=gtr[0], gtr1=gtr[1], stR0=stR[0], stR1=stR[1],
                x1b0=x1b[0], x1b1=x1b[1], x1b2=x1b[2], x1b3=x1b[3], h2T=h2T, actT=actT, mixT=mixT[0], hT0=hT[0], hT1=hT[1], hT2=hT[2])
    layout["aps"] = {k: (int(v.offset) * (2 if v.dtype == BF16 else 4), int(v.shape[1]), "bf16" if v.dtype == BF16 else "f32")
                     for k, v in _aps.items()}
    return nc, layout


def make_in_maps(x, c, ctx, c_ctx, w_ada, b_ada, norm1, w_in, conv_qk, gate_bias, pool_w,
                 pool_scale, head_norm, w_out, norm2, w_up, w_down, norm_f):
    f32 = np.float32
    x = np.asarray(x, f32)
    c = np.asarray(c, f32)
    ctx = np.asarray(ctx, f32)
    c_ctx = np.asarray(c_ctx, f32)
    w_ada = np.asarray(w_ada, f32)[0]
    b_ada = np.asarray(b_ada, f32)[0]
    norm1 = np.asarray(norm1, f32)[0]
    w_in = np.asarray(w_in, f32)[0]
    conv_qk = np.asarray(conv_qk, f32)[0]
    gate_bias = np.asarray(gate_bias, f32)[0]
    pool_w = np.asarray(pool_w, f32)[0]
    pool_scale = np.asarray(pool_scale, f32)[0]
    head_norm = np.asarray(head_norm, f32)[0]
    w_out = np.asarray(w_out, f32)[0]
    norm2 = np.asarray(norm2, f32)[0]
    w_up = np.asarray(w_up, f32)[0]
    w_down = np.asarray(w_down, f32)[0]
    norm_f = np.asarray(norm_f, f32)

    def colform(v, nchunk):
        return np.ascontiguousarray(v.reshape(nchunk, 128).T)

    w_ada_l = np.ascontiguousarray(w_ada.reshape(8, 128, 6, 1024).transpose(2, 1, 0, 3)).reshape(6, 128, 8 * 1024)
    bseg = b_ada.reshape(6, 1024)
    b_ada_col = np.ascontiguousarray(
        np.stack([colform(bseg[s], 8) for s in (0, 1, 3, 4)], axis=1)).reshape(128, 32)
    b_ada_row = np.ascontiguousarray(np.stack([bseg[2], bseg[5]], axis=0))
    w_in_l = np.ascontiguousarray(w_in.reshape(8, 128, INW))
    conv_c = np.ascontiguousarray(conv_qk.reshape(3, 4, 128).transpose(2, 1, 0)).reshape(128, 12)
    pool_w_l = np.ascontiguousarray(pool_w.transpose(1, 0, 2)).reshape(128, 512)
    pool_sc = colform(pool_scale, 4)
    w_out_l = np.ascontiguousarray(w_out.reshape(8, 128, 1024))
    wu = w_up.reshape(8, 128, 2, NJ, 128)
    w_up_l = np.ascontiguousarray(wu.transpose(3, 1, 0, 2, 4)).reshape(NJ, 128, 2048)
    w_down_l = np.ascontiguousarray(w_down.reshape(NJ, 128, 1024))
    s_idx = np.arange(128)
    tri = np.stack([
        (s_idx[:, None] <= s_idx[None, :]).astype(f32),
        (s_idx[:, None] >= s_idx[None, :]).astype(f32),
        np.ones((128, 128), f32)], axis=0)
    ident = np.eye(128, dtype=f32)
    pm = _pool_mats()
    ahl = np.ascontiguousarray(pm.transpose(2, 0, 1, 3)).reshape(128, 1024)

    shared = dict(
        w_ada_l=w_ada_l, b_ada_col=b_ada_col, b_ada_row=b_ada_row,
        norm1c=colform(norm1, 8), norm2c=colform(norm2, 8),
        normf_row=np.ascontiguousarray(norm_f.reshape(1, 1024)), hn_row=np.ascontiguousarray(head_norm.reshape(1, 512)),
        w_in_l=w_in_l, conv_c=conv_c, gb_row=np.ascontiguousarray(gate_bias.reshape(1, 16)),
        pool_w_l=pool_w_l, pool_sc=pool_sc, w_out_l=w_out_l, w_up_l=w_up_l, w_down_l=w_down_l,
        tri=tri, ident=ident, ahl=ahl,
        hmask=np.ascontiguousarray(np.stack([(s_idx < 64), (s_idx >= 64)], axis=1).astype(f32)),
    )
    in_maps = []
    for core in range(NCORES):
        b0 = 2 * core
        cT = np.stack([colform(c[b0], 8), colform(c[b0 + 1], 8), colform(c_ctx, 8)], axis=2).reshape(128, 24)
        m = dict(shared)
        m["x"] = np.ascontiguousarray(x[b0:b0 + 2])
        m["ctx"] = np.ascontiguousarray(ctx[b0:b0 + 2])
        m["cT"] = np.ascontiguousarray(cT)
        in_maps.append(m)
    return in_maps


_PROGRAM = None


def kernel(x, c, ctx, c_ctx, w_ada, b_ada, norm1, w_in, conv_qk, gate_bias, pool_w,
           pool_scale, head_norm, w_out, norm2, w_up, w_down, norm_f):
    global _PROGRAM
    in_maps = make_in_maps(x, c, ctx, c_ctx, w_ada, b_ada, norm1, w_in, conv_qk, gate_bias, pool_w,
                           pool_scale, head_norm, w_out, norm2, w_up, w_down, norm_f)
    if _PROGRAM is None:
        _PROGRAM = build_program()[0]
    res = run_bass_kernel_spmd(_PROGRAM, in_maps, core_ids=list(range(NCORES)))
    out = np.concatenate([np.asarray(r["out"], np.float32) for r in res.results], axis=0)
    return out
```
